# Optimizing a Trainium2 kernel written in Bass

```python
import jax, jax.numpy as jnp
from jax import lax
import numpy as np

D_MODEL = 1024
BATCH = 8
SEQ = 8192
DEPTH = 2

GRID_W = 64
CTX_LEN = 256
N_MOD = 9
EPS = 1e-6
ROPE_BASE = 10000.0
BLOCK = 128
NEG_INF = -1e30
D_FF = 2816

MLA_HEADS = 8
MLA_Q_RANK = 256
MLA_KV_RANK = 128
MLA_NOPE_DIM = 64
MLA_ROPE_DIM = 32
MLA_V_DIM = 64
MLA_QK_DIM = MLA_NOPE_DIM + MLA_ROPE_DIM
MLA_SCALE = MLA_QK_DIM ** -0.5

WIN_HEADS = 8
WIN_KV_HEADS = 2
WIN_GROUP = WIN_HEADS // WIN_KV_HEADS
WIN_HEAD_DIM = 64
WINDOW = 128
WIN_SCALE = WIN_HEAD_DIM ** -0.5

PAIR_KV_SIZES = (MLA_KV_RANK, MLA_ROPE_DIM, WIN_KV_HEADS * WIN_HEAD_DIM, WIN_KV_HEADS * WIN_HEAD_DIM)
PAIR_Q_SIZES = (MLA_Q_RANK, WIN_HEADS * WIN_HEAD_DIM)
PAIR_KV_COLS = sum(PAIR_KV_SIZES)
PAIR_IN = PAIR_KV_COLS + sum(PAIR_Q_SIZES)
PAIR_OUT = MLA_HEADS * MLA_V_DIM + WIN_HEADS * WIN_HEAD_DIM

GLA_HEADS = 4
GLA_DK = D_MODEL // 2
GLA_DV = D_MODEL
GLA_DK_HEAD = GLA_DK // GLA_HEADS
GLA_DV_HEAD = GLA_DV // GLA_HEADS
GLA_LOWRANK = 16
GLA_GATE_NORM = 16.0
GLA_CHUNK = 64
GLA_KV_SIZES = (GLA_DK, GLA_DV, GLA_LOWRANK, GLA_LOWRANK)
GLA_Q_SIZES = (GLA_DK, GLA_DV)
GLA_KV_COLS = sum(GLA_KV_SIZES)
GLA_IN = GLA_KV_COLS + sum(GLA_Q_SIZES)

kernel_name = "hybrid_mla_swa_gla_prefix_dit"


def rms_norm(x, g):
    xf = x.astype(jnp.float32)
    return xf * lax.rsqrt(jnp.mean(xf * xf, axis=-1, keepdims=True) + EPS) * g


def split_cols(z, sizes):
    out, start = [], 0
    for s in sizes:
        out.append(z[..., start:start + s])
        start += s
    return out


def swiglu(h, w_gu, w_down):
    gu = h @ w_gu
    return (jax.nn.silu(gu[..., :D_FF]) * gu[..., D_FF:]) @ w_down


def axial_rope_tables(rows, rot_dim):
    n_freq = rot_dim // 4
    inv = ROPE_BASE ** (-jnp.arange(n_freq, dtype=jnp.float32) / n_freq)
    t = jnp.arange(rows * GRID_W)
    row = (t // GRID_W).astype(jnp.float32)
    col = (t % GRID_W).astype(jnp.float32)
    ang = jnp.concatenate([row[:, None] * inv, col[:, None] * inv], axis=-1)
    return jnp.cos(ang), jnp.sin(ang)


def apply_rope(x, rope):
    cos, sin = rope
    cos = cos[None, :, None, :]
    sin = sin[None, :, None, :]
    x1 = x[..., 0::2]
    x2 = x[..., 1::2]
    return jnp.stack([x1 * cos - x2 * sin, x1 * sin + x2 * cos], axis=-1).reshape(x.shape)


def mla_kv(z_ckv, z_kr, p, rope):
    Bsz, T, _ = z_ckv.shape
    kv = (rms_norm(z_ckv, p["mla_g_kva"]) @ p["mla_w_ukv"]).reshape(Bsz, T, MLA_HEADS, MLA_NOPE_DIM + MLA_V_DIM)
    k_nope = rms_norm(kv[..., :MLA_NOPE_DIM], p["mla_g_kn"])
    v = kv[..., MLA_NOPE_DIM:]
    k_rope = rms_norm(z_kr, p["mla_g_kr"])[:, :, None, :]
    if rope is not None:
        k_rope = apply_rope(k_rope, rope)
    k = jnp.concatenate([k_nope, jnp.broadcast_to(k_rope, (Bsz, T, MLA_HEADS, MLA_ROPE_DIM))], axis=-1)
    return k, v


def mla_q(z_cq, p, rope):
    Bsz, T, _ = z_cq.shape
    q = (rms_norm(z_cq, p["mla_g_qa"]) @ p["mla_w_uq"]).reshape(Bsz, T, MLA_HEADS, MLA_QK_DIM)
    q_nope = rms_norm(q[..., :MLA_NOPE_DIM], p["mla_g_qn"])
    q_rope = rms_norm(q[..., MLA_NOPE_DIM:], p["mla_g_qr"])
    if rope is not None:
        q_rope = apply_rope(q_rope, rope)
    return jnp.concatenate([q_nope, q_rope], axis=-1)


def dense_block_attention(q, k, v):
    Bsz, S, H, dq = q.shape
    nb = S // BLOCK
    qb = jnp.swapaxes(q.reshape(Bsz, nb, BLOCK, H, dq), 0, 1)

    def one_block(qi):
        s = jnp.einsum("bqhd,bkhd->bhqk", qi, k).astype(jnp.float32) * MLA_SCALE
        pr = jax.nn.softmax(s, axis=-1)
        return jnp.einsum("bhqk,bkhd->bqhd", pr, v)

    o = lax.map(one_block, qb)
    return jnp.swapaxes(o, 0, 1).reshape(Bsz, S, H * v.shape[-1])


def context_attention(q, k, v):
    Bsz, T, H, _ = q.shape
    s = jnp.einsum("bqhd,bkhd->bhqk", q, k).astype(jnp.float32) * MLA_SCALE
    pr = jax.nn.softmax(s, axis=-1)
    return jnp.einsum("bhqk,bkhd->bqhd", pr, v).reshape(Bsz, T, H * v.shape[-1])


def win_kv(z_k, z_v, p, rope):
    Bsz, T, _ = z_k.shape
    k = rms_norm(z_k.reshape(Bsz, T, WIN_KV_HEADS, WIN_HEAD_DIM), p["win_g_k"])
    if rope is not None:
        k = apply_rope(k, rope)
    v = z_v.reshape(Bsz, T, WIN_KV_HEADS, WIN_HEAD_DIM)
    return k, v


def win_q(z_q, p, rope):
    Bsz, T, _ = z_q.shape
    q = rms_norm(z_q.reshape(Bsz, T, WIN_HEADS, WIN_HEAD_DIM), p["win_g_q"])
    if rope is not None:
        q = apply_rope(q, rope)
    return q.reshape(Bsz, T, WIN_KV_HEADS, WIN_GROUP, WIN_HEAD_DIM)


def window_block_attention(q, k, v, kc, vc, sink):
    Bsz, S, Hkv, G, d = q.shape
    nb = S // BLOCK
    n_ctx = kc.shape[1]
    pad = ((0, 0), (BLOCK, BLOCK), (0, 0), (0, 0))
    kp = jnp.pad(k, pad)
    vp = jnp.pad(v, pad)
    qb = jnp.swapaxes(q.reshape(Bsz, nb, BLOCK, Hkv, G, d), 0, 1)
    q_off = jnp.arange(BLOCK)
    k_off = jnp.arange(3 * BLOCK) - BLOCK
    band = jnp.abs(q_off[:, None] - k_off[None, :]) <= WINDOW
    sink_col = jnp.broadcast_to(sink.astype(jnp.float32)[None, :, :, None, None], (Bsz, Hkv, G, BLOCK, 1))

    def one_block(args):
        i, qi = args
        start = i * BLOCK
        kw = lax.dynamic_slice_in_dim(kp, start, 3 * BLOCK, axis=1)
        vw = lax.dynamic_slice_in_dim(vp, start, 3 * BLOCK, axis=1)
        k_abs = start + k_off
        valid = band & ((k_abs >= 0) & (k_abs < S))[None, :]
        s_win = jnp.einsum("bqngd,bknd->bngqk", qi, kw).astype(jnp.float32) * WIN_SCALE
        s_win = jnp.where(valid, s_win, NEG_INF)
        s_ctx = jnp.einsum("bqngd,bknd->bngqk", qi, kc).astype(jnp.float32) * WIN_SCALE
        pr = jax.nn.softmax(jnp.concatenate([s_ctx, s_win, sink_col], axis=-1), axis=-1)
        return (jnp.einsum("bngqk,bknd->bqngd", pr[..., :n_ctx], vc)
                + jnp.einsum("bngqk,bknd->bqngd", pr[..., n_ctx:n_ctx + 3 * BLOCK], vw))

    o = lax.map(one_block, (jnp.arange(nb), qb))
    return jnp.swapaxes(o, 0, 1).reshape(Bsz, S, Hkv * G * d)


def sink_context_attention(q, k, v, sink):
    Bsz, T, Hkv, G, d = q.shape
    s = jnp.einsum("bqngd,bknd->bngqk", q, k).astype(jnp.float32) * WIN_SCALE
    sink_col = jnp.broadcast_to(sink.astype(jnp.float32)[None, :, :, None, None], (Bsz, Hkv, G, T, 1))
    pr = jax.nn.softmax(jnp.concatenate([s, sink_col], axis=-1), axis=-1)
    return jnp.einsum("bngqk,bknd->bqngd", pr[..., :-1], v).reshape(Bsz, T, Hkv * G * d)


def attention_pair_mixer(h, hc, p, rope_mla, rope_win, ctx_out):
    Bsz, S, _ = h.shape
    z = h @ p["w_in"]
    ckv, kr, wk, wv, cq, wq = split_cols(z, PAIR_KV_SIZES + PAIR_Q_SIZES)
    zc = hc @ (p["w_in"] if ctx_out else p["w_in"][:, :PAIR_KV_COLS])
    ckv_c, kr_c, wk_c, wv_c = split_cols(zc, PAIR_KV_SIZES)
    k_a, v_a = mla_kv(ckv, kr, p, rope_mla)
    kc_a, vc_a = mla_kv(ckv_c, kr_c, p, None)
    o_a = dense_block_attention(mla_q(cq, p, rope_mla),
                                jnp.concatenate([kc_a, k_a], axis=1),
                                jnp.concatenate([vc_a, v_a], axis=1))
    k_b, v_b = win_kv(wk, wv, p, rope_win)
    kc_b, vc_b = win_kv(wk_c, wv_c, p, None)
    o_b = window_block_attention(win_q(wq, p, rope_win), k_b, v_b, kc_b, vc_b, p["win_sink"])
    y = jnp.concatenate([o_a, o_b], axis=-1) @ p["w_out"]
    if not ctx_out:
        return y, None
    cq_c, wq_c = split_cols(zc[..., PAIR_KV_COLS:], PAIR_Q_SIZES)
    oc_a = context_attention(mla_q(cq_c, p, None), kc_a, vc_a)
    oc_b = sink_context_attention(win_q(wq_c, p, None), kc_b, vc_b, p["win_sink"])
    yc = jnp.concatenate([oc_a, oc_b], axis=-1) @ p["w_out"]
    return y, yc


def gla_chunked(q, k, v, g, s0):
    Bsz, T, H, _ = q.shape
    dv = v.shape[-1]
    n = T // GLA_CHUNK

    def to_chunks(a):
        return a.astype(jnp.float32).reshape(Bsz, n, GLA_CHUNK, H, a.shape[-1]).transpose(1, 0, 3, 2, 4)

    qc, kc, vc = to_chunks(q), to_chunks(k), to_chunks(v)
    bc = jnp.cumsum(to_chunks(g), axis=3)
    lower = jnp.tril(jnp.ones((GLA_CHUNK, GLA_CHUNK), dtype=bool))

    def step(s, inp):
        qq, kk, vv, bb = inp
        b_end = bb[:, :, -1:, :]
        q_dec = qq * jnp.exp(bb)
        k_inv = kk * jnp.exp(-bb)
        k_end = kk * jnp.exp(b_end - bb)
        a = jnp.where(lower, jnp.einsum("bhtk,bhsk->bhts", q_dec, k_inv), 0.0)
        o = jnp.einsum("bhts,bhsv->bhtv", a, vv) + jnp.einsum("bhtk,bhkv->bhtv", q_dec, s)
        s = jnp.exp(b_end[:, :, 0, :])[..., None] * s + jnp.einsum("bhsk,bhsv->bhkv", k_end, vv)
        return s, o

    s_final, o = lax.scan(step, s0.astype(jnp.float32), (qc, kc, vc, bc))
    return o.transpose(1, 0, 3, 2, 4).reshape(Bsz, T, H, dv), s_final


def gla_final_state(k, v, g):
    G = jnp.cumsum(g.astype(jnp.float32), axis=1)
    w = jnp.exp(G[:, -1:] - G)
    return jnp.einsum("bthk,bthv->bhkv", k.astype(jnp.float32) * w, v.astype(jnp.float32))


def gla_gate(z_low, w_up, b):
    return jax.nn.log_sigmoid((z_low @ w_up + b).astype(jnp.float32)) / GLA_GATE_NORM


def gla_mixer(h, hc, p, ctx_out):
    Bsz, S, _ = h.shape

    def heads(t):
        return t.reshape(t.shape[0], t.shape[1], GLA_HEADS, -1)

    def flip(t):
        return jnp.flip(t, axis=1)

    q_scale = GLA_DK_HEAD ** -0.5
    z = h @ p["w_in"]
    k, v, lf, lb, q, g = split_cols(z, GLA_KV_SIZES + GLA_Q_SIZES)
    k, v, q = heads(k), heads(v), heads(q) * q_scale
    gf = heads(gla_gate(lf, p["w_gk_f"], p["b_gk_f"]))
    gb = heads(gla_gate(lb, p["w_gk_b"], p["b_gk_b"]))
    zc = hc @ (p["w_in"] if ctx_out else p["w_in"][:, :GLA_KV_COLS])
    kc, vc, lfc, lbc = split_cols(zc, GLA_KV_SIZES)
    kc, vc = heads(kc), heads(vc)
    gfc = heads(gla_gate(lfc, p["w_gk_f"], p["b_gk_f"]))
    gbc = heads(gla_gate(lbc, p["w_gk_b"], p["b_gk_b"]))
    yc = None
    if ctx_out:
        qc_raw, g_c = split_cols(zc[..., GLA_KV_COLS:], GLA_Q_SIZES)
        qc = heads(qc_raw) * q_scale
        zero = jnp.zeros((Bsz, GLA_HEADS, GLA_DK_HEAD, GLA_DV_HEAD), jnp.float32)
        oc_f, s_f = gla_chunked(qc, kc, vc, gfc, zero)
        oc_b, s_b = gla_chunked(flip(qc), flip(kc), flip(vc), flip(gbc), zero)
        oc = rms_norm(oc_f + flip(oc_b), p["g_norm"]) * jax.nn.silu(heads(g_c))
        yc = oc.reshape(Bsz, hc.shape[1], GLA_DV) @ p["w_out"]
    else:
        s_f = gla_final_state(kc, vc, gfc)
        s_b = gla_final_state(flip(kc), flip(vc), flip(gbc))
    o_f, _ = gla_chunked(q, k, v, gf, s_f)
    o_b, _ = gla_chunked(flip(q), flip(k), flip(v), flip(gb), s_b)
    o = rms_norm(o_f + flip(o_b), p["g_norm"]) * jax.nn.silu(heads(g))
    y = o.reshape(Bsz, S, GLA_DV) @ p["w_out"]
    return y, yc


def pre_mod(t, m, norm_g, i):
    h = rms_norm(t, norm_g[i]) * (1.0 + m[..., 3 * i + 1, :]) + m[..., 3 * i, :]
    return h, m[..., 3 * i + 2, :]


def trunk_layer(x, xc, c, c_ctx, p, layer, ropes, ctx_out):
    Bsz = x.shape[0]
    mod = (jax.nn.silu(c) @ p["w_mod"] + p["b_mod"]).reshape(Bsz, 1, N_MOD, D_MODEL)
    mod_c = (jax.nn.silu(c_ctx) @ p["w_mod"] + p["b_mod"]).reshape(1, 1, N_MOD, D_MODEL)
    ng = p["norm_g"]
    h, gt = pre_mod(x, mod, ng, 0)
    hc, gtc = pre_mod(xc, mod_c, ng, 0)
    x = x + 0.5 * gt * swiglu(h, p["ffn1_w_gu"], p["ffn1_w_down"])
    xc = xc + 0.5 * gtc * swiglu(hc, p["ffn1_w_gu"], p["ffn1_w_down"])
    h, gt = pre_mod(x, mod, ng, 1)
    hc, gtc = pre_mod(xc, mod_c, ng, 1)
    if layer % 2 == 0:
        y, yc = attention_pair_mixer(h, hc, p, ropes[0], ropes[1], ctx_out)
    else:
        y, yc = gla_mixer(h, hc, p, ctx_out)
    x = x + gt * y
    h, gt = pre_mod(x, mod, ng, 2)
    x = x + 0.5 * gt * swiglu(h, p["ffn2_w_gu"], p["ffn2_w_down"])
    if ctx_out:
        xc = xc + gtc * yc
        hc, gtc = pre_mod(xc, mod_c, ng, 2)
        xc = xc + 0.5 * gtc * swiglu(hc, p["ffn2_w_gu"], p["ffn2_w_down"])
    return x, xc


def _dense(key, fan_in, shape, gain=1.0):
    return jax.random.normal(key, shape, jnp.float32) * (gain * fan_in ** -0.5)


def _gain(key, shape):
    return 1.0 + 0.05 * jax.random.normal(key, shape, jnp.float32)


def _common_params(key, pre):
    ks = jax.random.split(key, 7)
    return {
        pre + "norm_g": _gain(ks[0], (3, D_MODEL)),
        pre + "w_mod": _dense(ks[1], D_MODEL, (D_MODEL, N_MOD * D_MODEL), 0.5),
        pre + "b_mod": 0.02 * jax.random.normal(ks[2], (N_MOD * D_MODEL,), jnp.float32),
        pre + "ffn1_w_gu": _dense(ks[3], D_MODEL, (D_MODEL, 2 * D_FF)),
        pre + "ffn1_w_down": _dense(ks[4], D_FF, (D_FF, D_MODEL)),
        pre + "ffn2_w_gu": _dense(ks[5], D_MODEL, (D_MODEL, 2 * D_FF)),
        pre + "ffn2_w_down": _dense(ks[6], D_FF, (D_FF, D_MODEL)),
    }


def _pair_params(key, pre):
    ks = jax.random.split(key, 13)
    return {
        pre + "w_in": _dense(ks[0], D_MODEL, (D_MODEL, PAIR_IN)),
        pre + "mla_g_qa": _gain(ks[1], (MLA_Q_RANK,)),
        pre + "mla_g_kva": _gain(ks[2], (MLA_KV_RANK,)),
        pre + "mla_w_uq": _dense(ks[3], MLA_Q_RANK, (MLA_Q_RANK, MLA_HEADS * MLA_QK_DIM)),
        pre + "mla_w_ukv": _dense(ks[4], MLA_KV_RANK, (MLA_KV_RANK, MLA_HEADS * (MLA_NOPE_DIM + MLA_V_DIM))),
        pre + "mla_g_qn": _gain(ks[5], (MLA_NOPE_DIM,)),
        pre + "mla_g_qr": _gain(ks[6], (MLA_ROPE_DIM,)),
        pre + "mla_g_kn": _gain(ks[7], (MLA_NOPE_DIM,)),
        pre + "mla_g_kr": _gain(ks[8], (MLA_ROPE_DIM,)),
        pre + "win_g_q": _gain(ks[9], (WIN_HEAD_DIM,)),
        pre + "win_g_k": _gain(ks[10], (WIN_HEAD_DIM,)),
        pre + "win_sink": 0.5 * jax.random.normal(ks[11], (WIN_KV_HEADS, WIN_GROUP), jnp.float32),
        pre + "w_out": _dense(ks[12], PAIR_OUT, (PAIR_OUT, D_MODEL)),
    }


def _gla_params(key, pre):
    ks = jax.random.split(key, 7)
    return {
        pre + "w_in": _dense(ks[0], D_MODEL, (D_MODEL, GLA_IN)),
        pre + "w_gk_f": _dense(ks[1], GLA_LOWRANK, (GLA_LOWRANK, GLA_DK)),
        pre + "b_gk_f": 0.1 * jax.random.normal(ks[2], (GLA_DK,), jnp.float32),
        pre + "w_gk_b": _dense(ks[3], GLA_LOWRANK, (GLA_LOWRANK, GLA_DK)),
        pre + "b_gk_b": 0.1 * jax.random.normal(ks[4], (GLA_DK,), jnp.float32),
        pre + "g_norm": _gain(ks[5], (GLA_DV_HEAD,)),
        pre + "w_out": _dense(ks[6], GLA_DV, (GLA_DV, D_MODEL)),
    }


def setup_inputs(seed: int = 0) -> dict:
    key = jax.random.key(seed)
    k_x, k_c, k_ctx, k_cc, k_0a, k_0b, k_1a, k_1b = jax.random.split(key, 8)
    out = {
        "x": jax.random.normal(k_x, (BATCH, SEQ, D_MODEL), jnp.float32),
        "c": jax.random.normal(k_c, (BATCH, D_MODEL), jnp.float32),
        "ctx": jax.random.normal(k_ctx, (BATCH, CTX_LEN, D_MODEL), jnp.float32),
        "c_ctx": jax.random.normal(k_cc, (D_MODEL,), jnp.float32),
    }
    out.update(_common_params(k_0a, "l0_"))
    out.update(_pair_params(k_0b, "l0_"))
    out.update(_common_params(k_1a, "l1_"))
    out.update(_gla_params(k_1b, "l1_"))
    return out


def reference(x, c, ctx, c_ctx,
              l0_norm_g, l0_w_mod, l0_b_mod, l0_ffn1_w_gu, l0_ffn1_w_down, l0_ffn2_w_gu, l0_ffn2_w_down,
              l0_w_in, l0_mla_g_qa, l0_mla_g_kva, l0_mla_w_uq, l0_mla_w_ukv, l0_mla_g_qn, l0_mla_g_qr,
              l0_mla_g_kn, l0_mla_g_kr, l0_win_g_q, l0_win_g_k, l0_win_sink, l0_w_out,
              l1_norm_g, l1_w_mod, l1_b_mod, l1_ffn1_w_gu, l1_ffn1_w_down, l1_ffn2_w_gu, l1_ffn2_w_down,
              l1_w_in, l1_w_gk_f, l1_b_gk_f, l1_w_gk_b, l1_b_gk_b, l1_g_norm, l1_w_out):
    out_dtype = x.dtype
    rows = x.shape[1] // GRID_W
    ropes = (axial_rope_tables(rows, MLA_ROPE_DIM), axial_rope_tables(rows, WIN_HEAD_DIM))
    layers = [
        dict(norm_g=l0_norm_g, w_mod=l0_w_mod, b_mod=l0_b_mod,
             ffn1_w_gu=l0_ffn1_w_gu, ffn1_w_down=l0_ffn1_w_down,
             ffn2_w_gu=l0_ffn2_w_gu, ffn2_w_down=l0_ffn2_w_down,
             w_in=l0_w_in, mla_g_qa=l0_mla_g_qa, mla_g_kva=l0_mla_g_kva,
             mla_w_uq=l0_mla_w_uq, mla_w_ukv=l0_mla_w_ukv,
             mla_g_qn=l0_mla_g_qn, mla_g_qr=l0_mla_g_qr, mla_g_kn=l0_mla_g_kn, mla_g_kr=l0_mla_g_kr,
             win_g_q=l0_win_g_q, win_g_k=l0_win_g_k, win_sink=l0_win_sink, w_out=l0_w_out),
        dict(norm_g=l1_norm_g, w_mod=l1_w_mod, b_mod=l1_b_mod,
             ffn1_w_gu=l1_ffn1_w_gu, ffn1_w_down=l1_ffn1_w_down,
             ffn2_w_gu=l1_ffn2_w_gu, ffn2_w_down=l1_ffn2_w_down,
             w_in=l1_w_in, w_gk_f=l1_w_gk_f, b_gk_f=l1_b_gk_f, w_gk_b=l1_w_gk_b, b_gk_b=l1_b_gk_b,
             g_norm=l1_g_norm, w_out=l1_w_out),
    ]
    xc = ctx
    for layer in range(DEPTH):
        x, xc = trunk_layer(x, xc, c, c_ctx, layers[layer], layer, ropes, layer < DEPTH - 1)
    return x.astype(out_dtype)
```

```python
import numpy as np
from contextlib import ExitStack
import ml_dtypes
import concourse.bass as bass
import concourse.mybir as mybir
from concourse.bass_utils import run_bass_kernel_spmd

F32 = mybir.dt.float32
BF16 = mybir.dt.bfloat16
AF = mybir.ActivationFunctionType
ALU = mybir.AluOpType

D = 1024
DFF = 2816
NF = DFF // 128
LC = 256
S = 8192
T = LC + S
EPS = 1e-6
NCORES = 8

ENGS = ("pe", "act", "dve", "pool", "sp")


class Tok:
    __slots__ = ("name", "lw", "rd", "dom")

    def __init__(self, name):
        self.name = name
        self.lw = None
        self.rd = {}
        self.dom = None


class Op:
    __slots__ = ("eng", "fn", "deps", "inc", "dom", "val", "sem", "pas", "isdma")


class SemPool:
    def __init__(self, nc, es, n):
        self.items = [[es.enter_context(nc.semaphore("sp%d" % i)), 0, False] for i in range(n)]

    def acquire(self):
        free = [it for it in self.items if not it[2]]
        it = min(free, key=lambda x: x[1])
        it[2] = True
        return it


SEMPOOL = [None]


class Pass:
    def __init__(self, nc, name):
        self.nc = nc
        self.sems = []
        self.name = name
        self.es = ExitStack()
        self.streams = {e: [] for e in ENGS}
        self.doms = {}
        self.last = {}
        self.nalloc = 0

    def sb(self, name, shape, dt):
        return self.es.enter_context(self.nc.sbuf_tensor(self.name + "_" + name, list(shape), dt))

    def ps(self, name, shape, dt=F32):
        return self.es.enter_context(self.nc.psum_tensor(self.name + "_" + name, list(shape), dt))

    def tok(self, name):
        return Tok(name)

    def _record(self, o, reads, writes):
        deps = set()
        for t in reads:
            if t.lw is not None:
                deps.add(t.lw)
        for t in writes:
            if t.lw is not None:
                deps.add(t.lw)
            for r in t.rd.values():
                deps.add(r)
        deps.discard(o)
        deps = [d for d in deps if d.pas is self]
        for d in deps:
            d.inc = True
        o.deps = deps
        for t in reads:
            if t not in writes:
                t.rd[o.dom] = o
        for t in writes:
            t.lw = o
            t.rd = {}
        self.streams[o.eng].append(o)
        self.last[o.dom] = o

    def op(self, eng, fn, reads=(), writes=()):
        o = Op()
        o.eng = eng
        o.fn = fn
        o.inc = False
        o.dom = eng
        o.pas = self
        o.isdma = False
        o.val = None
        o.sem = None
        self._record(o, list(reads), list(writes))
        return o

    def dma(self, out, in_, reads=(), writes=(), dom=None, q="sp"):
        assert dom is not None
        if dom.dom is None or dom.dom[0] is not self:
            it = SEMPOOL[0].acquire()
            self.sems.append(it)
            sem = it
            dom.dom = (self, sem, [0], "dma%d" % len(self.doms))
            self.doms[dom.dom[3]] = dom.dom
        o = Op()
        o.eng = q
        o.fn = lambda e, out=out, in_=in_: e.dma_start(out=out, in_=in_)
        o.inc = True
        o.dom = dom.dom[3]
        o.pas = self
        o.isdma = True
        dom.dom[2][0] += 16
        o.val = dom.dom[2][0]
        o.sem = dom.dom[1]
        self._record(o, list(reads), list(writes))
        return o

    def emit(self):
        nc = self.nc
        engsem = {}
        for e in ("pe", "act", "dve", "pool"):
            engsem[e] = SEMPOOL[0].acquire()
            self.sems.append(engsem[e])
        for e in ("pe", "act", "dve", "pool", "sp"):
            cnt = 0
            for o in self.streams[e]:
                if o.isdma:
                    continue
                if o.fn is None:
                    continue
                if o.inc:
                    cnt += 1
                    o.val = cnt
                    o.sem = engsem[e]
        finals = []
        for dom, o in self.last.items():
            if o.fn is None:
                continue
            if not o.isdma and not o.inc:
                o.inc = True
            finals.append(o)
        for e in ("pe", "act", "dve", "pool", "sp"):
            cnt = 0
            for o in self.streams[e]:
                if o.isdma or o.fn is None:
                    continue
                if o.inc:
                    cnt += 1
                    o.val = cnt
                    o.sem = engsem[e]
        handles = {"pe": nc.tensor, "act": nc.scalar, "dve": nc.vector, "pool": nc.gpsimd, "sp": nc.sync}
        streams = self.streams

        def run(ename, eh):
            seen = {}
            for o in streams[ename]:
                for d in o.deps:
                    if (not d.isdma) and d.eng == "pe" and ename == "pe" and not o.isdma:
                        continue
                    k = id(d.sem)
                    if seen.get(k, 0) < d.val:
                        eh.wait_ge(d.sem[0], d.sem[1] + d.val)
                        seen[k] = d.val
                if o.fn is None:
                    continue
                ins = o.fn(eh)
                if o.isdma:
                    ins.then_inc(o.sem[0], 16)
                elif o.inc:
                    ins.then_inc(o.sem[0], 1)
            for d in finals:
                k = id(d.sem)
                if seen.get(k, 0) < d.val:
                    eh.wait_ge(d.sem[0], d.sem[1] + d.val)
                    seen[k] = d.val

        with nc.Block() as blk:
            @blk.tensor
            def _(e):
                run("pe", e)

            @blk.scalar
            def _(e):
                run("act", e)

            @blk.vector
            def _(e):
                run("dve", e)

            @blk.gpsimd
            def _(e):
                run("pool", e)

            @blk.sync
            def _(e):
                run("sp", e)
        tot = {}
        for e in ENGS:
            for o in streams[e]:
                if o.fn is None:
                    continue
                if o.isdma:
                    tot[id(o.sem)] = tot.get(id(o.sem), 0) + 16
                elif o.inc:
                    tot[id(o.sem)] = tot.get(id(o.sem), 0) + 1
        for it in self.sems:
            it[1] += tot.get(id(it), 0)
            it[2] = False
            assert it[1] < 60000, it[1]

    def close(self):
        self.es.close()


def mm(P, out, lhsT, rhs, start, stop, reads, writes, sgc=False):
    if sgc:
        return P.op("pe", lambda e: e.matmul(out, lhsT, rhs, start=start, stop=stop, skip_group_check=True), reads, writes)
    return P.op("pe", lambda e: e.matmul(out, lhsT, rhs, start=start, stop=stop), reads, writes)


def token_tiles():
    tiles = [(0, LC, True)]
    for i in range(S // 512):
        tiles.append((LC + i * 512, 512, False))
    return tiles


def mod_pass(nc, G, dr):
    P = Pass(nc, "pm")
    c2 = P.sb("c2", [128, 2, 8], F32)
    r2 = P.sb("r2", [128, 2, 8], F32)
    tc2 = P.tok("c2")
    tr2 = P.tok("r2")
    P.dma(c2[:, 0, :], dr["c8"], writes=[tc2], dom=tc2)
    P.dma(c2[:, 1, :], dr["cc8"], writes=[tc2], dom=tc2)
    P.op("act", lambda e: e.activation(r2[:], c2[:], AF.Silu), [tc2], [tr2])
    GW = 1152
    NG = 9216 // GW
    wbuf = [P.sb("w%d" % i, [128, 8, GW], F32) for i in range(2)]
    twb = [P.tok("w%d" % i) for i in range(2)]
    bm = P.sb("bm", [128, 2, 72], F32)
    ng = P.sb("ng", [128, 2, 24], F32)
    tbm = P.tok("bm")
    for l in range(2):
        P.dma(bm[:, l, :], dr["l%d_bmod" % l], writes=[tbm], dom=tbm)
        P.dma(ng[:, l, :], dr["l%d_ng" % l], writes=[tbm], dom=tbm)
    pm = P.ps("pm", [128, 2, 72, 2], F32)
    tpm = P.tok("pm")
    it = 0
    for l in range(2):
        wm = dr["l%d_wmod" % l]
        for g in range(NG):
            b = it % 2
            it += 1
            for k in range(8):
                P.dma(wbuf[b][:, k, :], wm[k * 128:(k + 1) * 128, g * GW:(g + 1) * GW], writes=[twb[b]], dom=twb[b])
            for cb in range(GW // 128):
                col = g * (GW // 128) + cb
                for k in range(8):
                    mm(P, pm[:, l, col, :], wbuf[b][:, k, cb * 128:(cb + 1) * 128], r2[:, :, k],
                       k == 0, k == 7, [twb[b], tr2], [tpm])
    tG = G["tok"]
    for l in range(2):
        for w in range(2):
            P.op("dve", lambda e, l=l, w=w: e.tensor_tensor(G["MOD"][:, l, w, :], pm[:, l, :, w], bm[:, l, :], ALU.add),
                 [tpm, tbm], [tG])
    for l in range(2):
        for w in range(2):
            for i in range(3):
                sh = G["MOD"][:, l, w, (3 * i) * 8:(3 * i) * 8 + 8]
                sc = G["MOD"][:, l, w, (3 * i + 1) * 8:(3 * i + 1) * 8 + 8]
                gt = G["MOD"][:, l, w, (3 * i + 2) * 8:(3 * i + 2) * 8 + 8]
                P.op("dve", lambda e, l=l, w=w, i=i, sc=sc: e.scalar_tensor_tensor(
                    G["A"][:, l, w, i, :], sc, 1.0, ng[:, l, i * 8:(i + 1) * 8], ALU.add, ALU.mult), [tG, tbm], [tG])
                P.op("dve", lambda e, l=l, w=w, i=i, sh=sh: e.tensor_copy(G["B"][:, l, w, i, :], sh), [tG], [tG])
                fac = 1.0 if i == 1 else 0.5
                P.op("dve", lambda e, l=l, w=w, i=i, gt=gt, fac=fac: e.tensor_scalar(
                    G["GT"][:, l, w, i, :], gt, fac, None, ALU.mult), [tG], [tG])
    P.emit()
    P.close()


def emit_prenorm(P, R, G, l, w, i, x, tx, h, th, n, tag=""):
    sq, tsq = R["sq"], R["tsq"]
    for k in range(8):
        b = k % 2
        P.op("act", lambda e, k=k, b=b: e.activation(sq[b][:, :n], x[:, k, :n], AF.Square), [tx], [tsq[b]])
        mm(P, R["pss"][:, :n], G["ones1024"][:], sq[b][:, :n], k == 0, k == 7, [tsq[b], G["tokc"]], [R["tpss"]])
    emit_rsqrt(P, R["rstd"][:, :n], R["pss"][:, :n], R["rtmp"][:, :n], R["tpss"], R["trstd"], R["trtmp"])
    for k in range(8):
        b = k % 2
        tt, ttt = R["tt"], R["ttt"]
        P.op("dve", lambda e, k=k, b=b: e.scalar_tensor_tensor(
            tt[b][:, :n], x[:, k, :n], G["A"][:, l, w, i, k:k + 1], R["rstd"][:, :n], ALU.mult, ALU.mult),
            [tx, R["trstd"], G["tok"]], [ttt[b]])
        P.op("act", lambda e, k=k, b=b: e.activation(
            h[:, k, :n], tt[b][:, :n], AF.Identity, bias=G["B"][:, l, w, i, k:k + 1], scale=1.0),
            [ttt[b], G["tok"]], [th])


def emit_rsqrt(P, out, in_ps, tmp, tin, tout, ttmp):
    P.op("dve", lambda e: e.tensor_scalar(tmp, in_ps, EPS, None, ALU.add), [tin], [ttmp])
    P.op("act", lambda e: e.activation(tmp, tmp, AF.Sqrt), [ttmp], [ttmp])
    P.op("dve", lambda e: e.reciprocal(out, tmp), [ttmp], [tout])


def alloc_norm_scratch(P):
    R = {}
    R["rtmp"] = P.sb("rtmp", [128, 512], F32)
    R["trtmp"] = P.tok("rtmp")
    R["sq"] = [P.sb("sq%d" % i, [128, 512], BF16) for i in range(2)]
    R["tsq"] = [P.tok("sq%d" % i) for i in range(2)]
    R["tt"] = [P.sb("tt%d" % i, [128, 512], F32) for i in range(2)]
    R["ttt"] = [P.tok("tt%d" % i) for i in range(2)]
    R["rstd"] = P.sb("rstd", [128, 512], F32)
    R["trstd"] = P.tok("rstd")
    R["pss"] = P.ps("pss", [128, 512], F32)
    R["tpss"] = P.tok("pss")
    return R


def ffn_pass(nc, G, name, Xin, Xout, wgu_d, wd_d, l, i, tiles, xout_off=0):
    P = Pass(nc, name)
    Wgu = P.sb("wgu", [128, 8, 2 * DFF], BF16)
    Wd = P.sb("wd", [128, NF, D], BF16)
    tW = P.tok("W")
    act = P.sb("act", [128, NF, 512], BF16)
    tact = [P.tok("act%d" % j) for j in range(NF)]
    stage = act[:].rearrange("p a b -> p (a b)").bitcast(F32)
    SW = 1408
    tst = [P.tok("st%d" % j) for j in range(4)]
    jobs = []
    for k in range(8):
        for cb in range(2 * DFF // SW):
            jobs.append((wgu_d[k * 128:(k + 1) * 128, cb * SW:(cb + 1) * SW], Wgu[:, k, cb * SW:(cb + 1) * SW], SW))
    for f in range(NF):
        jobs.append((wd_d[f * 128:(f + 1) * 128, :], Wd[:, f, :], D))
    for j, (src, dst, wdt) in enumerate(jobs):
        s = j % 4
        sv = stage[:, s * SW:s * SW + wdt]
        P.dma(sv, src, writes=[tst[s]], dom=tst[s])
        eng = "dve" if j % 2 == 0 else "pool"
        P.op(eng, lambda e, dst=dst, sv=sv: e.tensor_copy(dst, sv), [tst[s]], [tW])
    for j in range(NF):
        for s in range(4):
            pass
    R = alloc_norm_scratch(P)
    xn = P.sb("xn", [128, 8, 512], F32)
    txn = P.tok("xn")
    h = P.sb("h", [128, 8, 512], BF16)
    th = P.tok("h")
    NXR = 4
    xr = [P.sb("xr%d" % j, [128, 512], F32) for j in range(NXR)]
    txr = [P.tok("xr%d" % j) for j in range(NXR)]
    sl = [P.sb("sl%d" % j, [128, 512], F32) for j in range(2)]
    tsl = [P.tok("sl%d" % j) for j in range(2)]
    pg = [P.ps("pg%d" % j, [128, 512], F32) for j in range(2)]
    pu = [P.ps("pu%d" % j, [128, 512], F32) for j in range(2)]
    tpg = [P.tok("pg%d" % j) for j in range(2)]
    tpu = [P.tok("pu%d" % j) for j in range(2)]
    py = [P.ps("py%d" % j, [128, 512], F32) for j in range(2)]
    tpy = [P.tok("py%d" % j) for j in range(2)]

    def load_x(t0, n):
        for k in range(8):
            P.dma(xn[:, k, :n], Xin[k * 128:(k + 1) * 128, t0:t0 + n], writes=[txn], dom=txn)

    alias_ops = []
    load_x(tiles[0][0], tiles[0][1])
    st = {"cnt": 0, "ycnt": 0}

    def tile_body(ti, t0, n, isctx):
        w = 1 if isctx else 0
        emit_prenorm(P, R, G, l, w, i, xn, txn, h, th, n)
        if ti + 1 < len(tiles):
            load_x(tiles[ti + 1][0], tiles[ti + 1][1])
        for f in range(NF):
            b = st["cnt"] % 2
            st["cnt"] += 1
            for k in range(8):
                mm(P, pg[b][:, :n], Wgu[:, k, f * 128:(f + 1) * 128], h[:, k, :n], k == 0, k == 7, [tW, th], [tpg[b]])
            for k in range(8):
                mm(P, pu[b][:, :n], Wgu[:, k, DFF + f * 128:DFF + (f + 1) * 128], h[:, k, :n], k == 0, k == 7,
                   [tW, th], [tpu[b]])
            P.op("act", lambda e, b=b: e.activation(sl[b][:, :n], pg[b][:, :n], AF.Silu), [tpg[b]], [tsl[b]])
            extra_w = list(tst) if ti == 0 else []
            P.op("dve", lambda e, b=b, f=f: e.tensor_tensor(act[:, f, :n], pu[b][:, :n], sl[b][:, :n], ALU.mult),
                 [tpu[b], tsl[b]], [tact[f]] + extra_w)

        def load_xr(dc, yc):
            r = yc % NXR
            P.dma(xr[r][:, :n], Xin[dc * 128:(dc + 1) * 128, t0:t0 + n], writes=[txr[r]], dom=txr[r])
        load_xr(0, st["ycnt"])
        load_xr(1, st["ycnt"] + 1)
        for dc in range(8):
            b = st["ycnt"] % 2
            r = st["ycnt"] % NXR
            if dc + 2 < 8:
                load_xr(dc + 2, st["ycnt"] + 2)
            st["ycnt"] += 1
            for f in range(NF):
                mm(P, py[b][:, :n], Wd[:, f, dc * 128:(dc + 1) * 128], act[:, f, :n], f == 0, f == NF - 1,
                   [tW, tact[f]], [tpy[b]])
            P.op("dve", lambda e, b=b, r=r, dc=dc: e.scalar_tensor_tensor(
                xr[r][:, :n], py[b][:, :n], G["GT"][:, l, w, i, dc:dc + 1], xr[r][:, :n], ALU.mult, ALU.add),
                [tpy[b], txr[r], G["tok"]], [tpy[b], txr[r]])
            P.dma(Xout[dc * 128:(dc + 1) * 128, t0 - xout_off:t0 - xout_off + n], xr[r][:, :n], reads=[txr[r]],
                  dom=txr[r])

    for ti, (t0, n, isctx) in enumerate(tiles):
        tile_body(ti, t0, n, isctx)
    P.emit()
    P.close()


def load_cast_weights(P, jobs, stage_tiles, tst, tW):
    ns = len(stage_tiles)
    for j, (src, dst, wdt) in enumerate(jobs):
        s = j % ns
        sv = stage_tiles[s][:, :wdt]
        P.dma(sv, src, writes=[tst[s]], dom=tst[s])
        eng = "dve" if j % 2 == 0 else "pool"
        P.op(eng, lambda e, dst=dst, sv=sv: e.tensor_copy(dst, sv), [tst[s]], [tW])


class Rot:
    def __init__(self, P, name, shape, dt, nbuf, psum=False):
        self.t = [(P.ps if psum else P.sb)("%s%d" % (name, i), shape, dt) for i in range(nbuf)]
        self.k = [P.tok("%s%d" % (name, i)) for i in range(nbuf)]
        self.i = 0

    def next(self):
        j = self.i % len(self.t)
        self.i += 1
        return self.t[j], self.k[j]


def prep0_pass(nc, G, dr, SC, Xin, tiles):
    P = Pass(nc, "p0")
    NCH = 16
    W = P.sb("W", [128, 8, NCH * 128], BF16)
    Wuq = P.sb("Wuq", [128, 2, 1024], BF16)
    Wukv = P.sb("Wukv", [128, 1024], BF16)
    tW = P.tok("W")
    stg = [P.sb("stg%d" % j, [128, 2048], F32) for j in range(2)]
    tst = [P.tok("stg%d" % j) for j in range(2)]
    jobs = [(dr["l0_win"][k * 128:(k + 1) * 128, :], W[:, k, :], NCH * 128) for k in range(8)]
    jobs += [(dr["l0_wuq"][k * 128:(k + 1) * 128, :], Wuq[:, k, :], 1024) for k in range(2)]
    jobs += [(dr["l0_wukv"][:, :], Wukv[:, :], 1024)]
    load_cast_weights(P, jobs, stg, tst, tW)
    gv = P.sb("gv", [128, 16], F32)
    Ms = P.sb("Ms", [128, 4, 128], BF16)
    tcst = P.tok("cst")
    P.dma(gv[:], dr["l0_gv"], writes=[tcst], dom=tcst)
    P.dma(Ms[:], dr["Ms"].rearrange("m p c -> p m c"), writes=[tcst], dom=tcst)
    R = alloc_norm_scratch(P)
    xn = P.sb("xn", [128, 8, 512], F32)
    txn = P.tok("xn")
    h = P.sb("h", [128, 8, 512], BF16)
    th = P.tok("h")
    tabs = [[P.sb("tab%d_%d" % (b, j), [128, 512], F32) for j in range(4)] for b in range(2)]
    ttab = [P.tok("tab%d" % b) for b in range(2)]
    tabd = [dr["cosm"], dr["sinm"], dr["cosw"], dr["sinw"]]
    pz = Rot(P, "pz", [128, 512], F32, 4, psum=True)
    pss2 = Rot(P, "pss2", [128, 512], F32, 1, psum=True)
    pv = Rot(P, "pv", [128, 512], F32, 1, psum=True)
    sq2 = Rot(P, "sq2", [128, 512], BF16, 2)
    rst = Rot(P, "rst", [128, 512], F32, 2)
    rtm = Rot(P, "rtm", [128, 512], F32, 2)
    ta = Rot(P, "ta", [128, 512], F32, 2)
    tb = Rot(P, "tb", [128, 512], F32, 2)
    outb = Rot(P, "outb", [128, 512], BF16, 4)
    vout = Rot(P, "vout", [128, 512], BF16, 2)
    ckvn = P.sb("ckvn", [128, 512], BF16)
    tckvn = P.tok("ckvn")
    cqn = P.sb("cqn", [128, 2, 512], BF16)
    tcqn = P.tok("cqn")

    def load_x(t0, n):
        for k in range(8):
            P.dma(xn[:, k, :n], Xin[k * 128:(k + 1) * 128, t0:t0 + n], writes=[txn], dom=txn)

    def proj(chunk, n, rows=128):
        z, tz = pz.next()
        for k in range(8):
            mm(P, z[:rows, :n], W[:, k, chunk * 128:chunk * 128 + rows], h[:, k, :n], k == 0, k == 7, [tW, th], [tz])
        return z, tz

    def rstd_of(zs, M, rows, n):
        p2, tp2 = pss2.next()
        for j, (z, tz) in enumerate(zs):
            q, tq = sq2.next()
            P.op("act", lambda e, q=q, z=z: e.activation(q[:rows, :n], z[:rows, :n], AF.Square), [tz], [tq])
            mm(P, p2[:rows, :n], M[:rows, :rows], q[:rows, :n], j == 0, j == len(zs) - 1, [tq, tcst], [tp2])
        r, tr = rst.next()
        tm, ttm = rtm.next()
        emit_rsqrt(P, r[:rows, :n], p2[:rows, :n], tm[:rows, :n], tp2, tr, ttm)
        return r, tr

    def scale_out(z, tz, r, tr, gcol, dst, tdst, rows, n):
        P.op("dve", lambda e: e.scalar_tensor_tensor(dst, z[:rows, :n], gv[:rows, gcol:gcol + 1], r[:rows, :n],
                                                     ALU.mult, ALU.mult), [tz, tr, tcst], [tdst])

    def rope_out(z, tz, zs, tzs, r, tr, gcol, cos, sin, ttb, dst, tdst, rows, n):
        a, tka = ta.next()
        b, tkb = tb.next()
        scale_out(z, tz, r, tr, gcol, a[:rows, :n], tka, rows, n)
        scale_out(zs, tzs, r, tr, gcol + 1, b[:rows, :n], tkb, rows, n)
        P.op("pool", lambda e: e.tensor_tensor(a[:rows, :n], a[:rows, :n], cos[:rows, :n], ALU.mult), [tka, ttb], [tka])
        P.op("pool", lambda e: e.tensor_tensor(b[:rows, :n], b[:rows, :n], sin[:rows, :n], ALU.mult), [tkb, ttb], [tkb])
        P.op("pool", lambda e: e.tensor_tensor(dst, a[:rows, :n], b[:rows, :n], ALU.add), [tka, tkb], [tdst])

    def store(dst_dram, src, tsrc):
        P.dma(dst_dram, src, reads=[tsrc], dom=tsrc)

    def load_tabs(bi, t0, n):
        for j in range(4):
            P.dma(tabs[bi][j][:, :n], tabd[j][:, t0:t0 + n], writes=[ttab[bi]], dom=ttab[bi])

    M128, M64, M32, M256 = Ms[:, 0, :], Ms[:, 1, :], Ms[:, 2, :], Ms[:, 3, :]
    load_x(tiles[0][0], tiles[0][1])
    load_tabs(0, tiles[0][0], tiles[0][1])

    def tile_body(ti, t0, n, isctx):
        w = 1 if isctx else 0
        bi = ti % 2
        cm, sm, cw, sw = tabs[bi]
        ttb = ttab[bi]
        emit_prenorm(P, R, G, 0, w, 1, xn, txn, h, th, n)
        if ti + 1 < len(tiles):
            load_x(tiles[ti + 1][0], tiles[ti + 1][1])
            load_tabs(1 - bi, tiles[ti + 1][0], tiles[ti + 1][1])
        nsub = n // 128
        z, tz = proj(0, n)
        r, tr = rstd_of([(z, tz)], M128, 128, n)
        scale_out(z, tz, r, tr, 0, ckvn[:, :n], tckvn, 128, n)
        for j in range(4):
            z, tz = pz.next()
            mm(P, z[:, :n], Wukv[:, j * 128:(j + 1) * 128], ckvn[:, :n], True, True, [tW, tckvn], [tz])
            r, tr = rstd_of([(z, tz)], M64, 128, n)
            o, to = outb.next()
            scale_out(z, tz, r, tr, 9, o[:, :n], to, 128, n)
            store(SC["KN"][j, :, t0:t0 + n], o[:, :n], to)
        for sub in range(nsub):
            p, tp = pv.next()
            mm(P, p[:, :], ckvn[:, sub * 128:(sub + 1) * 128], Wukv[:, 512:1024], True, True, [tW, tckvn], [tp])
            v, tv = vout.next()
            P.op("act", lambda e, v=v, p=p: e.copy(v[:, :], p[:, :]), [tp], [tv])
            store(SC["VA"][t0 + sub * 128:t0 + (sub + 1) * 128, :], v[:, :], tv)
        z, tz = proj(1, n, 32)
        zs, tzs = proj(2, n, 32)
        r, tr = rstd_of([(z, tz)], M32, 32, n)
        o, to = outb.next()
        rope_out(z, tz, zs, tzs, r, tr, 1, cm, sm, ttb, o[:32, :n], to, 32, n)
        store(SC["KR"][:, t0:t0 + n], o[:32, :n], to)
        z, tz = proj(3, n)
        zs, tzs = proj(4, n)
        r, tr = rstd_of([(z, tz)], M64, 128, n)
        o, to = outb.next()
        rope_out(z, tz, zs, tzs, r, tr, 3, cw, sw, ttb, o[:, :n], to, 128, n)
        store(SC["KB"][:, t0:t0 + n], o[:, :n], to)
        z0, tz0 = proj(5, n)
        z1, tz1 = proj(6, n)
        r, tr = rstd_of([(z0, tz0), (z1, tz1)], M256, 128, n)
        scale_out(z0, tz0, r, tr, 5, cqn[:, 0, :n], tcqn, 128, n)
        scale_out(z1, tz1, r, tr, 6, cqn[:, 1, :n], tcqn, 128, n)
        for j in range(4):
            z, tz = pz.next()
            for k in range(2):
                mm(P, z[:, :n], Wuq[:, k, j * 128:(j + 1) * 128], cqn[:, k, :n], k == 0, k == 1, [tW, tcqn], [tz])
            r, tr = rstd_of([(z, tz)], M64, 128, n)
            o, to = outb.next()
            scale_out(z, tz, r, tr, 10, o[:, :n], to, 128, n)
            store(SC["QN"][j, :, t0:t0 + n], o[:, :n], to)
        for j in range(2):
            z, tz = pz.next()
            for k in range(2):
                mm(P, z[:, :n], Wuq[:, k, 512 + j * 128:512 + (j + 1) * 128], cqn[:, k, :n], k == 0, k == 1,
                   [tW, tcqn], [tz])
            zs, tzs = pz.next()
            for k in range(2):
                mm(P, zs[:, :n], Wuq[:, k, 768 + j * 128:768 + (j + 1) * 128], cqn[:, k, :n], k == 0, k == 1,
                   [tW, tcqn], [tzs])
            r, tr = rstd_of([(z, tz)], M32, 128, n)
            o, to = outb.next()
            rope_out(z, tz, zs, tzs, r, tr, 11, cm, sm, ttb, o[:, :n], to, 128, n)
            store(SC["QR"][j, :, t0:t0 + n], o[:, :n], to)
        for j in range(4):
            z, tz = proj(7 + j, n)
            zs, tzs = proj(11 + j, n)
            r, tr = rstd_of([(z, tz)], M64, 128, n)
            o, to = outb.next()
            rope_out(z, tz, zs, tzs, r, tr, 7, cw, sw, ttb, o[:, :n], to, 128, n)
            store(SC["QB"][j, :, t0:t0 + n], o[:, :n], to)
        for sub in range(nsub):
            p, tp = pv.next()
            for k in range(8):
                mm(P, p[:, :128], h[:, k, sub * 128:(sub + 1) * 128], W[:, k, 15 * 128:16 * 128], k == 0, k == 7,
                   [tW, th], [tp])
            v, tv = vout.next()
            P.op("act", lambda e, v=v, p=p: e.copy(v[:, :128], p[:, :128]), [tp], [tv])
            store(SC["VB"][t0 + sub * 128:t0 + (sub + 1) * 128, :], v[:, :128], tv)

    for ti, (t0, n, isctx) in enumerate(tiles):
        tile_body(ti, t0, n, isctx)
    P.emit()
    P.close()


def attn_finalize(P, A, po, tpo, n, dst, tdst, sink=None, pview=False):
    rsb, trsb = A["rsb"].next()
    if sink is not None:
        P.op("dve", lambda e: e.tensor_tensor(rsb[64:65, :n], po[64:65, :n], sink, ALU.add), [tpo, A["tcst"]], [trsb])
        P.op("dve", lambda e: e.reciprocal(rsb[64:65, :n], rsb[64:65, :n]), [trsb], [trsb])
    else:
        P.op("dve", lambda e: e.reciprocal(rsb[64:65, :n], po[64:65, :n]), [tpo], [trsb])
    pbc, tpbc = A["pbc"].next()
    mm(P, pbc[0:64, :n], A["onesf"][64:65, 0:64], rsb[64:65, :n], True, True, [trsb, A["tcst"]], [tpbc])
    bcs, tbcs = A["bcs"].next()
    P.op("act", lambda e: e.copy(bcs[0:64, :n], pbc[0:64, :n]), [tpbc], [tbcs])
    if pview:
        a0 = po[0:64, :n].rearrange("p (g q) -> p g q", g=4)
        a1 = bcs[0:64, :n].rearrange("p (g q) -> p g q", g=4)
    else:
        a0 = po[0:64, :n]
        a1 = bcs[0:64, :n]
    P.op("dve", lambda e: e.tensor_tensor(dst, a0, a1, ALU.mult), [tpo, tbcs], [tdst, tpo])


def attn_common(P, dr):
    A = {}
    A["rsb"] = Rot(P, "rsb", [128, 512], F32, 2)
    A["pbc"] = Rot(P, "pbc", [128, 512], F32, 1, psum=True)
    A["bcs"] = Rot(P, "bcs", [64, 512], F32, 2)
    A["onesf"] = P.sb("onesf", [128, 64], F32)
    A["tcst"] = P.tok("acst")
    P.dma(A["onesf"][:], dr["onesf"], writes=[A["tcst"]], dom=A["tcst"])
    return A


def mla_pass(nc, G, dr, SC, tiles, heads=range(8)):
    P = Pass(nc, "pa")
    A = attn_common(P, dr)
    NKT = T // 128
    Kt = [P.sb("Kt%d" % b, [96, T], BF16) for b in range(2)]
    Qt = [P.sb("Qt%d" % b, [96, T], BF16) for b in range(2)]
    Vh = [P.sb("Vh%d" % b, [128, NKT, 65], BF16) for b in range(2)]
    tKQV = [P.tok("kqv%d" % b) for b in range(2)]
    for b in range(2):
        P.op("pool", lambda e, b=b: e.memset(Vh[b][:, :, 64:65], 1.0), [], [tKQV[b]])
    pS = Rot(P, "pS", [128, 512], F32, 4, psum=True)
    pO = Rot(P, "pO", [128, 512], F32, 2, psum=True)
    Pm = Rot(P, "Pm", [128, 512], BF16, 4)
    ob = Rot(P, "ob", [64, 512], BF16, 2)
    scale = 96.0 ** -0.5
    VAv = SC["VA"].rearrange("(kt p) c -> p kt c", p=128)

    def load_head(hh, b):
        tk = tKQV[b]
        P.dma(Kt[b][0:64, :], SC["KN"][hh // 2, (hh % 2) * 64:(hh % 2) * 64 + 64, :], writes=[tk], dom=tk)
        P.dma(Kt[b][64:96, :], SC["KR"][:, :], writes=[tk], dom=tk)
        P.dma(Qt[b][0:64, :], SC["QN"][hh // 2, (hh % 2) * 64:(hh % 2) * 64 + 64, :], writes=[tk], dom=tk)
        P.dma(Qt[b][64:96, :], SC["QR"][hh // 4, (hh % 4) * 32:(hh % 4) * 32 + 32, :], writes=[tk], dom=tk)
        for c0 in range(0, NKT, 11):
            P.dma(Vh[b][:, c0:c0 + 11, 0:64], VAv[:, c0:c0 + 11, hh * 64:(hh + 1) * 64], writes=[tk], dom=tk)

    heads = list(heads)
    load_head(heads[0], 0)
    for hi, hh in enumerate(heads):
        b = hi % 2
        if hi + 1 < len(heads):
            load_head(heads[hi + 1], 1 - b)
        tk = tKQV[b]
        steps = []
        for (t0, n, isctx) in tiles:
            nkt = 2 if isctx else NKT
            for kt in range(nkt):
                steps.append((t0, n, kt, nkt))
        LAG = 2
        pend = []
        cur = {}

        def issue_S(t0, n, kt, nkt):
            ps, tps = pS.next()
            mm(P, ps[:, :n], Kt[b][:, kt * 128:(kt + 1) * 128], Qt[b][:, t0:t0 + n], True, True, [tk], [tps])
            pm, tpm = Pm.next()
            P.op("act", lambda e: e.activation(pm[:, :n], ps[:, :n], AF.Exp, scale=scale), [tps], [tpm, tps])
            return pm, tpm

        def issue_PV(t0, n, kt, nkt, pm, tpm):
            if kt == 0:
                cur["po"], cur["tpo"] = pO.next()
            po, tpo = cur["po"], cur["tpo"]
            mm(P, po[0:65, :n], Vh[b][:, kt, :], pm[:, :n], kt == 0, kt == nkt - 1, [tk, tpm], [tpo])
            if kt == nkt - 1:
                o, to = ob.next()
                attn_finalize(P, A, po, tpo, n, o[:, :n], to, None)
                P.dma(SC["OT"][hh * 64:(hh + 1) * 64, t0:t0 + n], o[:, :n], reads=[to], dom=to)

        for si, st_ in enumerate(steps):
            pm, tpm = issue_S(*st_)
            pend.append((st_, pm, tpm))
            if len(pend) > LAG:
                s0, pm0, tpm0 = pend.pop(0)
                issue_PV(*s0, pm0, tpm0)
        while pend:
            s0, pm0, tpm0 = pend.pop(0)
            issue_PV(*s0, pm0, tpm0)
    P.emit()
    P.close()


def win_pass(nc, G, dr, SC):
    P = Pass(nc, "pw")
    A = attn_common(P, dr)
    tc = A["tcst"]
    masks = P.sb("masks", [128, 2, 512], BF16)
    P.dma(masks[:], dr["wmask"].rearrange("m p c -> p m c"), writes=[tc], dom=tc)
    sink = P.sb("sink", [128, 1024], F32)
    P.dma(sink[64:65, :], dr["l0_sink"], writes=[tc], dom=tc)
    P.op("act", lambda e: e.activation(sink[64:65, :], sink[64:65, :], AF.Exp), [tc], [tc])
    Kc = P.sb("Kc", [64, 2, LC], BF16)
    Vc = P.sb("Vc", [128, 2, 2, 65], BF16)
    P.op("pool", lambda e: e.memset(Vc[:, :, :, 64:65], 1.0), [], [tc])
    VBv = SC["VB"].rearrange("(kt p) c -> p kt c", p=128)
    for nk in range(2):
        P.dma(Kc[:, nk, :], SC["KB"][nk * 64:(nk + 1) * 64, 0:LC], writes=[tc], dom=tc)
        P.dma(Vc[:, nk, :, 0:64], VBv[:, 0:2, nk * 64:(nk + 1) * 64], writes=[tc], dom=tc)
    Qg = Rot(P, "Qg", [64, 4, 512], BF16, 2)
    Kg = Rot(P, "Kg", [64, 6 * 128], BF16, 2)
    Vg = Rot(P, "Vg", [128, 6, 65], BF16, 2)
    for v, tv in zip(Vg.t, Vg.k):
        P.op("pool", lambda e, v=v: e.memset(v[:, :, 64:65], 1.0), [], [tv])
    pS = Rot(P, "pS", [128, 512], F32, 3, psum=True)
    pO = Rot(P, "pO", [128, 512], F32, 2, psum=True)
    Pm = Rot(P, "Pm", [128, 512], BF16, 3)
    obg = Rot(P, "obg", [64, 4, 512], BF16, 2)
    scale = 64.0 ** -0.5
    groups = [("ctx", 0)] + [("lat", g) for g in range(S // 512)]
    for nk in range(2):
        for kind, gi in groups:
            q, tq = Qg.next()
            if kind == "ctx":
                q0, nq = 0, LC
            else:
                q0, nq = LC + gi * 512, 512
            for g in range(4):
                hq = nk * 4 + g
                P.dma(q[:, g, :nq], SC["QB"][hq // 2, (hq % 2) * 64:(hq % 2) * 64 + 64, q0:q0 + nq], writes=[tq], dom=tq)
            if kind == "lat":
                blo = max(4 * gi - 1, 0)
                bhi = min(4 * gi + 4, S // 128 - 1)
                nb = bhi - blo + 1
                kg, tkg = Kg.next()
                vg, tvg = Vg.next()
                P.dma(kg[:, :nb * 128], SC["KB"][nk * 64:(nk + 1) * 64, LC + blo * 128:LC + (bhi + 1) * 128],
                      writes=[tkg], dom=tkg)
                P.dma(vg[:, :nb, 0:64], VBv[:, 2 + blo:2 + bhi + 1, nk * 64:(nk + 1) * 64], writes=[tvg], dom=tvg)
            og, tog = obg.next()
            for qb in range(nq // 128):
                keys = [("c", 0, None), ("c", 1, None)]
                if kind == "lat":
                    i = 4 * gi + qb
                    if i - 1 >= 0:
                        keys.append(("l", i - 1 - blo, 0))
                    keys.append(("l", i - blo, None))
                    if i + 1 <= S // 128 - 1:
                        keys.append(("l", i + 1 - blo, 1))
                po, tpo = pO.next()
                for ki, (kk, idx, mk) in enumerate(keys):
                    ps, tps = pS.next()
                    if kk == "c":
                        lhsT = Kc[:, nk, idx * 128:(idx + 1) * 128]
                        vv = Vc[:, nk, idx, :]
                        rd = [tc]
                    else:
                        lhsT = kg[:, idx * 128:(idx + 1) * 128]
                        vv = vg[:, idx, :]
                        rd = [tkg, tvg]
                    mm(P, ps[:, :].rearrange("p (g q) -> p g q", g=4), lhsT, q[:, :, qb * 128:(qb + 1) * 128],
                       True, True, rd + [tq], [tps])
                    pm, tpm = Pm.next()
                    P.op("act", lambda e, pm=pm, ps=ps: e.activation(pm[:, :], ps[:, :], AF.Exp, scale=scale),
                         [tps], [tpm, tps])
                    if mk is not None:
                        P.op("pool", lambda e, pm=pm, mk=mk: e.tensor_tensor(pm[:, :], pm[:, :], masks[:, mk, :], ALU.mult),
                             [tpm, tc], [tpm])
                    mm(P, po[0:65, :], vv, pm[:, :], ki == 0, ki == len(keys) - 1, rd + [tpm], [tpo])
                dst = og[:, :, qb * 128:(qb + 1) * 128]
                attn_finalize(P, A, po, tpo, 512, dst, tog, sink=sink[64:65, nk * 512:(nk + 1) * 512], pview=True)
            for g in range(4):
                hq = nk * 4 + g
                P.dma(SC["OT"][512 + hq * 64:512 + (hq + 1) * 64, q0:q0 + nq], og[:, g, :nq], reads=[tog], dom=tog)
    P.emit()
    P.close()


def out_pass(nc, G, name, Xin, Xout, OT, wout_d, l, tiles, ot_off=0):
    P = Pass(nc, name)
    Wo = P.sb("Wo", [128, 8, D], BF16)
    tW = P.tok("W")
    stg = [P.sb("stg%d" % j, [128, 1024], F32) for j in range(2)]
    tst = [P.tok("stg%d" % j) for j in range(2)]
    jobs = [(wout_d[k * 128:(k + 1) * 128, :], Wo[:, k, :], D) for k in range(8)]
    load_cast_weights(P, jobs, stg, tst, tW)
    ot = Rot(P, "ot", [128, 8, 512], BF16, 2)
    xr = Rot(P, "xr", [128, 512], F32, 4)
    py = Rot(P, "py", [128, 512], F32, 2, psum=True)

    def tile_body(t0, n, isctx):
        w = 1 if isctx else 0
        o, to = ot.next()
        for k in range(8):
            P.dma(o[:, k, :n], OT[k * 128:(k + 1) * 128, t0 - ot_off:t0 - ot_off + n], writes=[to], dom=to)
        for dc in range(8):
            x, tx = xr.next()
            P.dma(x[:, :n], Xin[dc * 128:(dc + 1) * 128, t0:t0 + n], writes=[tx], dom=tx)
            p, tp = py.next()
            for k in range(8):
                mm(P, p[:, :n], Wo[:, k, dc * 128:(dc + 1) * 128], o[:, k, :n], k == 0, k == 7, [tW, to], [tp])
            P.op("dve", lambda e, x=x, p=p, dc=dc: e.scalar_tensor_tensor(
                x[:, :n], p[:, :n], G["GT"][:, l, w, 1, dc:dc + 1], x[:, :n], ALU.mult, ALU.add),
                [tp, tx, G["tok"]], [tp, tx])
            P.dma(Xout[dc * 128:(dc + 1) * 128, t0:t0 + n], x[:, :n], reads=[tx], dom=tx)

    for (t0, n, isctx) in tiles:
        tile_body(t0, n, isctx)
    P.emit()
    P.close()


def prep1_pass(nc, G, dr, SC, Xin, tiles):
    P = Pass(nc, "p1")
    NCH = 25
    W = P.sb("W", [128, 8, NCH * 128], BF16)
    tW = P.tok("W")
    stg = [P.sb("stg%d" % j, [128, 1600], F32) for j in range(2)]
    tst = [P.tok("stg%d" % j) for j in range(2)]
    jobs = []
    for k in range(8):
        for hf in range(2):
            jobs.append((dr["l1_win"][k * 128:(k + 1) * 128, hf * 1600:(hf + 1) * 1600], W[:, k, hf * 1600:(hf + 1) * 1600], 1600))
    load_cast_weights(P, jobs, stg, tst, tW)
    tc = P.tok("cst")
    TRI = P.sb("TRI", [128, 2, 128], F32)
    ident = P.sb("ident", [128, 128], BF16)
    WG = P.sb("WG", [64, 512], F32)
    P.dma(TRI[:], dr["tri"].rearrange("m p c -> p m c"), writes=[tc], dom=tc)
    P.dma(ident[:], dr["ident"], writes=[tc], dom=tc)
    P.dma(WG[:], dr["l1_wg"], writes=[tc], dom=tc)
    LFB = P.sb("LFB", [64, 512], F32)
    tLFB = P.tok("LFB")
    P.op("dve", lambda e: e.memset(LFB[:], 1.0), [], [tLFB])
    R = alloc_norm_scratch(P)
    xn = P.sb("xn", [128, 8, 512], F32)
    txn = P.tok("xn")
    h = P.sb("h", [128, 8, 512], BF16)
    th = P.tok("h")
    qT = P.sb("qT", [128, 4, 512], F32)
    kT = P.sb("kT", [128, 4, 512], F32)
    tqT = P.tok("qT")
    tkT = P.tok("kT")
    og = Rot(P, "og", [128, 8, 512], BF16, 1)
    pz = Rot(P, "pz", [128, 512], F32, 2, psum=True)
    pv = Rot(P, "pv", [128, 512], F32, 1, psum=True)
    pG = Rot(P, "pG", [128, 4, 128], F32, 2, psum=True)
    pT = Rot(P, "pT", [128, 512], BF16, 1, psum=True)
    ee = Rot(P, "ee", [128, 512], F32, 2)
    ll = Rot(P, "ll", [128, 512], F32, 2)
    E = Rot(P, "E", [128, 4, 128], F32, 2)
    Ei = Rot(P, "Ei", [128, 4, 128], F32, 2)
    QDs = [Rot(P, "QDs%d" % d, [128, 4, 512], BF16, 1) for d in range(2)]
    KIs = [Rot(P, "KIs%d" % d, [128, 4, 512], BF16, 1) for d in range(2)]
    KEs = Rot(P, "KEs", [128, 512], BF16, 2)
    keT = Rot(P, "keT", [128, 128], BF16, 4)
    Vs = Rot(P, "Vs", [128, 1024], BF16, 2)
    DECs = Rot(P, "DECs", [128, 2, 4, 8], F32, 2)
    qscale = 128.0 ** -0.5

    def load_x(t0, n):
        for k in range(8):
            P.dma(xn[:, k, :n], Xin[k * 128:(k + 1) * 128, t0:t0 + n], writes=[txn], dom=txn)

    def proj(chunk, n):
        z, tz = pz.next()
        for k in range(8):
            mm(P, z[:, :n], W[:, k, chunk * 128:(chunk + 1) * 128], h[:, k, :n], k == 0, k == 7, [tW, th], [tz])
        return z, tz

    load_x(tiles[0][0], tiles[0][1])

    def tile_body(ti, t0, n, isctx):
        w = 1 if isctx else 0
        emit_prenorm(P, R, G, 1, w, 1, xn, txn, h, th, n)
        if ti + 1 < len(tiles):
            load_x(tiles[ti + 1][0], tiles[ti + 1][1])
        nsub = n // 128
        c0 = t0 // 64
        for hd in range(4):
            z, tz = proj(hd, n)
            P.op("act", lambda e, z=z, hd=hd: e.copy(kT[:, hd, :n], z[:, :n]), [tz], [tkT, tz])
            z, tz = proj(4 + hd, n)
            P.op("act", lambda e, z=z, hd=hd: e.mul(qT[:, hd, :n], z[:, :n], qscale), [tz], [tqT, tz])
        z, tz = proj(16, n)
        P.op("act", lambda e, z=z: e.copy(LFB[0:16, :n], z[0:16, :n]), [tz], [tLFB])
        P.op("act", lambda e, z=z: e.copy(LFB[32:48, :n], z[32:48, :n]), [tz], [tLFB, tz])
        o_g, tog = og.next()
        for j in range(8):
            z, tz = proj(8 + j, n)
            P.op("act", lambda e, z=z, j=j: e.activation(o_g[:, j, :n], z[:, :n], AF.Silu), [tz], [tog, tz])
        for j in range(8):
            P.dma(SC["OG"][j * 128:(j + 1) * 128, t0:t0 + n], o_g[:, j, :n], reads=[tog], dom=tog)
        for sub in range(nsub):
            v, tv = Vs.next()
            for hf in range(2):
                p, tp = pv.next()
                for k in range(8):
                    mm(P, p[:, :], h[:, k, sub * 128:(sub + 1) * 128], W[:, k, (17 + 4 * hf) * 128:(21 + 4 * hf) * 128],
                       k == 0, k == 7, [tW, th], [tp])
                P.op("act", lambda e, v=v, p=p, hf=hf: e.copy(v[:, hf * 512:(hf + 1) * 512], p[:, :]), [tp], [tv, tp])
            P.dma(SC["V1"][t0 + sub * 128:t0 + (sub + 1) * 128, :], v[:, :], reads=[tv], dom=tv)
        dec, tdec = DECs.next()

        def dir_body(d):
            qd, tqd = QDs[d].next()
            ki, tki = KIs[d].next()
            r0 = 32 * d
            for sub in range(nsub):
                sl_ = slice(sub * 128, (sub + 1) * 128)
                p, tp = pv.next()
                mm(P, p[:, :], LFB[r0:r0 + 17, sl_], WG[r0:r0 + 17, :], True, True, [tLFB, tc], [tp])
                e1, te1 = ee.next()
                P.op("act", lambda e, e1=e1, p=p: e.activation(e1[:, :], p[:, :], AF.Exp, scale=-1.0), [tp], [te1, tp])
                l1, tl1 = ll.next()
                P.op("act", lambda e, e1=e1, l1=l1: e.activation(l1[:, :], e1[:, :], AF.Ln, bias=1.0), [te1], [tl1])
                g, tg = pG.next()
                for hd in range(4):
                    mm(P, g[:, hd, :], l1[:, hd * 128:(hd + 1) * 128], TRI[:, d, :], True, True, [tl1, tc], [tg])
                Ex, tEx = E.next()
                Eix, tEix = Ei.next()
                P.op("act", lambda e, Ex=Ex, g=g: e.activation(Ex[:], g[:], AF.Exp), [tg], [tEx])
                P.op("act", lambda e, Eix=Eix, g=g: e.activation(Eix[:], g[:], AF.Exp, scale=-1.0), [tg], [tEix, tg])
                ke_s, tke_s = KEs.next()
                pt, tpt = pT.next()
                for hd in range(4):
                    for c in range(2):
                        col = (63 + 64 * c) if d == 0 else (64 * c)
                        P.op("dve", lambda e, hd=hd, c=c, col=col, Ex=Ex, sub=sub: e.tensor_copy(
                            dec[:, d, hd, 2 * sub + c:2 * sub + c + 1], Ex[:, hd, col:col + 1]), [tEx], [tdec])
                    P.op("dve", lambda e, hd=hd, Ex=Ex, sl_=sl_: e.tensor_tensor(
                        qd[:, hd, sl_], qT[:, hd, sl_], Ex[:, hd, :], ALU.mult), [tqT, tEx], [tqd])
                    P.op("pool", lambda e, hd=hd, Eix=Eix, sl_=sl_: e.tensor_tensor(
                        ki[:, hd, sl_], kT[:, hd, sl_], Eix[:, hd, :], ALU.mult), [tkT, tEix], [tki])
                    kt_, tkt_ = keT.next()
                    for c in range(2):
                        cs = slice(sub * 128 + 64 * c, sub * 128 + 64 * (c + 1))
                        P.op("dve", lambda e, hd=hd, c=c, cs=cs, kt_=kt_, Eix=Eix, sub=sub: e.scalar_tensor_tensor(
                            kt_[:, 64 * c:64 * (c + 1)], kT[:, hd, cs], dec[:, d, hd, 2 * sub + c:2 * sub + c + 1],
                            Eix[:, hd, 64 * c:64 * (c + 1)], ALU.mult, ALU.mult), [tkT, tdec, tEix], [tkt_])
                    P.op("pe", lambda e, hd=hd, kt_=kt_, pt=pt: e.transpose(pt[:, hd * 128:(hd + 1) * 128], kt_[:, :], ident[:]),
                         [tkt_, tc], [tpt])
                P.op("act", lambda e, ke_s=ke_s, pt=pt: e.copy(ke_s[:, :], pt[:, :]), [tpt], [tke_s, tpt])
                P.dma(SC["KE"][d, t0 + sub * 128:t0 + (sub + 1) * 128, :], ke_s[:, :], reads=[tke_s], dom=tke_s)
            for hd in range(4):
                P.dma(SC["QD"][d, hd, :, t0:t0 + n], qd[:, hd, :n], reads=[tqd], dom=tqd)
                P.dma(SC["KI"][d, hd, :, t0:t0 + n], ki[:, hd, :n], reads=[tki], dom=tki)
        for d in range(2):
            dir_body(d)
        nch = n // 64
        for d in range(2):
            P.dma(SC["DEC"][d, :, :, c0:c0 + nch], dec[:, d, :, :nch], reads=[tdec], dom=tdec)

    for ti, (t0, n, isctx) in enumerate(tiles):
        tile_body(ti, t0, n, isctx)
    P.emit()
    P.close()


def scan_pass(nc, G, dr, SC, d):
    P = Pass(nc, "s%d" % d)
    tc = P.tok("cst")
    MASK = P.sb("MASK", [128, 4, 128], F32)
    P.dma(MASK[:], dr["gmask"][d], writes=[tc], dom=tc)
    DEC = P.sb("DEC", [128, 4, T // 64], F32)
    P.dma(DEC[:], SC["DEC"][d], writes=[tc], dom=tc)
    M256 = P.sb("M256", [128, 128], BF16)
    P.dma(M256[:], dr["Ms"][3], writes=[tc], dom=tc)
    gn = P.sb("gn", [128, 2], F32)
    P.dma(gn[:], dr["l1_gn"], writes=[tc], dom=tc)
    S32 = P.sb("S32", [128, 4, 256], F32)
    S16 = P.sb("S16", [128, 4, 256], BF16)
    tS32 = P.tok("S32")
    tS16 = P.tok("S16")
    P.op("dve", lambda e: e.memset(S32[:], 0.0), [], [tS32])
    P.op("pool", lambda e: e.memset(S16[:], 0.0), [], [tS16])
    QD = Rot(P, "QD", [128, 4, 512], BF16, 2)
    KI = Rot(P, "KI", [128, 4, 512], BF16, 2)
    KE = Rot(P, "KE", [128, 4, 512], BF16, 2)
    V = Rot(P, "V", [128, 4, 1024], BF16, 2)
    OB = Rot(P, "OB", [128, 8, 512], F32, 2)
    OGt = Rot(P, "OGt", [128, 8, 512], BF16, 2)
    O1 = Rot(P, "O1", [128, 8, 512], BF16, 2)
    Am = Rot(P, "Am", [128, 4, 128], BF16, 2)
    osum = Rot(P, "osum", [128, 8, 128], F32, 2)
    sqs = Rot(P, "sqs", [128, 8, 128], BF16, 2)
    rs = Rot(P, "rs", [128, 4, 128], F32, 2)
    rt = Rot(P, "rt", [128, 4, 128], F32, 2)
    o1f = Rot(P, "o1f", [128, 8, 128], F32, 2)
    pA = Rot(P, "pA", [128, 4, 128], F32, 2, psum=True)
    pO = Rot(P, "pO", [128, 8, 128], F32, 1, psum=True)
    pKV = Rot(P, "pKV", [128, 4, 256], F32, 1, psum=True)
    pss = Rot(P, "pss", [128, 4, 128], F32, 1, psum=True)
    KEv = SC["KE"][d].rearrange("(s p) c -> p s c", p=128)
    V1v = SC["V1"].rearrange("(s p) c -> p s c", p=128)
    blocks = [(0, LC, True)] + [(LC + i * 512, 512, False) for i in range(S // 512)]
    if d == 1:
        blocks = [blocks[0]] + blocks[1:][::-1]

    def state_update(ke, tke, v, tv, sub, c, chunk_id):
        p, tp = pKV.next()
        rows = slice(64 * c, 64 * (c + 1))
        for hd in range(4):
            mm(P, p[:, hd, :], ke[rows, sub, hd * 128:(hd + 1) * 128], v[rows, sub, hd * 256:(hd + 1) * 256],
               True, True, [tke, tv], [tp])
        for hd in range(4):
            P.op("dve", lambda e, hd=hd, p=p: e.scalar_tensor_tensor(
                S32[:, hd, :], S32[:, hd, :], DEC[:, hd, chunk_id:chunk_id + 1], p[:, hd, :], ALU.mult, ALU.add),
                [tp, tS32, tc], [tS32, tp])
        P.op("act", lambda e: e.copy(S16[:], S32[:]), [tS32], [tS16])

    for (t0, nb, isctx) in blocks:
        nsub = nb // 128
        ke, tke = KE.next()
        v, tv = V.next()
        s0 = t0 // 128
        P.dma(ke[:, :nsub, :], KEv[:, s0:s0 + nsub, :], writes=[tke], dom=tke)
        for sub in range(nsub):
            P.dma(v[:, sub, :], V1v[:, s0 + sub, :], writes=[tv], dom=tv)
        subs = list(range(nsub))
        corder = [0, 1]
        if d == 1:
            subs = subs[::-1]
            corder = [1, 0]
        if isctx:
            for sub in subs:
                for c in corder:
                    state_update(ke, tke, v, tv, sub, c, (t0 + sub * 128) // 64 + c)
            continue
        qd, tqd = QD.next()
        ki, tki = KI.next()
        for hd in range(4):
            P.dma(qd[:, hd, :], SC["QD"][d, hd, :, t0:t0 + nb], writes=[tqd], dom=tqd)
            P.dma(ki[:, hd, :], SC["KI"][d, hd, :, t0:t0 + nb], writes=[tki], dom=tki)
        if d == 1:
            ob, tob = OB.next()
        else:
            ob, tob = OB.next()
            ogt, togt = OGt.next()
            o1, to1 = O1.next()
            for j in range(8):
                P.dma(ob[:, j, :], SC["OBF"][j * 128:(j + 1) * 128, t0 - LC:t0 - LC + nb], writes=[tob], dom=tob)
                P.dma(ogt[:, j, :], SC["OG"][j * 128:(j + 1) * 128, t0:t0 + nb], writes=[togt], dom=togt)
        for sub in subs:
            sl_ = slice(sub * 128, (sub + 1) * 128)
            a, ta = pA.next()
            for hd in range(4):
                mm(P, a[:, hd, :], ki[:, hd, sl_], qd[:, hd, sl_], True, True, [tki, tqd], [ta])
            am, tam = Am.next()
            P.op("dve", lambda e, am=am, a=a: e.tensor_tensor(am[:], a[:], MASK[:], ALU.mult), [ta, tc], [tam, ta])
            po, tpo = pO.next()
            for hd in range(4):
                for dvc in range(2):
                    mm(P, po[:, hd * 2 + dvc, :], v[:, sub, hd * 256 + dvc * 128:hd * 256 + (dvc + 1) * 128], am[:, hd, :],
                       (hd * 2 + dvc) % 4 == 0, False, [tv, tam], [tpo], sgc=True)
            for ci, c in enumerate(corder):
                cs = slice(sub * 128 + 64 * c, sub * 128 + 64 * (c + 1))
                for hd in range(4):
                    for dvc in range(2):
                        mm(P, po[:, hd * 2 + dvc, 64 * c:64 * (c + 1)], S16[:, hd, dvc * 128:(dvc + 1) * 128], qd[:, hd, cs],
                           False, True, [tS16, tqd], [tpo], sgc=True)
                state_update(ke, tke, v, tv, sub, c, (t0 + sub * 128) // 64 + c)
            if d == 1:
                for hb in range(2):
                    P.op("act", lambda e, po=po, ob=ob, sl_=sl_, hb=hb: e.copy(ob[:, 4 * hb:4 * hb + 4, sl_], po[:, 4 * hb:4 * hb + 4, :]),
                         [tpo], [tob, tpo])
            else:
                os_, tos = osum.next()
                for hb in range(2):
                    P.op("dve", lambda e, os_=os_, po=po, ob=ob, sl_=sl_, hb=hb: e.tensor_tensor(
                        os_[:, 4 * hb:4 * hb + 4, :], po[:, 4 * hb:4 * hb + 4, :], ob[:, 4 * hb:4 * hb + 4, sl_], ALU.add),
                        [tpo, tob], [tos, tpo])
                sq, tsq = sqs.next()
                P.op("act", lambda e, sq=sq, os_=os_: e.activation(sq[:], os_[:], AF.Square), [tos], [tsq])
                ps_, tps = pss.next()
                for hd in range(4):
                    for dvc in range(2):
                        mm(P, ps_[:, hd, :], M256[:], sq[:, hd * 2 + dvc, :], dvc == 0, dvc == 1, [tsq, tc], [tps])
                r, tr = rs.next()
                tm, ttm = rt.next()
                emit_rsqrt(P, r[:], ps_[:], tm[:], tps, tr, ttm)
                of, tof = o1f.next()
                for hd in range(4):
                    for dvc in range(2):
                        P.op("dve", lambda e, hd=hd, dvc=dvc, of=of, os_=os_, r=r: e.scalar_tensor_tensor(
                            of[:, hd * 2 + dvc, :], os_[:, hd * 2 + dvc, :], gn[:, dvc:dvc + 1], r[:, hd, :],
                            ALU.mult, ALU.mult), [tos, tr, tc], [tof])
                P.op("pool", lambda e, of=of, o1=o1, ogt=ogt, sl_=sl_: e.tensor_tensor(o1[:, :, sl_], of[:], ogt[:, :, sl_], ALU.mult),
                     [tof, togt], [to1])
        if d == 1:
            for j in range(8):
                P.dma(SC["OBF"][j * 128:(j + 1) * 128, t0 - LC:t0 - LC + nb], ob[:, j, :], reads=[tob], dom=tob)
        else:
            for j in range(8):
                P.dma(SC["OT"][j * 128:(j + 1) * 128, t0:t0 + nb], o1[:, j, :], reads=[to1], dom=to1)
    P.emit()
    P.close()


def build_program(upto=99, dumps=()):
    nc = bass.Bass("TRN2", target_bir_lowering=False)
    dr = {}

    def din(name, shape, dt=F32):
        dr[name] = nc.dram_tensor(name, list(shape), dt, kind="ExternalInput").ap()

    din("xT", [D, T])
    din("c8", [128, 8])
    din("cc8", [128, 8])
    for l in range(2):
        din("l%d_wmod" % l, [D, 9 * D])
        din("l%d_bmod" % l, [128, 72])
        din("l%d_ng" % l, [128, 24])
        for f in (1, 2):
            din("l%d_ffn%d_wgu" % (l, f), [D, 2 * DFF])
            din("l%d_ffn%d_wd" % (l, f), [DFF, D])
    din("ones1024", [128, 128], BF16)
    din("l0_win", [D, 2048])
    din("l0_wuq", [256, 1024])
    din("l0_wukv", [128, 1024])
    din("l0_gv", [128, 16])
    din("l0_wout", [D, D])
    din("l0_sink", [1, 1024])
    din("Ms", [4, 128, 128], BF16)
    din("wmask", [2, 128, 512], BF16)
    din("onesf", [128, 64])
    for nm in ("cosm", "sinm", "cosw", "sinw"):
        din(nm, [128, T])
    din("l1_win", [D, 3200])
    din("l1_wg", [64, 512])
    din("l1_gn", [128, 2])
    din("l1_wout", [D, D])
    din("tri", [2, 128, 128])
    din("gmask", [2, 128, 4, 128])
    din("ident", [128, 128], BF16)
    SC = {}
    for nm, shp in (("QN", [4, 128, T]), ("QR", [2, 128, T]), ("KN", [4, 128, T]), ("KR", [32, T]), ("KB", [128, T]),
                    ("QB", [4, 128, T]), ("VA", [T, 512]), ("VB", [T, 128]), ("OT", [D, T])):
        SC[nm] = nc.dram_tensor("sc_" + nm, shp, BF16, kind="Internal").ap()
    for nm, shp, dt in (("QD", [2, 4, 128, T], BF16), ("KI", [2, 4, 128, T], BF16), ("KE", [2, T, 512], BF16),
                        ("V1", [T, 1024], BF16), ("OG", [D, T], BF16), ("DEC", [2, 128, 4, T // 64], F32),
                        ("OBF", [D, S], F32)):
        SC[nm] = nc.dram_tensor("sc_" + nm, shp, dt, kind="Internal").ap()
    out = nc.dram_tensor("outT", [D, S], F32, kind="ExternalOutput").ap()
    XA = nc.dram_tensor("XA", [D, T], F32, kind="Internal").ap()
    XB = nc.dram_tensor("XB", [D, T], F32, kind="Internal").ap()
    dump_aps = {}
    sc_dumps = {}
    for nm, shape in dumps:
        if nm in SC:
            sc_dumps[nm] = nc.dram_tensor("dump_" + nm, list(SC[nm].shape), SC[nm].dtype, kind="ExternalOutput").ap()
        else:
            dump_aps[nm] = nc.dram_tensor("dump_" + nm, list(shape), F32, kind="ExternalOutput").ap()

    with ExitStack() as ges:
        SEMPOOL[0] = SemPool(nc, ges, 96)
        G = {}
        G["MOD"] = ges.enter_context(nc.sbuf_tensor("gMOD", [128, 2, 2, 72], F32))
        G["A"] = ges.enter_context(nc.sbuf_tensor("gA", [128, 2, 2, 3, 8], F32))
        G["B"] = ges.enter_context(nc.sbuf_tensor("gB", [128, 2, 2, 3, 8], F32))
        G["GT"] = ges.enter_context(nc.sbuf_tensor("gGT", [128, 2, 2, 3, 8], F32))
        G["ones1024"] = ges.enter_context(nc.sbuf_tensor("gones", [128, 128], BF16))
        G["tok"] = Tok("G")
        G["tokc"] = Tok("Gc")
        P = Pass(nc, "pc")
        P.dma(G["ones1024"][:], dr["ones1024"], writes=[G["tokc"]], dom=G["tokc"])
        P.emit()
        P.close()
        mod_pass(nc, G, dr)
        tiles = token_tiles()
        final_src = XA
        if upto >= 1:
            ffn_pass(nc, G, "f1", dr["xT"], XA, dr["l0_ffn1_wgu"], dr["l0_ffn1_wd"], 0, 0, tiles)
        if upto >= 2:
            prep0_pass(nc, G, dr, SC, XA, tiles)
        if upto >= 3:
            mla_pass(nc, G, dr, SC, tiles)
            win_pass(nc, G, dr, SC)
        if upto >= 4:
            out_pass(nc, G, "o0", XA, XB, SC["OT"], dr["l0_wout"], 0, tiles)
            final_src = XB
        if upto >= 5:
            ffn_pass(nc, G, "f2", XB, XA, dr["l0_ffn2_wgu"], dr["l0_ffn2_wd"], 0, 2, tiles)
            final_src = XA
        lat_tiles = [t for t in tiles if not t[2]]
        if upto >= 6:
            ffn_pass(nc, G, "f3", XA, XB, dr["l1_ffn1_wgu"], dr["l1_ffn1_wd"], 1, 0, tiles)
            final_src = XB
        if upto >= 7:
            prep1_pass(nc, G, dr, SC, XB, tiles)
        if upto >= 8:
            scan_pass(nc, G, dr, SC, 1)
            scan_pass(nc, G, dr, SC, 0)
        if upto >= 9:
            out_pass(nc, G, "o1", XB, XA, SC["OT"], dr["l1_wout"], 1, lat_tiles)
            final_src = XA
        if upto >= 10:
            ffn_pass(nc, G, "f4", XA, out, dr["l1_ffn2_wgu"], dr["l1_ffn2_wd"], 1, 2, lat_tiles, xout_off=LC)
            final_src = None
        P = Pass(nc, "pd")
        tdd = P.tok("dd")
        for nm, ap in sc_dumps.items():
            src = SC[nm]
            if len(src.shape) == 2:
                for r0 in range(0, src.shape[0], 128):
                    P.dma(ap[r0:r0 + 128, :], src[r0:r0 + 128, :], dom=tdd)
            elif len(src.shape) == 3:
                for a in range(src.shape[0]):
                    for r0 in range(0, src.shape[1], 128):
                        P.dma(ap[a, r0:r0 + 128, :], src[a, r0:r0 + 128, :], dom=tdd)
            else:
                for a in range(src.shape[0]):
                    for b2 in range(src.shape[1]):
                        P.dma(ap[a, b2], src[a, b2], dom=tdd)
        cp = [P.sb("cp%d" % j, [128, 2048], F32) for j in range(2)]
        tcp = [P.tok("cp%d" % j) for j in range(2)]
        srcs = []
        if "MOD" in dump_aps:
            P.dma(dump_aps["MOD"], G["MOD"][:].rearrange("p a b c -> p (a b c)"),
                  reads=[G["tok"]], dom=tcp[0])
        jj = 0
        for nm, ap in list(dump_aps.items()) + [("__out", out)]:
            if nm == "MOD":
                continue
            src = {"XA": XA, "XB": XB}.get(nm, None)
            if src is None and nm != "__out":
                continue
            col0 = 0
            if nm == "__out":
                if final_src is None:
                    continue
                src = final_src
                col0 = LC
            ncol = ap.shape[1]
            for k in range(8):
                for c0 in range(0, ncol, 2048):
                    b = jj % 2
                    jj += 1
                    wd = min(2048, ncol - c0)
                    P.dma(cp[b][:, :wd], src[k * 128:(k + 1) * 128, col0 + c0:col0 + c0 + wd], writes=[tcp[b]], dom=tcp[b])
                    P.dma(ap[k * 128:(k + 1) * 128, c0:c0 + wd], cp[b][:, :wd], reads=[tcp[b]], dom=tcp[b])
        P.emit()
        P.close()
    return nc


def host_inputs(inp, b):
    f = np.float32
    m = {}
    xT = np.concatenate([inp["ctx"][b], inp["x"][b]], axis=0).T
    m["xT"] = np.ascontiguousarray(xT, dtype=f)
    m["c8"] = np.ascontiguousarray(inp["c"][b].reshape(8, 128).T, dtype=f)
    m["cc8"] = np.ascontiguousarray(inp["c_ctx"].reshape(8, 128).T, dtype=f)
    for l in range(2):
        p = "l%d_" % l
        m[p + "wmod"] = np.ascontiguousarray(inp[p + "w_mod"], dtype=f)
        m[p + "bmod"] = np.ascontiguousarray(inp[p + "b_mod"].reshape(72, 128).T, dtype=f)
        m[p + "ng"] = np.ascontiguousarray(inp[p + "norm_g"].reshape(24, 128).T, dtype=f)
        for k in (1, 2):
            m[p + "ffn%d_wgu" % k] = np.ascontiguousarray(inp[p + "ffn%d_w_gu" % k], dtype=f)
            m[p + "ffn%d_wd" % k] = np.ascontiguousarray(inp[p + "ffn%d_w_down" % k], dtype=f)
    m["ones1024"] = np.full((128, 128), 1.0 / 1024, dtype=ml_dtypes.bfloat16)
    m.update(host_consts())
    m.update(host_l0(inp))
    m.update(host_l1(inp))
    return m


def host_l1(inp):
    f = np.float32
    m = {}
    w = inp["l1_w_in"].astype(f)
    k, v, lf, lb, q, g = w[:, 0:512], w[:, 512:1536], w[:, 1536:1552], w[:, 1552:1568], w[:, 1568:2080], w[:, 2080:3104]
    z16 = np.zeros((D, 16), f)
    z80 = np.zeros((D, 80), f)
    m["l1_win"] = np.ascontiguousarray(np.concatenate([k, q, g, lf, z16, lb, z80, v], axis=1))
    assert m["l1_win"].shape == (D, 3200)
    wg = np.zeros((64, 512), f)
    wg[0:16] = inp["l1_w_gk_f"]
    wg[16] = inp["l1_b_gk_f"]
    wg[32:48] = inp["l1_w_gk_b"]
    wg[48] = inp["l1_b_gk_b"]
    m["l1_wg"] = wg
    m["l1_gn"] = np.ascontiguousarray(inp["l1_g_norm"].astype(f).reshape(2, 128).T)
    m["l1_wout"] = np.ascontiguousarray(inp["l1_w_out"], dtype=f)
    return m


def _swap(a):
    return a[:, np.arange(a.shape[1]) ^ 1]


def rope_tables(rot_dim, rep):
    n_freq = rot_dim // 4
    inv = (np.float32(10000.0) ** (-np.arange(n_freq, dtype=np.float32) / np.float32(n_freq))).astype(np.float32)
    t = np.arange(S)
    row = (t // 64).astype(np.float32)
    col = (t % 64).astype(np.float32)
    ang = np.concatenate([row[:, None] * inv, col[:, None] * inv], axis=-1).astype(np.float32)
    cos = np.cos(ang).astype(np.float32)
    sin = np.sin(ang).astype(np.float32)
    d = np.arange(rot_dim)
    C = np.ones((rot_dim, T), np.float32)
    Sg = np.zeros((rot_dim, T), np.float32)
    C[:, LC:] = cos[:, d // 2].T
    sign = np.where(d % 2 == 0, -1.0, 1.0).astype(np.float32)
    Sg[:, LC:] = sin[:, d // 2].T * sign[:, None]
    return np.ascontiguousarray(np.tile(C, (rep, 1))), np.ascontiguousarray(np.tile(Sg, (rep, 1)))


_CONSTS = {}


def host_consts():
    if _CONSTS:
        return _CONSTS
    bf = ml_dtypes.bfloat16
    m = {}
    Ms = np.zeros((4, 128, 128), np.float32)
    Ms[0] = 1.0 / 128
    for b in range(2):
        Ms[1, b * 64:(b + 1) * 64, b * 64:(b + 1) * 64] = 1.0 / 64
    for b in range(4):
        Ms[2, b * 32:(b + 1) * 32, b * 32:(b + 1) * 32] = 1.0 / 32
    Ms[3] = 1.0 / 256
    m["Ms"] = Ms.astype(bf)
    kk = np.arange(128)[:, None]
    qq = np.arange(128)[None, :]
    wm = np.stack([np.tile((kk >= qq).astype(np.float32), (1, 4)), np.tile((kk <= qq).astype(np.float32), (1, 4))])
    m["wmask"] = wm.astype(bf)
    m["onesf"] = np.ones((128, 64), np.float32)
    sidx = np.arange(128)[:, None]
    tidx = np.arange(128)[None, :]
    same = (sidx // 64) == (tidx // 64)
    lo = (same & (sidx <= tidx)).astype(np.float32)
    hi = (same & (sidx >= tidx)).astype(np.float32)
    m["tri"] = np.stack([lo, hi]) * np.float32(-1.0 / 16.0)
    m["gmask"] = np.ascontiguousarray(np.stack([np.repeat(lo[:, None, :], 4, axis=1), np.repeat(hi[:, None, :], 4, axis=1)]))
    m["ident"] = np.eye(128, dtype=np.float32).astype(bf)
    m["cosm"], m["sinm"] = rope_tables(32, 4)
    m["cosw"], m["sinw"] = rope_tables(64, 2)
    _CONSTS.update(m)
    return _CONSTS


def host_l0(inp):
    f = np.float32
    m = {}
    w = inp["l0_w_in"].astype(f)
    ckv, kr, wk, wv, cq, wq = w[:, 0:128], w[:, 128:160], w[:, 160:288], w[:, 288:416], w[:, 416:672], w[:, 672:1184]
    pad = np.zeros((D, 96), f)
    m["l0_win"] = np.ascontiguousarray(np.concatenate(
        [ckv, kr, pad, _swap(kr), pad, wk, _swap(wk), cq, wq, _swap(wq), wv], axis=1))
    assert m["l0_win"].shape == (D, 2048)
    uq = inp["l0_mla_w_uq"].astype(f).reshape(256, 8, 96)
    nope = uq[:, :, :64].reshape(256, 512)
    rope = uq[:, :, 64:].reshape(256, 256)
    m["l0_wuq"] = np.ascontiguousarray(np.concatenate([nope, rope, _swap(rope)], axis=1))
    ukv = inp["l0_mla_w_ukv"].astype(f).reshape(128, 8, 128)
    m["l0_wukv"] = np.ascontiguousarray(np.concatenate([ukv[:, :, :64].reshape(128, 512), ukv[:, :, 64:].reshape(128, 512)], axis=1))
    gv = np.ones((128, 16), f)
    sw = lambda g: g[np.arange(g.shape[0]) ^ 1]
    gv[:, 0] = inp["l0_mla_g_kva"]
    gv[:32, 1] = inp["l0_mla_g_kr"]
    gv[:32, 2] = sw(inp["l0_mla_g_kr"])
    gv[:, 3] = np.tile(inp["l0_win_g_k"], 2)
    gv[:, 4] = np.tile(sw(inp["l0_win_g_k"]), 2)
    gv[:, 5] = inp["l0_mla_g_qa"][:128]
    gv[:, 6] = inp["l0_mla_g_qa"][128:]
    gv[:, 7] = np.tile(inp["l0_win_g_q"], 2)
    gv[:, 8] = np.tile(sw(inp["l0_win_g_q"]), 2)
    gv[:, 9] = np.tile(inp["l0_mla_g_kn"], 2)
    gv[:, 10] = np.tile(inp["l0_mla_g_qn"], 2)
    gv[:, 11] = np.tile(inp["l0_mla_g_qr"], 4)
    gv[:, 12] = np.tile(sw(inp["l0_mla_g_qr"]), 4)
    m["l0_gv"] = gv
    m["l0_wout"] = np.ascontiguousarray(inp["l0_w_out"], dtype=f)
    m["l0_sink"] = np.ascontiguousarray(np.repeat(inp["l0_win_sink"].astype(f).reshape(8), 128).reshape(1, 1024))
    return m


def kernel(**inp):
    inp = {k: np.asarray(v) for k, v in inp.items()}
    nc = build_program()
    in_maps = [host_inputs(inp, b) for b in range(NCORES)]
    res = run_bass_kernel_spmd(nc, in_maps, core_ids=list(range(NCORES)))
    outs = [np.asarray(res.results[b]["outT"]).T for b in range(NCORES)]
    return np.ascontiguousarray(np.stack(outs, axis=0).astype(np.float32))
```

```python
import numpy as np
from contextlib import ExitStack
import ml_dtypes
import concourse.bass as bass
import concourse.mybir as mybir
from concourse.bass_utils import run_bass_kernel_spmd

F32 = mybir.dt.float32
BF16 = mybir.dt.bfloat16
AF = mybir.ActivationFunctionType
ALU = mybir.AluOpType

D = 1024
DFF = 2816
NF = DFF // 128
LC = 256
S = 8192
T = LC + S
EPS = 1e-6
NCORES = 8

ENGS = ("pe", "act", "dve", "pool", "sp")


class Tok:
    __slots__ = ("name", "lw", "rd", "dom")

    def __init__(self, name):
        self.name = name
        self.lw = None
        self.rd = {}
        self.dom = None


class Op:
    __slots__ = ("eng", "fn", "deps", "inc", "dom", "val", "sem", "pas", "isdma")


class SemPool:
    def __init__(self, nc, es, n):
        self.items = [[es.enter_context(nc.semaphore("sp%d" % i)), 0, False] for i in range(n)]

    def acquire(self):
        free = [it for it in self.items if not it[2]]
        it = min(free, key=lambda x: x[1])
        it[2] = True
        return it


SEMPOOL = [None]


class Pass:
    def __init__(self, nc, name):
        self.nc = nc
        self.sems = []
        self.name = name
        self.es = ExitStack()
        self.streams = {e: [] for e in ENGS}
        self.doms = {}
        self.last = {}
        self.nalloc = 0

    def sb(self, name, shape, dt):
        return self.es.enter_context(self.nc.sbuf_tensor(self.name + "_" + name, list(shape), dt))

    def ps(self, name, shape, dt=F32):
        return self.es.enter_context(self.nc.psum_tensor(self.name + "_" + name, list(shape), dt))

    def tok(self, name):
        return Tok(name)

    def _record(self, o, reads, writes):
        deps = set()
        for t in reads:
            if t.lw is not None:
                deps.add(t.lw)
        for t in writes:
            if t.lw is not None:
                deps.add(t.lw)
            for r in t.rd.values():
                deps.add(r)
        deps.discard(o)
        deps = [d for d in deps if d.pas is self]
        for d in deps:
            d.inc = True
        o.deps = deps
        for t in reads:
            if t not in writes:
                t.rd[o.dom] = o
        for t in writes:
            t.lw = o
            t.rd = {}
        self.streams[o.eng].append(o)
        self.last[o.dom] = o

    def op(self, eng, fn, reads=(), writes=()):
        o = Op()
        o.eng = eng
        o.fn = fn
        o.inc = False
        o.dom = eng
        o.pas = self
        o.isdma = False
        o.val = None
        o.sem = None
        self._record(o, list(reads), list(writes))
        return o

    def dma(self, out, in_, reads=(), writes=(), dom=None, q="sp"):
        assert dom is not None
        if dom.dom is None or dom.dom[0] is not self:
            it = SEMPOOL[0].acquire()
            self.sems.append(it)
            sem = it
            dom.dom = (self, sem, [0], "dma%d" % len(self.doms))
            self.doms[dom.dom[3]] = dom.dom
        o = Op()
        o.eng = q
        o.fn = lambda e, out=out, in_=in_: e.dma_start(out=out, in_=in_)
        o.inc = True
        o.dom = dom.dom[3]
        o.pas = self
        o.isdma = True
        dom.dom[2][0] += 16
        o.val = dom.dom[2][0]
        o.sem = dom.dom[1]
        self._record(o, list(reads), list(writes))
        return o

    def emit(self):
        nc = self.nc
        engsem = {}
        for e in ("pe", "act", "dve", "pool"):
            engsem[e] = SEMPOOL[0].acquire()
            self.sems.append(engsem[e])
        for e in ("pe", "act", "dve", "pool", "sp"):
            cnt = 0
            for o in self.streams[e]:
                if o.isdma:
                    continue
                if o.fn is None:
                    continue
                if o.inc:
                    cnt += 1
                    o.val = cnt
                    o.sem = engsem[e]
        finals = []
        for dom, o in self.last.items():
            if o.fn is None:
                continue
            if not o.isdma and not o.inc:
                o.inc = True
            finals.append(o)
        for e in ("pe", "act", "dve", "pool", "sp"):
            cnt = 0
            for o in self.streams[e]:
                if o.isdma or o.fn is None:
                    continue
                if o.inc:
                    cnt += 1
                    o.val = cnt
                    o.sem = engsem[e]
        handles = {"pe": nc.tensor, "act": nc.scalar, "dve": nc.vector, "pool": nc.gpsimd, "sp": nc.sync}
        streams = self.streams

        def run(ename, eh):
            seen = {}
            for o in streams[ename]:
                for d in o.deps:
                    if (not d.isdma) and d.eng == "pe" and ename == "pe" and not o.isdma:
                        continue
                    k = id(d.sem)
                    if seen.get(k, 0) < d.val:
                        eh.wait_ge(d.sem[0], d.sem[1] + d.val)
                        seen[k] = d.val
                if o.fn is None:
                    continue
                ins = o.fn(eh)
                if o.isdma:
                    ins.then_inc(o.sem[0], 16)
                elif o.inc:
                    ins.then_inc(o.sem[0], 1)
            for d in finals:
                k = id(d.sem)
                if seen.get(k, 0) < d.val:
                    eh.wait_ge(d.sem[0], d.sem[1] + d.val)
                    seen[k] = d.val

        with nc.named_scope(self.name), nc.Block() as blk:
            @blk.tensor
            def _(e):
                run("pe", e)

            @blk.scalar
            def _(e):
                run("act", e)

            @blk.vector
            def _(e):
                run("dve", e)

            @blk.gpsimd
            def _(e):
                run("pool", e)

            @blk.sync
            def _(e):
                run("sp", e)
        tot = {}
        for e in ENGS:
            for o in streams[e]:
                if o.fn is None:
                    continue
                if o.isdma:
                    tot[id(o.sem)] = tot.get(id(o.sem), 0) + 16
                elif o.inc:
                    tot[id(o.sem)] = tot.get(id(o.sem), 0) + 1
        for it in self.sems:
            it[1] += tot.get(id(it), 0)
            it[2] = False
            assert it[1] < 60000, it[1]

    def close(self):
        self.es.close()


def mm(P, out, lhsT, rhs, start, stop, reads, writes, sgc=False):
    if sgc:
        return P.op("pe", lambda e: e.matmul(out, lhsT, rhs, start=start, stop=stop, skip_group_check=True), reads, writes)
    return P.op("pe", lambda e: e.matmul(out, lhsT, rhs, start=start, stop=stop), reads, writes)


def token_tiles():
    tiles = [(0, LC, True)]
    for i in range(S // 512):
        tiles.append((LC + i * 512, 512, False))
    return tiles


def mod_pass(nc, G, dr):
    P = Pass(nc, "pm")
    c2 = P.sb("c2", [128, 2, 8], F32)
    r2 = P.sb("r2", [128, 2, 8], F32)
    tc2 = P.tok("c2")
    tr2 = P.tok("r2")
    P.dma(c2[:, 0, :], dr["c8"], writes=[tc2], dom=tc2)
    P.dma(c2[:, 1, :], dr["cc8"], writes=[tc2], dom=tc2)
    P.op("act", lambda e: e.activation(r2[:], c2[:], AF.Silu), [tc2], [tr2])
    GW = 1152
    NG = 9216 // GW
    wbuf = [P.sb("w%d" % i, [128, 8, GW], F32) for i in range(2)]
    twb = [P.tok("w%d" % i) for i in range(2)]
    bm = P.sb("bm", [128, 2, 72], F32)
    ng = P.sb("ng", [128, 2, 24], F32)
    tbm = P.tok("bm")
    for l in range(2):
        P.dma(bm[:, l, :], dr["l%d_bmod" % l], writes=[tbm], dom=tbm)
        P.dma(ng[:, l, :], dr["l%d_ng" % l], writes=[tbm], dom=tbm)
    pm = P.ps("pm", [128, 2, 72, 2], F32)
    tpm = P.tok("pm")
    it = 0
    for l in range(2):
        wm = dr["l%d_wmod" % l]
        for g in range(NG):
            b = it % 2
            it += 1
            for k in range(8):
                P.dma(wbuf[b][:, k, :], wm[k * 128:(k + 1) * 128, g * GW:(g + 1) * GW], writes=[twb[b]], dom=twb[b])
            for cb in range(GW // 128):
                col = g * (GW // 128) + cb
                for k in range(8):
                    mm(P, pm[:, l, col, :], wbuf[b][:, k, cb * 128:(cb + 1) * 128], r2[:, :, k],
                       k == 0, k == 7, [twb[b], tr2], [tpm])
    tG = G["tok"]
    for l in range(2):
        for w in range(2):
            P.op("dve", lambda e, l=l, w=w: e.tensor_tensor(G["MOD"][:, l, w, :], pm[:, l, :, w], bm[:, l, :], ALU.add),
                 [tpm, tbm], [tG])
    for l in range(2):
        for w in range(2):
            for i in range(3):
                sh = G["MOD"][:, l, w, (3 * i) * 8:(3 * i) * 8 + 8]
                sc = G["MOD"][:, l, w, (3 * i + 1) * 8:(3 * i + 1) * 8 + 8]
                gt = G["MOD"][:, l, w, (3 * i + 2) * 8:(3 * i + 2) * 8 + 8]
                P.op("dve", lambda e, l=l, w=w, i=i, sc=sc: e.scalar_tensor_tensor(
                    G["A"][:, l, w, i, :], sc, 1.0, ng[:, l, i * 8:(i + 1) * 8], ALU.add, ALU.mult), [tG, tbm], [tG])
                P.op("dve", lambda e, l=l, w=w, i=i, sh=sh: e.tensor_copy(G["B"][:, l, w, i, :], sh), [tG], [tG])
                fac = 1.0 if i == 1 else 0.5
                P.op("dve", lambda e, l=l, w=w, i=i, gt=gt, fac=fac: e.tensor_scalar(
                    G["GT"][:, l, w, i, :], gt, fac, None, ALU.mult), [tG], [tG])
    P.emit()
    P.close()


def emit_prenorm(P, R, G, l, w, i, x, tx, h, th, n, tag=""):
    sq, tsq = R["sq"], R["tsq"]
    for k in range(8):
        b = k % 2
        P.op("act", lambda e, k=k, b=b: e.activation(sq[b][:, :n], x[:, k, :n], AF.Square), [tx], [tsq[b]])
        mm(P, R["pss"][:, :n], G["ones1024"][:], sq[b][:, :n], k == 0, k == 7, [tsq[b], G["tokc"]], [R["tpss"]])
    emit_rsqrt(P, R["rstd"][:, :n], R["pss"][:, :n], R["rtmp"][:, :n], R["tpss"], R["trstd"], R["trtmp"])
    for k in range(8):
        b = k % 2
        tt, ttt = R["tt"], R["ttt"]
        P.op("dve", lambda e, k=k, b=b: e.scalar_tensor_tensor(
            tt[b][:, :n], x[:, k, :n], G["A"][:, l, w, i, k:k + 1], R["rstd"][:, :n], ALU.mult, ALU.mult),
            [tx, R["trstd"], G["tok"]], [ttt[b]])
        P.op("act", lambda e, k=k, b=b: e.activation(
            h[:, k, :n], tt[b][:, :n], AF.Identity, bias=G["B"][:, l, w, i, k:k + 1], scale=1.0),
            [ttt[b], G["tok"]], [th])


def emit_rsqrt(P, out, in_ps, tmp, tin, tout, ttmp):
    P.op("dve", lambda e: e.tensor_scalar(tmp, in_ps, EPS, None, ALU.add), [tin], [ttmp])
    P.op("act", lambda e: e.activation(tmp, tmp, AF.Sqrt), [ttmp], [ttmp])
    P.op("dve", lambda e: e.reciprocal(out, tmp), [ttmp], [tout])


def alloc_norm_scratch(P):
    R = {}
    R["rtmp"] = P.sb("rtmp", [128, 512], F32)
    R["trtmp"] = P.tok("rtmp")
    R["sq"] = [P.sb("sq%d" % i, [128, 512], BF16) for i in range(2)]
    R["tsq"] = [P.tok("sq%d" % i) for i in range(2)]
    R["tt"] = [P.sb("tt%d" % i, [128, 512], F32) for i in range(2)]
    R["ttt"] = [P.tok("tt%d" % i) for i in range(2)]
    R["rstd"] = P.sb("rstd", [128, 512], F32)
    R["trstd"] = P.tok("rstd")
    R["pss"] = P.ps("pss", [128, 512], F32)
    R["tpss"] = P.tok("pss")
    return R


def ffn_pass(nc, G, name, Xin, Xout, wgu_d, wd_d, l, i, tiles, xout_off=0):
    P = Pass(nc, name)
    Wgu = P.sb("wgu", [128, 8, 2 * DFF], BF16)
    Wd = P.sb("wd", [128, NF, D], BF16)
    tW = P.tok("W")
    act = P.sb("act", [128, NF, 512], BF16)
    tact = [P.tok("act%d" % j) for j in range(NF)]
    stage = act[:].rearrange("p a b -> p (a b)").bitcast(F32)
    SW = 1408
    tst = [P.tok("st%d" % j) for j in range(4)]
    jobs = []
    for k in range(8):
        for cb in range(2 * DFF // SW):
            jobs.append((wgu_d[k * 128:(k + 1) * 128, cb * SW:(cb + 1) * SW], Wgu[:, k, cb * SW:(cb + 1) * SW], SW))
    for f in range(NF):
        jobs.append((wd_d[f * 128:(f + 1) * 128, :], Wd[:, f, :], D))
    for j, (src, dst, wdt) in enumerate(jobs):
        s = j % 4
        sv = stage[:, s * SW:s * SW + wdt]
        P.dma(sv, src, writes=[tst[s]], dom=tst[s])
        eng = "dve" if j % 2 == 0 else "pool"
        P.op(eng, lambda e, dst=dst, sv=sv: e.tensor_copy(dst, sv), [tst[s]], [tW])
    for j in range(NF):
        for s in range(4):
            pass
    R = alloc_norm_scratch(P)
    xn = P.sb("xn", [128, 8, 512], F32)
    txn = P.tok("xn")
    h = P.sb("h", [128, 8, 512], BF16)
    th = P.tok("h")
    NXR = 4
    xr = [P.sb("xr%d" % j, [128, 512], F32) for j in range(NXR)]
    txr = [P.tok("xr%d" % j) for j in range(NXR)]
    sl = [P.sb("sl%d" % j, [128, 512], F32) for j in range(2)]
    tsl = [P.tok("sl%d" % j) for j in range(2)]
    pg = [P.ps("pg%d" % j, [128, 512], F32) for j in range(2)]
    pu = [P.ps("pu%d" % j, [128, 512], F32) for j in range(2)]
    tpg = [P.tok("pg%d" % j) for j in range(2)]
    tpu = [P.tok("pu%d" % j) for j in range(2)]
    py = [P.ps("py%d" % j, [128, 512], F32) for j in range(2)]
    tpy = [P.tok("py%d" % j) for j in range(2)]

    def load_x(t0, n):
        for k in range(8):
            P.dma(xn[:, k, :n], Xin[k * 128:(k + 1) * 128, t0:t0 + n], writes=[txn], dom=txn)

    alias_ops = []
    load_x(tiles[0][0], tiles[0][1])
    st = {"cnt": 0, "ycnt": 0}

    def prenorm_tile(tj):
        t0j, nj, cj = tiles[tj]
        emit_prenorm(P, R, G, l, 1 if cj else 0, i, xn, txn, h, th, nj)
        if tj + 1 < len(tiles):
            load_x(tiles[tj + 1][0], tiles[tj + 1][1])

    prenorm_tile(0)

    def tile_body(ti, t0, n, isctx):
        w = 1 if isctx else 0
        for f in range(NF):
            b = st["cnt"] % 2
            st["cnt"] += 1
            for k in range(8):
                mm(P, pg[b][:, :n], Wgu[:, k, f * 128:(f + 1) * 128], h[:, k, :n], k == 0, k == 7, [tW, th], [tpg[b]])
            for k in range(8):
                mm(P, pu[b][:, :n], Wgu[:, k, DFF + f * 128:DFF + (f + 1) * 128], h[:, k, :n], k == 0, k == 7,
                   [tW, th], [tpu[b]])
            P.op("act", lambda e, b=b: e.activation(sl[b][:, :n], pg[b][:, :n], AF.Silu), [tpg[b]], [tsl[b]])
            extra_w = list(tst) if ti == 0 else []
            P.op("dve", lambda e, b=b, f=f: e.tensor_tensor(act[:, f, :n], pu[b][:, :n], sl[b][:, :n], ALU.mult),
                 [tpu[b], tsl[b]], [tact[f]] + extra_w)

        def load_xr(dc, yc):
            r = yc % NXR
            P.dma(xr[r][:, :n], Xin[dc * 128:(dc + 1) * 128, t0:t0 + n], writes=[txr[r]], dom=txr[r])
        load_xr(0, st["ycnt"])
        load_xr(1, st["ycnt"] + 1)
        for dc in range(8):
            b = st["ycnt"] % 2
            r = st["ycnt"] % NXR
            if dc + 2 < 8:
                load_xr(dc + 2, st["ycnt"] + 2)
            st["ycnt"] += 1
            if dc == 4 and ti + 1 < len(tiles):
                prenorm_tile(ti + 1)
            for f in range(NF):
                mm(P, py[b][:, :n], Wd[:, f, dc * 128:(dc + 1) * 128], act[:, f, :n], f == 0, f == NF - 1,
                   [tW, tact[f]], [tpy[b]])
            P.op("dve", lambda e, b=b, r=r, dc=dc: e.scalar_tensor_tensor(
                xr[r][:, :n], py[b][:, :n], G["GT"][:, l, w, i, dc:dc + 1], xr[r][:, :n], ALU.mult, ALU.add),
                [tpy[b], txr[r], G["tok"]], [tpy[b], txr[r]])
            P.dma(Xout[dc * 128:(dc + 1) * 128, t0 - xout_off:t0 - xout_off + n], xr[r][:, :n], reads=[txr[r]],
                  dom=txr[r])

    for ti, (t0, n, isctx) in enumerate(tiles):
        tile_body(ti, t0, n, isctx)
    P.emit()
    P.close()


def load_cast_weights(P, jobs, stage_tiles, tst, tW):
    ns = len(stage_tiles)
    for j, (src, dst, wdt) in enumerate(jobs):
        s = j % ns
        sv = stage_tiles[s][:, :wdt]
        P.dma(sv, src, writes=[tst[s]], dom=tst[s])
        eng = "dve" if j % 2 == 0 else "pool"
        P.op(eng, lambda e, dst=dst, sv=sv: e.tensor_copy(dst, sv), [tst[s]], [tW])


class Rot:
    def __init__(self, P, name, shape, dt, nbuf, psum=False):
        self.t = [(P.ps if psum else P.sb)("%s%d" % (name, i), shape, dt) for i in range(nbuf)]
        self.k = [P.tok("%s%d" % (name, i)) for i in range(nbuf)]
        self.i = 0

    def next(self):
        j = self.i % len(self.t)
        self.i += 1
        return self.t[j], self.k[j]


def prep0_pass(nc, G, dr, SC, Xin, tiles):
    P = Pass(nc, "p0")
    NCH = 16
    W = P.sb("W", [128, 8, NCH * 128], BF16)
    Wuq = P.sb("Wuq", [128, 2, 1024], BF16)
    Wukv = P.sb("Wukv", [128, 1024], BF16)
    tW = P.tok("W")
    stg = [P.sb("stg%d" % j, [128, 2048], F32) for j in range(2)]
    tst = [P.tok("stg%d" % j) for j in range(2)]
    jobs = [(dr["l0_win"][k * 128:(k + 1) * 128, :], W[:, k, :], NCH * 128) for k in range(8)]
    jobs += [(dr["l0_wuq"][k * 128:(k + 1) * 128, :], Wuq[:, k, :], 1024) for k in range(2)]
    jobs += [(dr["l0_wukv"][:, :], Wukv[:, :], 1024)]
    load_cast_weights(P, jobs, stg, tst, tW)
    gv = P.sb("gv", [128, 16], F32)
    Ms = P.sb("Ms", [128, 4, 128], BF16)
    tcst = P.tok("cst")
    P.dma(gv[:], dr["l0_gv"], writes=[tcst], dom=tcst)
    P.dma(Ms[:], dr["Ms"].rearrange("m p c -> p m c"), writes=[tcst], dom=tcst)
    R = alloc_norm_scratch(P)
    xn = P.sb("xn", [128, 8, 512], F32)
    txn = P.tok("xn")
    h = P.sb("h", [128, 8, 512], BF16)
    th = P.tok("h")
    tabs = [[P.sb("tab%d_%d" % (b, j), [128, 512], F32) for j in range(4)] for b in range(2)]
    ttab = [P.tok("tab%d" % b) for b in range(2)]
    tabd = [dr["cosm"], dr["sinm"], dr["cosw"], dr["sinw"]]
    pz = Rot(P, "pz", [128, 512], F32, 4, psum=True)
    pss2 = Rot(P, "pss2", [128, 512], F32, 1, psum=True)
    pv = Rot(P, "pv", [128, 512], F32, 1, psum=True)
    sq2 = Rot(P, "sq2", [128, 512], BF16, 2)
    rst = Rot(P, "rst", [128, 512], F32, 2)
    rtm = Rot(P, "rtm", [128, 512], F32, 2)
    ta = Rot(P, "ta", [128, 512], F32, 2)
    tb = Rot(P, "tb", [128, 512], F32, 2)
    outb = Rot(P, "outb", [128, 512], BF16, 4)
    vout = Rot(P, "vout", [128, 512], BF16, 2)
    ckvn = P.sb("ckvn", [128, 512], BF16)
    tckvn = P.tok("ckvn")
    cqn = P.sb("cqn", [128, 2, 512], BF16)
    tcqn = P.tok("cqn")

    def load_x(t0, n):
        for k in range(8):
            P.dma(xn[:, k, :n], Xin[k * 128:(k + 1) * 128, t0:t0 + n], writes=[txn], dom=txn)

    def proj(chunk, n, rows=128):
        z, tz = pz.next()
        for k in range(8):
            mm(P, z[:rows, :n], W[:, k, chunk * 128:chunk * 128 + rows], h[:, k, :n], k == 0, k == 7, [tW, th], [tz])
        return z, tz

    def rstd_of(zs, M, rows, n):
        p2, tp2 = pss2.next()
        for j, (z, tz) in enumerate(zs):
            q, tq = sq2.next()
            P.op("act", lambda e, q=q, z=z: e.activation(q[:rows, :n], z[:rows, :n], AF.Square), [tz], [tq])
            mm(P, p2[:rows, :n], M[:rows, :rows], q[:rows, :n], j == 0, j == len(zs) - 1, [tq, tcst], [tp2])
        r, tr = rst.next()
        tm, ttm = rtm.next()
        emit_rsqrt(P, r[:rows, :n], p2[:rows, :n], tm[:rows, :n], tp2, tr, ttm)
        return r, tr

    def scale_out(z, tz, r, tr, gcol, dst, tdst, rows, n):
        P.op("dve", lambda e: e.scalar_tensor_tensor(dst, z[:rows, :n], gv[:rows, gcol:gcol + 1], r[:rows, :n],
                                                     ALU.mult, ALU.mult), [tz, tr, tcst], [tdst])

    def rope_out(z, tz, zs, tzs, r, tr, gcol, cos, sin, ttb, dst, tdst, rows, n):
        a, tka = ta.next()
        b, tkb = tb.next()
        scale_out(z, tz, r, tr, gcol, a[:rows, :n], tka, rows, n)
        scale_out(zs, tzs, r, tr, gcol + 1, b[:rows, :n], tkb, rows, n)
        P.op("pool", lambda e: e.tensor_tensor(a[:rows, :n], a[:rows, :n], cos[:rows, :n], ALU.mult), [tka, ttb], [tka])
        P.op("pool", lambda e: e.tensor_tensor(b[:rows, :n], b[:rows, :n], sin[:rows, :n], ALU.mult), [tkb, ttb], [tkb])
        P.op("pool", lambda e: e.tensor_tensor(dst, a[:rows, :n], b[:rows, :n], ALU.add), [tka, tkb], [tdst])

    def store(dst_dram, src, tsrc):
        P.dma(dst_dram, src, reads=[tsrc], dom=tsrc)

    def load_tabs(bi, t0, n):
        for j in range(4):
            P.dma(tabs[bi][j][:, :n], tabd[j][:, t0:t0 + n], writes=[ttab[bi]], dom=ttab[bi])

    M128, M64, M32, M256 = Ms[:, 0, :], Ms[:, 1, :], Ms[:, 2, :], Ms[:, 3, :]
    load_x(tiles[0][0], tiles[0][1])
    load_tabs(0, tiles[0][0], tiles[0][1])

    def tile_body(ti, t0, n, isctx):
        w = 1 if isctx else 0
        bi = ti % 2
        cm, sm, cw, sw = tabs[bi]
        ttb = ttab[bi]
        emit_prenorm(P, R, G, 0, w, 1, xn, txn, h, th, n)
        if ti + 1 < len(tiles):
            load_x(tiles[ti + 1][0], tiles[ti + 1][1])
            load_tabs(1 - bi, tiles[ti + 1][0], tiles[ti + 1][1])
        nsub = n // 128
        z, tz = proj(0, n)
        r, tr = rstd_of([(z, tz)], M128, 128, n)
        scale_out(z, tz, r, tr, 0, ckvn[:, :n], tckvn, 128, n)
        for j in range(4):
            z, tz = pz.next()
            mm(P, z[:, :n], Wukv[:, j * 128:(j + 1) * 128], ckvn[:, :n], True, True, [tW, tckvn], [tz])
            r, tr = rstd_of([(z, tz)], M64, 128, n)
            o, to = outb.next()
            scale_out(z, tz, r, tr, 9, o[:, :n], to, 128, n)
            store(SC["KN"][j, :, t0:t0 + n], o[:, :n], to)
        for sub in range(nsub):
            p, tp = pv.next()
            mm(P, p[:, :], ckvn[:, sub * 128:(sub + 1) * 128], Wukv[:, 512:1024], True, True, [tW, tckvn], [tp])
            v, tv = vout.next()
            P.op("act", lambda e, v=v, p=p: e.copy(v[:, :], p[:, :]), [tp], [tv])
            store(SC["VA"][t0 + sub * 128:t0 + (sub + 1) * 128, :], v[:, :], tv)
        z, tz = proj(1, n, 32)
        zs, tzs = proj(2, n, 32)
        r, tr = rstd_of([(z, tz)], M32, 32, n)
        o, to = outb.next()
        rope_out(z, tz, zs, tzs, r, tr, 1, cm, sm, ttb, o[:32, :n], to, 32, n)
        store(SC["KR"][:, t0:t0 + n], o[:32, :n], to)
        z, tz = proj(3, n)
        zs, tzs = proj(4, n)
        r, tr = rstd_of([(z, tz)], M64, 128, n)
        o, to = outb.next()
        rope_out(z, tz, zs, tzs, r, tr, 3, cw, sw, ttb, o[:, :n], to, 128, n)
        store(SC["KB"][:, t0:t0 + n], o[:, :n], to)
        z0, tz0 = proj(5, n)
        z1, tz1 = proj(6, n)
        r, tr = rstd_of([(z0, tz0), (z1, tz1)], M256, 128, n)
        scale_out(z0, tz0, r, tr, 5, cqn[:, 0, :n], tcqn, 128, n)
        scale_out(z1, tz1, r, tr, 6, cqn[:, 1, :n], tcqn, 128, n)
        for j in range(4):
            z, tz = pz.next()
            for k in range(2):
                mm(P, z[:, :n], Wuq[:, k, j * 128:(j + 1) * 128], cqn[:, k, :n], k == 0, k == 1, [tW, tcqn], [tz])
            r, tr = rstd_of([(z, tz)], M64, 128, n)
            o, to = outb.next()
            scale_out(z, tz, r, tr, 10, o[:, :n], to, 128, n)
            store(SC["QN"][j, :, t0:t0 + n], o[:, :n], to)
        for j in range(2):
            z, tz = pz.next()
            for k in range(2):
                mm(P, z[:, :n], Wuq[:, k, 512 + j * 128:512 + (j + 1) * 128], cqn[:, k, :n], k == 0, k == 1,
                   [tW, tcqn], [tz])
            zs, tzs = pz.next()
            for k in range(2):
                mm(P, zs[:, :n], Wuq[:, k, 768 + j * 128:768 + (j + 1) * 128], cqn[:, k, :n], k == 0, k == 1,
                   [tW, tcqn], [tzs])
            r, tr = rstd_of([(z, tz)], M32, 128, n)
            o, to = outb.next()
            rope_out(z, tz, zs, tzs, r, tr, 11, cm, sm, ttb, o[:, :n], to, 128, n)
            store(SC["QR"][j, :, t0:t0 + n], o[:, :n], to)
        for j in range(4):
            z, tz = proj(7 + j, n)
            zs, tzs = proj(11 + j, n)
            r, tr = rstd_of([(z, tz)], M64, 128, n)
            o, to = outb.next()
            rope_out(z, tz, zs, tzs, r, tr, 7, cw, sw, ttb, o[:, :n], to, 128, n)
            store(SC["QB"][j, :, t0:t0 + n], o[:, :n], to)
        for sub in range(nsub):
            p, tp = pv.next()
            for k in range(8):
                mm(P, p[:, :128], h[:, k, sub * 128:(sub + 1) * 128], W[:, k, 15 * 128:16 * 128], k == 0, k == 7,
                   [tW, th], [tp])
            v, tv = vout.next()
            P.op("act", lambda e, v=v, p=p: e.copy(v[:, :128], p[:, :128]), [tp], [tv])
            store(SC["VB"][t0 + sub * 128:t0 + (sub + 1) * 128, :], v[:, :128], tv)

    for ti, (t0, n, isctx) in enumerate(tiles):
        tile_body(ti, t0, n, isctx)
    P.emit()
    P.close()


def attn_finalize(P, A, po, tpo, n, dst, tdst, sink=None, pview=False):
    rsb, trsb = A["rsb"].next()
    if sink is not None:
        P.op("dve", lambda e: e.tensor_tensor(rsb[64:65, :n], po[64:65, :n], sink, ALU.add), [tpo, A["tcst"]], [trsb])
        P.op("dve", lambda e: e.reciprocal(rsb[64:65, :n], rsb[64:65, :n]), [trsb], [trsb])
    else:
        P.op("dve", lambda e: e.reciprocal(rsb[64:65, :n], po[64:65, :n]), [tpo], [trsb])
    pbc, tpbc = A["pbc"].next()
    mm(P, pbc[0:64, :n], A["onesf"][64:65, 0:64], rsb[64:65, :n], True, True, [trsb, A["tcst"]], [tpbc])
    bcs, tbcs = A["bcs"].next()
    P.op("act", lambda e: e.copy(bcs[0:64, :n], pbc[0:64, :n]), [tpbc], [tbcs])
    if pview:
        a0 = po[0:64, :n].rearrange("p (g q) -> p g q", g=4)
        a1 = bcs[0:64, :n].rearrange("p (g q) -> p g q", g=4)
    else:
        a0 = po[0:64, :n]
        a1 = bcs[0:64, :n]
    P.op("dve", lambda e: e.tensor_tensor(dst, a0, a1, ALU.mult), [tpo, tbcs], [tdst, tpo])


def attn_common(P, dr):
    A = {}
    A["rsb"] = Rot(P, "rsb", [128, 512], F32, 2)
    A["pbc"] = Rot(P, "pbc", [128, 512], F32, 1, psum=True)
    A["bcs"] = Rot(P, "bcs", [64, 512], F32, 2)
    A["onesf"] = P.sb("onesf", [128, 64], F32)
    A["tcst"] = P.tok("acst")
    P.dma(A["onesf"][:], dr["onesf"], writes=[A["tcst"]], dom=A["tcst"])
    return A


def mla_pass(nc, G, dr, SC, tiles, heads=range(8)):
    P = Pass(nc, "pa")
    A = attn_common(P, dr)
    NKT = T // 128
    Kt = [P.sb("Kt%d" % b, [96, T], BF16) for b in range(2)]
    Qt = [P.sb("Qt%d" % b, [96, T], BF16) for b in range(2)]
    Vh = [P.sb("Vh%d" % b, [128, NKT, 65], BF16) for b in range(2)]
    tKQV = [P.tok("kqv%d" % b) for b in range(2)]
    for b in range(2):
        P.op("pool", lambda e, b=b: e.memset(Vh[b][:, :, 64:65], 1.0), [], [tKQV[b]])
    pS = Rot(P, "pS", [128, 512], F32, 4, psum=True)
    pO = Rot(P, "pO", [128, 512], F32, 2, psum=True)
    Pm = Rot(P, "Pm", [128, 512], BF16, 4)
    ob = Rot(P, "ob", [64, 512], BF16, 2)
    scale = 96.0 ** -0.5
    VAv = SC["VA"].rearrange("(kt p) c -> p kt c", p=128)

    def load_head(hh, b):
        tk = tKQV[b]
        P.dma(Kt[b][0:64, :], SC["KN"][hh // 2, (hh % 2) * 64:(hh % 2) * 64 + 64, :], writes=[tk], dom=tk)
        P.dma(Kt[b][64:96, :], SC["KR"][:, :], writes=[tk], dom=tk)
        P.dma(Qt[b][0:64, :], SC["QN"][hh // 2, (hh % 2) * 64:(hh % 2) * 64 + 64, :], writes=[tk], dom=tk)
        P.dma(Qt[b][64:96, :], SC["QR"][hh // 4, (hh % 4) * 32:(hh % 4) * 32 + 32, :], writes=[tk], dom=tk)
        for c0 in range(0, NKT, 11):
            P.dma(Vh[b][:, c0:c0 + 11, 0:64], VAv[:, c0:c0 + 11, hh * 64:(hh + 1) * 64], writes=[tk], dom=tk)

    heads = list(heads)
    load_head(heads[0], 0)
    for hi, hh in enumerate(heads):
        b = hi % 2
        if hi + 1 < len(heads):
            load_head(heads[hi + 1], 1 - b)
        tk = tKQV[b]
        steps = []
        for (t0, n, isctx) in tiles:
            nkt = 2 if isctx else NKT
            for kt in range(nkt):
                steps.append((t0, n, kt, nkt))
        LAG = 2
        pend = []
        cur = {}

        def issue_S(t0, n, kt, nkt):
            ps, tps = pS.next()
            mm(P, ps[:, :n], Kt[b][:, kt * 128:(kt + 1) * 128], Qt[b][:, t0:t0 + n], True, True, [tk], [tps])
            pm, tpm = Pm.next()
            P.op("act", lambda e: e.activation(pm[:, :n], ps[:, :n], AF.Exp, scale=scale), [tps], [tpm, tps])
            return pm, tpm

        def issue_PV(t0, n, kt, nkt, pm, tpm):
            if kt == 0:
                cur["po"], cur["tpo"] = pO.next()
            po, tpo = cur["po"], cur["tpo"]
            mm(P, po[0:65, :n], Vh[b][:, kt, :], pm[:, :n], kt == 0, kt == nkt - 1, [tk, tpm], [tpo])
            if kt == nkt - 1:
                def fin(po=po, tpo=tpo, n=n, t0=t0):
                    o, to = ob.next()
                    attn_finalize(P, A, po, tpo, n, o[:, :n], to, None)
                    P.dma(SC["OT"][hh * 64:(hh + 1) * 64, t0:t0 + n], o[:, :n], reads=[to], dom=to)
                fin_q.append([6, fin])

        fin_q = []
        for si, st_ in enumerate(steps):
            pm, tpm = issue_S(*st_)
            pend.append((st_, pm, tpm))
            if len(pend) > LAG:
                s0, pm0, tpm0 = pend.pop(0)
                issue_PV(*s0, pm0, tpm0)
            for it in fin_q:
                it[0] -= 1
            while fin_q and fin_q[0][0] <= 0:
                fin_q.pop(0)[1]()
        while pend:
            s0, pm0, tpm0 = pend.pop(0)
            issue_PV(*s0, pm0, tpm0)
        while fin_q:
            fin_q.pop(0)[1]()
    P.emit()
    P.close()


def win_pass(nc, G, dr, SC):
    P = Pass(nc, "pw")
    A = attn_common(P, dr)
    tc = A["tcst"]
    masks = P.sb("masks", [128, 2, 512], BF16)
    P.dma(masks[:], dr["wmask"].rearrange("m p c -> p m c"), writes=[tc], dom=tc)
    identb = P.sb("identb", [128, 128], BF16)
    P.dma(identb[:], dr["ident"], writes=[tc], dom=tc)
    sink = P.sb("sink", [128, 1024], F32)
    P.dma(sink[64:65, :], dr["l0_sink"], writes=[tc], dom=tc)
    P.op("act", lambda e: e.activation(sink[64:65, :], sink[64:65, :], AF.Exp), [tc], [tc])
    Kc = P.sb("Kc", [64, 2, LC], BF16)
    Vc = P.sb("Vc", [128, 2, 2, 65], BF16)
    P.op("pool", lambda e: e.memset(Vc[:, :, :, 64:65], 1.0), [], [tc])
    VBv = SC["VB"].rearrange("(kt p) c -> p kt c", p=128)
    for nk in range(2):
        P.dma(Kc[:, nk, :], SC["KB"][nk * 64:(nk + 1) * 64, 0:LC], writes=[tc], dom=tc)
        P.dma(Vc[:, nk, :, 0:64], VBv[:, 0:2, nk * 64:(nk + 1) * 64], writes=[tc], dom=tc)
    Qg = Rot(P, "Qg", [64, 4, 512], BF16, 2)
    Kg = Rot(P, "Kg", [64, 6 * 128], BF16, 2)
    Vg = Rot(P, "Vg", [128, 6, 65], BF16, 2)
    for v, tv in zip(Vg.t, Vg.k):
        P.op("pool", lambda e, v=v: e.memset(v[:, :, 64:65], 1.0), [], [tv])
    pS = Rot(P, "pS", [128, 512], F32, 3, psum=True)
    pO = Rot(P, "pO", [128, 512], F32, 3, psum=True)
    Pm = Rot(P, "Pm", [128, 512], BF16, 3)
    obg = Rot(P, "obg", [64, 4, 512], BF16, 2)
    scale = 64.0 ** -0.5
    fin_q = []
    groups = [("ctx", 0)] + [("lat", g) for g in range(S // 512)]
    for nk in range(2):
        for kind, gi in groups:
            q, tq = Qg.next()
            if kind == "ctx":
                q0, nq = 0, LC
            else:
                q0, nq = LC + gi * 512, 512
            for g in range(4):
                hq = nk * 4 + g
                P.dma(q[:, g, :nq], SC["QB"][hq // 2, (hq % 2) * 64:(hq % 2) * 64 + 64, q0:q0 + nq], writes=[tq], dom=tq)
            if kind == "lat":
                blo = max(4 * gi - 1, 0)
                bhi = min(4 * gi + 4, S // 128 - 1)
                nb = bhi - blo + 1
                kg, tkg = Kg.next()
                vg, tvg = Vg.next()
                P.dma(kg[:, :nb * 128], SC["KB"][nk * 64:(nk + 1) * 64, LC + blo * 128:LC + (bhi + 1) * 128],
                      writes=[tkg], dom=tkg)
                P.dma(vg[:, :nb, 0:64], VBv[:, 2 + blo:2 + bhi + 1, nk * 64:(nk + 1) * 64], writes=[tvg], dom=tvg)
            og, tog = obg.next()
            for qb in range(nq // 128):
                keys = [("c", 0, None), ("c", 1, None)]
                if kind == "lat":
                    i = 4 * gi + qb
                    if i - 1 >= 0:
                        keys.append(("l", i - 1 - blo, 0))
                    keys.append(("l", i - blo, None))
                    if i + 1 <= S // 128 - 1:
                        keys.append(("l", i + 1 - blo, 1))
                po, tpo = pO.next()
                for ki, (kk, idx, mk) in enumerate(keys):
                    ps, tps = pS.next()
                    if kk == "c":
                        lhsT = Kc[:, nk, idx * 128:(idx + 1) * 128]
                        vv = Vc[:, nk, idx, :]
                        rd = [tc]
                    else:
                        lhsT = kg[:, idx * 128:(idx + 1) * 128]
                        vv = vg[:, idx, :]
                        rd = [tkg, tvg]
                    mm(P, ps[:, :].rearrange("p (g q) -> p g q", g=4), lhsT, q[:, :, qb * 128:(qb + 1) * 128],
                       True, mk is None, rd + [tq], [tps])
                    if mk is not None:
                        mm(P, ps[:, :], identb[:, :], masks[:, mk, :], False, True, [tc], [tps])
                    pm, tpm = Pm.next()
                    P.op("act", lambda e, pm=pm, ps=ps: e.activation(pm[:, :], ps[:, :], AF.Exp, scale=scale),
                         [tps], [tpm, tps])
                    mm(P, po[0:65, :], vv, pm[:, :], ki == 0, ki == len(keys) - 1, rd + [tpm], [tpo])
                while fin_q:
                    fin_q.pop(0)()

                def fin(po=po, tpo=tpo, og=og, tog=tog, qb=qb, nk=nk):
                    dst = og[:, :, qb * 128:(qb + 1) * 128]
                    attn_finalize(P, A, po, tpo, 512, dst, tog, sink=sink[64:65, nk * 512:(nk + 1) * 512], pview=True)
                fin_q.append(fin)
            while fin_q:
                fin_q.pop(0)()
            for g in range(4):
                hq = nk * 4 + g
                P.dma(SC["OT"][512 + hq * 64:512 + (hq + 1) * 64, q0:q0 + nq], og[:, g, :nq], reads=[tog], dom=tog)
    P.emit()
    P.close()


def out_pass(nc, G, name, Xin, Xout, OT, wout_d, l, tiles, ot_off=0):
    P = Pass(nc, name)
    Wo = P.sb("Wo", [128, 8, D], BF16)
    tW = P.tok("W")
    stg = [P.sb("stg%d" % j, [128, 1024], F32) for j in range(2)]
    tst = [P.tok("stg%d" % j) for j in range(2)]
    jobs = [(wout_d[k * 128:(k + 1) * 128, :], Wo[:, k, :], D) for k in range(8)]
    load_cast_weights(P, jobs, stg, tst, tW)
    ot = Rot(P, "ot", [128, 8, 512], BF16, 2)
    xr = Rot(P, "xr", [128, 512], F32, 4)
    py = Rot(P, "py", [128, 512], F32, 2, psum=True)

    def tile_body(t0, n, isctx):
        w = 1 if isctx else 0
        o, to = ot.next()
        for k in range(8):
            P.dma(o[:, k, :n], OT[k * 128:(k + 1) * 128, t0 - ot_off:t0 - ot_off + n], writes=[to], dom=to)
        for dc in range(8):
            x, tx = xr.next()
            P.dma(x[:, :n], Xin[dc * 128:(dc + 1) * 128, t0:t0 + n], writes=[tx], dom=tx)
            p, tp = py.next()
            for k in range(8):
                mm(P, p[:, :n], Wo[:, k, dc * 128:(dc + 1) * 128], o[:, k, :n], k == 0, k == 7, [tW, to], [tp])
            P.op("dve", lambda e, x=x, p=p, dc=dc: e.scalar_tensor_tensor(
                x[:, :n], p[:, :n], G["GT"][:, l, w, 1, dc:dc + 1], x[:, :n], ALU.mult, ALU.add),
                [tp, tx, G["tok"]], [tp, tx])
            P.dma(Xout[dc * 128:(dc + 1) * 128, t0:t0 + n], x[:, :n], reads=[tx], dom=tx)

    for (t0, n, isctx) in tiles:
        tile_body(t0, n, isctx)
    P.emit()
    P.close()


def prep1_pass(nc, G, dr, SC, Xin, tiles):
    P = Pass(nc, "p1")
    NCH = 25
    W = P.sb("W", [128, 8, NCH * 128], BF16)
    tW = P.tok("W")
    stg = [P.sb("stg%d" % j, [128, 1600], F32) for j in range(2)]
    tst = [P.tok("stg%d" % j) for j in range(2)]
    jobs = []
    for k in range(8):
        for hf in range(2):
            jobs.append((dr["l1_win"][k * 128:(k + 1) * 128, hf * 1600:(hf + 1) * 1600], W[:, k, hf * 1600:(hf + 1) * 1600], 1600))
    load_cast_weights(P, jobs, stg, tst, tW)
    tc = P.tok("cst")
    TRI = P.sb("TRI", [128, 2, 128], F32)
    ident = P.sb("ident", [128, 128], BF16)
    WG = P.sb("WG", [64, 512], F32)
    P.dma(TRI[:], dr["tri"].rearrange("m p c -> p m c"), writes=[tc], dom=tc)
    P.dma(ident[:], dr["ident"], writes=[tc], dom=tc)
    P.dma(WG[:], dr["l1_wg"], writes=[tc], dom=tc)
    LFB = P.sb("LFB", [64, 512], F32)
    tLFB = P.tok("LFB")
    P.op("dve", lambda e: e.memset(LFB[:], 1.0), [], [tLFB])
    R = alloc_norm_scratch(P)
    xn = P.sb("xn", [128, 8, 512], F32)
    txn = P.tok("xn")
    h = P.sb("h", [128, 8, 512], BF16)
    th = P.tok("h")
    qT = P.sb("qT", [128, 4, 512], F32)
    kT = P.sb("kT", [128, 4, 512], F32)
    tqT = P.tok("qT")
    tkT = P.tok("kT")
    og = Rot(P, "og", [128, 8, 512], BF16, 1)
    pz = Rot(P, "pz", [128, 512], F32, 2, psum=True)
    pv = Rot(P, "pv", [128, 512], F32, 1, psum=True)
    pG = Rot(P, "pG", [128, 4, 128], F32, 2, psum=True)
    pT = Rot(P, "pT", [128, 512], BF16, 1, psum=True)
    ee = Rot(P, "ee", [128, 512], F32, 2)
    ll = Rot(P, "ll", [128, 512], F32, 2)
    E = Rot(P, "E", [128, 4, 128], F32, 2)
    Ei = Rot(P, "Ei", [128, 4, 128], F32, 2)
    QDs = [Rot(P, "QDs%d" % d, [128, 4, 512], BF16, 1) for d in range(2)]
    KIs = [Rot(P, "KIs%d" % d, [128, 4, 512], BF16, 1) for d in range(2)]
    KEs = Rot(P, "KEs", [128, 512], BF16, 2)
    keT = Rot(P, "keT", [128, 128], BF16, 4)
    Vs = Rot(P, "Vs", [128, 1024], BF16, 2)
    DECs = Rot(P, "DECs", [128, 2, 4, 8], F32, 2)
    qscale = 128.0 ** -0.5

    def load_x(t0, n):
        for k in range(8):
            P.dma(xn[:, k, :n], Xin[k * 128:(k + 1) * 128, t0:t0 + n], writes=[txn], dom=txn)

    def proj(chunk, n):
        z, tz = pz.next()
        for k in range(8):
            mm(P, z[:, :n], W[:, k, chunk * 128:(chunk + 1) * 128], h[:, k, :n], k == 0, k == 7, [tW, th], [tz])
        return z, tz

    load_x(tiles[0][0], tiles[0][1])

    def tile_body(ti, t0, n, isctx):
        w = 1 if isctx else 0
        emit_prenorm(P, R, G, 1, w, 1, xn, txn, h, th, n)
        if ti + 1 < len(tiles):
            load_x(tiles[ti + 1][0], tiles[ti + 1][1])
        nsub = n // 128
        c0 = t0 // 64
        for hd in range(4):
            z, tz = proj(hd, n)
            P.op("act", lambda e, z=z, hd=hd: e.copy(kT[:, hd, :n], z[:, :n]), [tz], [tkT, tz])
            z, tz = proj(4 + hd, n)
            P.op("act", lambda e, z=z, hd=hd: e.mul(qT[:, hd, :n], z[:, :n], qscale), [tz], [tqT, tz])
        z, tz = proj(16, n)
        P.op("act", lambda e, z=z: e.copy(LFB[0:16, :n], z[0:16, :n]), [tz], [tLFB])
        P.op("act", lambda e, z=z: e.copy(LFB[32:48, :n], z[32:48, :n]), [tz], [tLFB, tz])
        o_g, tog = og.next()
        for j in range(8):
            z, tz = proj(8 + j, n)
            P.op("act", lambda e, z=z, j=j: e.activation(o_g[:, j, :n], z[:, :n], AF.Silu), [tz], [tog, tz])
        for j in range(8):
            P.dma(SC["OG"][j * 128:(j + 1) * 128, t0:t0 + n], o_g[:, j, :n], reads=[tog], dom=tog)
        for sub in range(nsub):
            v, tv = Vs.next()
            for hf in range(2):
                p, tp = pv.next()
                for k in range(8):
                    mm(P, p[:, :], h[:, k, sub * 128:(sub + 1) * 128], W[:, k, (17 + 4 * hf) * 128:(21 + 4 * hf) * 128],
                       k == 0, k == 7, [tW, th], [tp])
                P.op("act", lambda e, v=v, p=p, hf=hf: e.copy(v[:, hf * 512:(hf + 1) * 512], p[:, :]), [tp], [tv, tp])
            P.dma(SC["V1"][t0 + sub * 128:t0 + (sub + 1) * 128, :], v[:, :], reads=[tv], dom=tv)
        dec, tdec = DECs.next()

        def dir_body(d):
            qd, tqd = QDs[d].next()
            ki, tki = KIs[d].next()
            r0 = 32 * d
            for sub in range(nsub):
                sl_ = slice(sub * 128, (sub + 1) * 128)
                p, tp = pv.next()
                mm(P, p[:, :], LFB[r0:r0 + 17, sl_], WG[r0:r0 + 17, :], True, True, [tLFB, tc], [tp])
                e1, te1 = ee.next()
                P.op("act", lambda e, e1=e1, p=p: e.activation(e1[:, :], p[:, :], AF.Exp, scale=-1.0), [tp], [te1, tp])
                l1, tl1 = ll.next()
                P.op("act", lambda e, e1=e1, l1=l1: e.activation(l1[:, :], e1[:, :], AF.Ln, bias=1.0), [te1], [tl1])
                g, tg = pG.next()
                for hd in range(4):
                    mm(P, g[:, hd, :], l1[:, hd * 128:(hd + 1) * 128], TRI[:, d, :], True, True, [tl1, tc], [tg])
                Ex, tEx = E.next()
                Eix, tEix = Ei.next()
                P.op("act", lambda e, Ex=Ex, g=g: e.activation(Ex[:], g[:], AF.Exp), [tg], [tEx])
                P.op("act", lambda e, Eix=Eix, g=g: e.activation(Eix[:], g[:], AF.Exp, scale=-1.0), [tg], [tEix, tg])
                ke_s, tke_s = KEs.next()
                pt, tpt = pT.next()
                for hd in range(4):
                    for c in range(2):
                        col = (63 + 64 * c) if d == 0 else (64 * c)
                        P.op("dve", lambda e, hd=hd, c=c, col=col, Ex=Ex, sub=sub: e.tensor_copy(
                            dec[:, d, hd, 2 * sub + c:2 * sub + c + 1], Ex[:, hd, col:col + 1]), [tEx], [tdec])
                    P.op("dve", lambda e, hd=hd, Ex=Ex, sl_=sl_: e.tensor_tensor(
                        qd[:, hd, sl_], qT[:, hd, sl_], Ex[:, hd, :], ALU.mult), [tqT, tEx], [tqd])
                    P.op("pool", lambda e, hd=hd, Eix=Eix, sl_=sl_: e.tensor_tensor(
                        ki[:, hd, sl_], kT[:, hd, sl_], Eix[:, hd, :], ALU.mult), [tkT, tEix], [tki])
                    kt_, tkt_ = keT.next()
                    for c in range(2):
                        cs = slice(sub * 128 + 64 * c, sub * 128 + 64 * (c + 1))
                        P.op("dve", lambda e, hd=hd, c=c, cs=cs, kt_=kt_, Eix=Eix, sub=sub: e.scalar_tensor_tensor(
                            kt_[:, 64 * c:64 * (c + 1)], kT[:, hd, cs], dec[:, d, hd, 2 * sub + c:2 * sub + c + 1],
                            Eix[:, hd, 64 * c:64 * (c + 1)], ALU.mult, ALU.mult), [tkT, tdec, tEix], [tkt_])
                    P.op("pe", lambda e, hd=hd, kt_=kt_, pt=pt: e.transpose(pt[:, hd * 128:(hd + 1) * 128], kt_[:, :], ident[:]),
                         [tkt_, tc], [tpt])
                P.op("act", lambda e, ke_s=ke_s, pt=pt: e.copy(ke_s[:, :], pt[:, :]), [tpt], [tke_s, tpt])
                P.dma(SC["KE"][d, t0 + sub * 128:t0 + (sub + 1) * 128, :], ke_s[:, :], reads=[tke_s], dom=tke_s)
            for hd in range(4):
                P.dma(SC["QD"][d, hd, :, t0:t0 + n], qd[:, hd, :n], reads=[tqd], dom=tqd)
                P.dma(SC["KI"][d, hd, :, t0:t0 + n], ki[:, hd, :n], reads=[tki], dom=tki)
        for d in range(2):
            dir_body(d)
        nch = n // 64
        for d in range(2):
            P.dma(SC["DEC"][d, :, :, c0:c0 + nch], dec[:, d, :, :nch], reads=[tdec], dom=tdec)

    for ti, (t0, n, isctx) in enumerate(tiles):
        tile_body(ti, t0, n, isctx)
    P.emit()
    P.close()


def scan_pass(nc, G, dr, SC, d):
    P = Pass(nc, "s%d" % d)
    tc = P.tok("cst")
    MASK = P.sb("MASK", [128, 4, 128], F32)
    P.dma(MASK[:], dr["gmask"][d], writes=[tc], dom=tc)
    DEC = P.sb("DEC", [128, 4, T // 64], F32)
    P.dma(DEC[:], SC["DEC"][d], writes=[tc], dom=tc)
    M256 = P.sb("M256", [128, 128], BF16)
    P.dma(M256[:], dr["Ms"][3], writes=[tc], dom=tc)
    gn = P.sb("gn", [128, 2], F32)
    P.dma(gn[:], dr["l1_gn"], writes=[tc], dom=tc)
    S32 = P.sb("S32", [128, 4, 256], F32)
    S16 = P.sb("S16", [128, 4, 256], BF16)
    tS32h = [P.tok("S32_%d" % j) for j in range(4)]
    tS16 = P.tok("S16")
    P.op("dve", lambda e: e.memset(S32[:], 0.0), [], tS32h)
    P.op("pool", lambda e: e.memset(S16[:], 0.0), [], [tS16])
    QD = Rot(P, "QD", [128, 4, 512], BF16, 2)
    KI = Rot(P, "KI", [128, 4, 512], BF16, 2)
    KE = Rot(P, "KE", [128, 4, 512], BF16, 2)
    V = Rot(P, "V", [128, 4, 1024], BF16, 2)
    OB = Rot(P, "OB", [128, 8, 512], F32, 2)
    OGt = Rot(P, "OGt", [128, 8, 512], BF16, 2)
    O1 = Rot(P, "O1", [128, 8, 512], BF16, 2)
    Am = Rot(P, "Am", [128, 4, 128], BF16, 2)
    osum = Rot(P, "osum", [128, 8, 128], F32, 2)
    sqs = Rot(P, "sqs", [128, 8, 128], BF16, 2)
    rs = Rot(P, "rs", [128, 4, 128], F32, 2)
    rt = Rot(P, "rt", [128, 4, 128], F32, 2)
    o1f = Rot(P, "o1f", [128, 8, 128], F32, 2)
    pA = Rot(P, "pA", [128, 4, 128], F32, 2, psum=True)
    pO = Rot(P, "pO", [128, 8, 128], F32, 1, psum=True)
    pKV = Rot(P, "pKV", [128, 4, 256], F32, 1, psum=True)
    pss = Rot(P, "pss", [128, 4, 128], F32, 1, psum=True)
    KEv = SC["KE"][d].rearrange("(s p) c -> p s c", p=128)
    V1v = SC["V1"].rearrange("(s p) c -> p s c", p=128)
    blocks = [(0, LC, True)] + [(LC + i * 512, 512, False) for i in range(S // 512)]
    if d == 1:
        blocks = [blocks[0]] + blocks[1:][::-1]

    def state_update(ke, tke, v, tv, sub, c, chunk_id):
        p, tp = pKV.next()
        rows = slice(64 * c, 64 * (c + 1))
        for hd in range(4):
            mm(P, p[:, hd, :], ke[rows, sub, hd * 128:(hd + 1) * 128], v[rows, sub, hd * 256:(hd + 1) * 256],
               True, True, [tke, tv], [tp])
        for hd in range(4):
            P.op("dve", lambda e, hd=hd, p=p: e.scalar_tensor_tensor(
                S32[:, hd, :], S32[:, hd, :], DEC[:, hd, chunk_id:chunk_id + 1], p[:, hd, :], ALU.mult, ALU.add),
                [tp, tS32h[hd], tc], [tS32h[hd], tp])
        P.op("act", lambda e: e.copy(S16[:], S32[:]), tS32h, [tS16])

    comb_q = []

    def combine(os_, tos, sl_, o1, to1, ogt, togt):
        sq, tsq = sqs.next()
        P.op("act", lambda e: e.activation(sq[:], os_[:], AF.Square), [tos], [tsq])
        ps_, tps = pss.next()
        for hd in range(4):
            for dvc in range(2):
                mm(P, ps_[:, hd, :], M256[:], sq[:, hd * 2 + dvc, :], dvc == 0, dvc == 1, [tsq, tc], [tps])
        r, tr = rs.next()
        tm, ttm = rt.next()
        emit_rsqrt(P, r[:], ps_[:], tm[:], tps, tr, ttm)
        of, tof = o1f.next()
        for hd in range(4):
            for dvc in range(2):
                P.op("dve", lambda e, hd=hd, dvc=dvc: e.scalar_tensor_tensor(
                    of[:, hd * 2 + dvc, :], os_[:, hd * 2 + dvc, :], gn[:, dvc:dvc + 1], r[:, hd, :],
                    ALU.mult, ALU.mult), [tos, tr, tc], [tof])
        P.op("pool", lambda e: e.tensor_tensor(o1[:, :, sl_], of[:], ogt[:, :, sl_], ALU.mult), [tof, togt], [to1])

    for (t0, nb, isctx) in blocks:
        nsub = nb // 128
        ke, tke = KE.next()
        v, tv = V.next()
        s0 = t0 // 128
        P.dma(ke[:, :nsub, :], KEv[:, s0:s0 + nsub, :], writes=[tke], dom=tke)
        for sub in range(nsub):
            P.dma(v[:, sub, :], V1v[:, s0 + sub, :], writes=[tv], dom=tv)
        subs = list(range(nsub))
        corder = [0, 1]
        if d == 1:
            subs = subs[::-1]
            corder = [1, 0]
        if isctx:
            for sub in subs:
                for c in corder:
                    state_update(ke, tke, v, tv, sub, c, (t0 + sub * 128) // 64 + c)
            continue
        qd, tqd = QD.next()
        ki, tki = KI.next()
        for hd in range(4):
            P.dma(qd[:, hd, :], SC["QD"][d, hd, :, t0:t0 + nb], writes=[tqd], dom=tqd)
            P.dma(ki[:, hd, :], SC["KI"][d, hd, :, t0:t0 + nb], writes=[tki], dom=tki)
        if d == 1:
            ob, tob = OB.next()
        else:
            ob, tob = OB.next()
            ogt, togt = OGt.next()
            o1, to1 = O1.next()
            for j in range(8):
                P.dma(ob[:, j, :], SC["OBF"][j * 128:(j + 1) * 128, t0 - LC:t0 - LC + nb], writes=[tob], dom=tob)
                P.dma(ogt[:, j, :], SC["OG"][j * 128:(j + 1) * 128, t0:t0 + nb], writes=[togt], dom=togt)
        for sub in subs:
            sl_ = slice(sub * 128, (sub + 1) * 128)
            a, ta = pA.next()
            for hd in range(4):
                mm(P, a[:, hd, :], ki[:, hd, sl_], qd[:, hd, sl_], True, True, [tki, tqd], [ta])
            am, tam = Am.next()
            P.op("dve", lambda e, am=am, a=a: e.tensor_tensor(am[:], a[:], MASK[:], ALU.mult), [ta, tc], [tam, ta])
            po, tpo = pO.next()
            for hd in range(4):
                for dvc in range(2):
                    mm(P, po[:, hd * 2 + dvc, :], v[:, sub, hd * 256 + dvc * 128:hd * 256 + (dvc + 1) * 128], am[:, hd, :],
                       (hd * 2 + dvc) % 4 == 0, False, [tv, tam], [tpo], sgc=True)
            for ci, c in enumerate(corder):
                cs = slice(sub * 128 + 64 * c, sub * 128 + 64 * (c + 1))
                for hd in range(4):
                    for dvc in range(2):
                        mm(P, po[:, hd * 2 + dvc, 64 * c:64 * (c + 1)], S16[:, hd, dvc * 128:(dvc + 1) * 128], qd[:, hd, cs],
                           False, True, [tS16, tqd], [tpo], sgc=True)
                state_update(ke, tke, v, tv, sub, c, (t0 + sub * 128) // 64 + c)
            if d == 1:
                for hb in range(2):
                    P.op("act", lambda e, po=po, ob=ob, sl_=sl_, hb=hb: e.copy(ob[:, 4 * hb:4 * hb + 4, sl_], po[:, 4 * hb:4 * hb + 4, :]),
                         [tpo], [tob, tpo])
            else:
                os_, tos = osum.next()
                for hb in range(2):
                    P.op("dve", lambda e, os_=os_, po=po, ob=ob, sl_=sl_, hb=hb: e.tensor_tensor(
                        os_[:, 4 * hb:4 * hb + 4, :], po[:, 4 * hb:4 * hb + 4, :], ob[:, 4 * hb:4 * hb + 4, sl_], ALU.add),
                        [tpo, tob], [tos, tpo])
                while comb_q:
                    comb_q.pop(0)()

                def comb(os_=os_, tos=tos, sl_=sl_, o1=o1, to1=to1, ogt=ogt, togt=togt):
                    combine(os_, tos, sl_, o1, to1, ogt, togt)
                comb_q.append(comb)
        while comb_q:
            comb_q.pop(0)()
        if d == 1:
            for j in range(8):
                P.dma(SC["OBF"][j * 128:(j + 1) * 128, t0 - LC:t0 - LC + nb], ob[:, j, :], reads=[tob], dom=tob)
        else:
            for j in range(8):
                P.dma(SC["OT"][j * 128:(j + 1) * 128, t0:t0 + nb], o1[:, j, :], reads=[to1], dom=to1)
    P.emit()
    P.close()


def build_program(upto=99, dumps=()):
    nc = bass.Bass("TRN2", target_bir_lowering=False)
    dr = {}

    def din(name, shape, dt=F32):
        dr[name] = nc.dram_tensor(name, list(shape), dt, kind="ExternalInput").ap()

    din("xT", [D, T])
    din("c8", [128, 8])
    din("cc8", [128, 8])
    for l in range(2):
        din("l%d_wmod" % l, [D, 9 * D])
        din("l%d_bmod" % l, [128, 72])
        din("l%d_ng" % l, [128, 24])
        for f in (1, 2):
            din("l%d_ffn%d_wgu" % (l, f), [D, 2 * DFF])
            din("l%d_ffn%d_wd" % (l, f), [DFF, D])
    din("ones1024", [128, 128], BF16)
    din("l0_win", [D, 2048])
    din("l0_wuq", [256, 1024])
    din("l0_wukv", [128, 1024])
    din("l0_gv", [128, 16])
    din("l0_wout", [D, D])
    din("l0_sink", [1, 1024])
    din("Ms", [4, 128, 128], BF16)
    din("wmask", [2, 128, 512], BF16)
    din("onesf", [128, 64])
    for nm in ("cosm", "sinm", "cosw", "sinw"):
        din(nm, [128, T])
    din("l1_win", [D, 3200])
    din("l1_wg", [64, 512])
    din("l1_gn", [128, 2])
    din("l1_wout", [D, D])
    din("tri", [2, 128, 128])
    din("gmask", [2, 128, 4, 128])
    din("ident", [128, 128], BF16)
    SC = {}
    for nm, shp in (("QN", [4, 128, T]), ("QR", [2, 128, T]), ("KN", [4, 128, T]), ("KR", [32, T]), ("KB", [128, T]),
                    ("QB", [4, 128, T]), ("VA", [T, 512]), ("VB", [T, 128]), ("OT", [D, T])):
        SC[nm] = nc.dram_tensor("sc_" + nm, shp, BF16, kind="Internal").ap()
    for nm, shp, dt in (("QD", [2, 4, 128, T], BF16), ("KI", [2, 4, 128, T], BF16), ("KE", [2, T, 512], BF16),
                        ("V1", [T, 1024], BF16), ("OG", [D, T], BF16), ("DEC", [2, 128, 4, T // 64], F32),
                        ("OBF", [D, S], F32)):
        SC[nm] = nc.dram_tensor("sc_" + nm, shp, dt, kind="Internal").ap()
    out = nc.dram_tensor("outT", [D, S], F32, kind="ExternalOutput").ap()
    XA = nc.dram_tensor("XA", [D, T], F32, kind="Internal").ap()
    XB = nc.dram_tensor("XB", [D, T], F32, kind="Internal").ap()
    dump_aps = {}
    sc_dumps = {}
    for nm, shape in dumps:
        if nm in SC:
            sc_dumps[nm] = nc.dram_tensor("dump_" + nm, list(SC[nm].shape), SC[nm].dtype, kind="ExternalOutput").ap()
        else:
            dump_aps[nm] = nc.dram_tensor("dump_" + nm, list(shape), F32, kind="ExternalOutput").ap()

    with ExitStack() as ges:
        SEMPOOL[0] = SemPool(nc, ges, 96)
        G = {}
        G["MOD"] = ges.enter_context(nc.sbuf_tensor("gMOD", [128, 2, 2, 72], F32))
        G["A"] = ges.enter_context(nc.sbuf_tensor("gA", [128, 2, 2, 3, 8], F32))
        G["B"] = ges.enter_context(nc.sbuf_tensor("gB", [128, 2, 2, 3, 8], F32))
        G["GT"] = ges.enter_context(nc.sbuf_tensor("gGT", [128, 2, 2, 3, 8], F32))
        G["ones1024"] = ges.enter_context(nc.sbuf_tensor("gones", [128, 128], BF16))
        G["tok"] = Tok("G")
        G["tokc"] = Tok("Gc")
        P = Pass(nc, "pc")
        P.dma(G["ones1024"][:], dr["ones1024"], writes=[G["tokc"]], dom=G["tokc"])
        P.emit()
        P.close()
        mod_pass(nc, G, dr)
        tiles = token_tiles()
        final_src = XA
        if upto >= 1:
            ffn_pass(nc, G, "f1", dr["xT"], XA, dr["l0_ffn1_wgu"], dr["l0_ffn1_wd"], 0, 0, tiles)
        if upto >= 2:
            prep0_pass(nc, G, dr, SC, XA, tiles)
        if upto >= 3:
            mla_pass(nc, G, dr, SC, tiles)
            win_pass(nc, G, dr, SC)
        if upto >= 4:
            out_pass(nc, G, "o0", XA, XB, SC["OT"], dr["l0_wout"], 0, tiles)
            final_src = XB
        if upto >= 5:
            ffn_pass(nc, G, "f2", XB, XA, dr["l0_ffn2_wgu"], dr["l0_ffn2_wd"], 0, 2, tiles)
            final_src = XA
        lat_tiles = [t for t in tiles if not t[2]]
        if upto >= 6:
            ffn_pass(nc, G, "f3", XA, XB, dr["l1_ffn1_wgu"], dr["l1_ffn1_wd"], 1, 0, tiles)
            final_src = XB
        if upto >= 7:
            prep1_pass(nc, G, dr, SC, XB, tiles)
        if upto >= 8:
            scan_pass(nc, G, dr, SC, 1)
            scan_pass(nc, G, dr, SC, 0)
        if upto >= 9:
            out_pass(nc, G, "o1", XB, XA, SC["OT"], dr["l1_wout"], 1, lat_tiles)
            final_src = XA
        if upto >= 10:
            ffn_pass(nc, G, "f4", XA, out, dr["l1_ffn2_wgu"], dr["l1_ffn2_wd"], 1, 2, lat_tiles, xout_off=LC)
            final_src = None
        P = Pass(nc, "pd")
        tdd = P.tok("dd")
        for nm, ap in sc_dumps.items():
            src = SC[nm]
            if len(src.shape) == 2:
                for r0 in range(0, src.shape[0], 128):
                    P.dma(ap[r0:r0 + 128, :], src[r0:r0 + 128, :], dom=tdd)
            elif len(src.shape) == 3:
                for a in range(src.shape[0]):
                    for r0 in range(0, src.shape[1], 128):
                        P.dma(ap[a, r0:r0 + 128, :], src[a, r0:r0 + 128, :], dom=tdd)
            else:
                for a in range(src.shape[0]):
                    for b2 in range(src.shape[1]):
                        P.dma(ap[a, b2], src[a, b2], dom=tdd)
        cp = [P.sb("cp%d" % j, [128, 2048], F32) for j in range(2)]
        tcp = [P.tok("cp%d" % j) for j in range(2)]
        srcs = []
        if "MOD" in dump_aps:
            P.dma(dump_aps["MOD"], G["MOD"][:].rearrange("p a b c -> p (a b c)"),
                  reads=[G["tok"]], dom=tcp[0])
        jj = 0
        for nm, ap in list(dump_aps.items()) + [("__out", out)]:
            if nm == "MOD":
                continue
            src = {"XA": XA, "XB": XB}.get(nm, None)
            if src is None and nm != "__out":
                continue
            col0 = 0
            if nm == "__out":
                if final_src is None:
                    continue
                src = final_src
                col0 = LC
            ncol = ap.shape[1]
            for k in range(8):
                for c0 in range(0, ncol, 2048):
                    b = jj % 2
                    jj += 1
                    wd = min(2048, ncol - c0)
                    P.dma(cp[b][:, :wd], src[k * 128:(k + 1) * 128, col0 + c0:col0 + c0 + wd], writes=[tcp[b]], dom=tcp[b])
                    P.dma(ap[k * 128:(k + 1) * 128, c0:c0 + wd], cp[b][:, :wd], reads=[tcp[b]], dom=tcp[b])
        P.emit()
        P.close()
    return nc


def host_inputs(inp, b):
    f = np.float32
    m = {}
    xT = np.concatenate([inp["ctx"][b], inp["x"][b]], axis=0).T
    m["xT"] = np.ascontiguousarray(xT, dtype=f)
    m["c8"] = np.ascontiguousarray(inp["c"][b].reshape(8, 128).T, dtype=f)
    m["cc8"] = np.ascontiguousarray(inp["c_ctx"].reshape(8, 128).T, dtype=f)
    for l in range(2):
        p = "l%d_" % l
        m[p + "wmod"] = np.ascontiguousarray(inp[p + "w_mod"], dtype=f)
        m[p + "bmod"] = np.ascontiguousarray(inp[p + "b_mod"].reshape(72, 128).T, dtype=f)
        m[p + "ng"] = np.ascontiguousarray(inp[p + "norm_g"].reshape(24, 128).T, dtype=f)
        for k in (1, 2):
            m[p + "ffn%d_wgu" % k] = np.ascontiguousarray(inp[p + "ffn%d_w_gu" % k], dtype=f)
            m[p + "ffn%d_wd" % k] = np.ascontiguousarray(inp[p + "ffn%d_w_down" % k], dtype=f)
    m["ones1024"] = np.full((128, 128), 1.0 / 1024, dtype=ml_dtypes.bfloat16)
    m.update(host_consts())
    m.update(host_l0(inp))
    m.update(host_l1(inp))
    return m


def host_l1(inp):
    f = np.float32
    m = {}
    w = inp["l1_w_in"].astype(f)
    k, v, lf, lb, q, g = w[:, 0:512], w[:, 512:1536], w[:, 1536:1552], w[:, 1552:1568], w[:, 1568:2080], w[:, 2080:3104]
    z16 = np.zeros((D, 16), f)
    z80 = np.zeros((D, 80), f)
    m["l1_win"] = np.ascontiguousarray(np.concatenate([k, q, g, lf, z16, lb, z80, v], axis=1))
    assert m["l1_win"].shape == (D, 3200)
    wg = np.zeros((64, 512), f)
    wg[0:16] = inp["l1_w_gk_f"]
    wg[16] = inp["l1_b_gk_f"]
    wg[32:48] = inp["l1_w_gk_b"]
    wg[48] = inp["l1_b_gk_b"]
    m["l1_wg"] = wg
    m["l1_gn"] = np.ascontiguousarray(inp["l1_g_norm"].astype(f).reshape(2, 128).T)
    m["l1_wout"] = np.ascontiguousarray(inp["l1_w_out"], dtype=f)
    return m


def _swap(a):
    return a[:, np.arange(a.shape[1]) ^ 1]


def rope_tables(rot_dim, rep):
    n_freq = rot_dim // 4
    inv = (np.float32(10000.0) ** (-np.arange(n_freq, dtype=np.float32) / np.float32(n_freq))).astype(np.float32)
    t = np.arange(S)
    row = (t // 64).astype(np.float32)
    col = (t % 64).astype(np.float32)
    ang = np.concatenate([row[:, None] * inv, col[:, None] * inv], axis=-1).astype(np.float32)
    cos = np.cos(ang).astype(np.float32)
    sin = np.sin(ang).astype(np.float32)
    d = np.arange(rot_dim)
    C = np.ones((rot_dim, T), np.float32)
    Sg = np.zeros((rot_dim, T), np.float32)
    C[:, LC:] = cos[:, d // 2].T
    sign = np.where(d % 2 == 0, -1.0, 1.0).astype(np.float32)
    Sg[:, LC:] = sin[:, d // 2].T * sign[:, None]
    return np.ascontiguousarray(np.tile(C, (rep, 1))), np.ascontiguousarray(np.tile(Sg, (rep, 1)))


_CONSTS = {}


def host_consts():
    if _CONSTS:
        return _CONSTS
    bf = ml_dtypes.bfloat16
    m = {}
    Ms = np.zeros((4, 128, 128), np.float32)
    Ms[0] = 1.0 / 128
    for b in range(2):
        Ms[1, b * 64:(b + 1) * 64, b * 64:(b + 1) * 64] = 1.0 / 64
    for b in range(4):
        Ms[2, b * 32:(b + 1) * 32, b * 32:(b + 1) * 32] = 1.0 / 32
    Ms[3] = 1.0 / 256
    m["Ms"] = Ms.astype(bf)
    kk = np.arange(128)[:, None]
    qq = np.arange(128)[None, :]
    wm = np.stack([np.tile((kk >= qq).astype(np.float32), (1, 4)), np.tile((kk <= qq).astype(np.float32), (1, 4))])
    wm = (1.0 - wm) * np.float32(-30000.0)
    m["wmask"] = wm.astype(bf)
    m["onesf"] = np.ones((128, 64), np.float32)
    sidx = np.arange(128)[:, None]
    tidx = np.arange(128)[None, :]
    same = (sidx // 64) == (tidx // 64)
    lo = (same & (sidx <= tidx)).astype(np.float32)
    hi = (same & (sidx >= tidx)).astype(np.float32)
    m["tri"] = np.stack([lo, hi]) * np.float32(-1.0 / 16.0)
    m["gmask"] = np.ascontiguousarray(np.stack([np.repeat(lo[:, None, :], 4, axis=1), np.repeat(hi[:, None, :], 4, axis=1)]))
    m["ident"] = np.eye(128, dtype=np.float32).astype(bf)
    m["cosm"], m["sinm"] = rope_tables(32, 4)
    m["cosw"], m["sinw"] = rope_tables(64, 2)
    _CONSTS.update(m)
    return _CONSTS


def host_l0(inp):
    f = np.float32
    m = {}
    w = inp["l0_w_in"].astype(f)
    ckv, kr, wk, wv, cq, wq = w[:, 0:128], w[:, 128:160], w[:, 160:288], w[:, 288:416], w[:, 416:672], w[:, 672:1184]
    pad = np.zeros((D, 96), f)
    m["l0_win"] = np.ascontiguousarray(np.concatenate(
        [ckv, kr, pad, _swap(kr), pad, wk, _swap(wk), cq, wq, _swap(wq), wv], axis=1))
    assert m["l0_win"].shape == (D, 2048)
    uq = inp["l0_mla_w_uq"].astype(f).reshape(256, 8, 96)
    nope = uq[:, :, :64].reshape(256, 512)
    rope = uq[:, :, 64:].reshape(256, 256)
    m["l0_wuq"] = np.ascontiguousarray(np.concatenate([nope, rope, _swap(rope)], axis=1))
    ukv = inp["l0_mla_w_ukv"].astype(f).reshape(128, 8, 128)
    m["l0_wukv"] = np.ascontiguousarray(np.concatenate([ukv[:, :, :64].reshape(128, 512), ukv[:, :, 64:].reshape(128, 512)], axis=1))
    gv = np.ones((128, 16), f)
    sw = lambda g: g[np.arange(g.shape[0]) ^ 1]
    gv[:, 0] = inp["l0_mla_g_kva"]
    gv[:32, 1] = inp["l0_mla_g_kr"]
    gv[:32, 2] = sw(inp["l0_mla_g_kr"])
    gv[:, 3] = np.tile(inp["l0_win_g_k"], 2)
    gv[:, 4] = np.tile(sw(inp["l0_win_g_k"]), 2)
    gv[:, 5] = inp["l0_mla_g_qa"][:128]
    gv[:, 6] = inp["l0_mla_g_qa"][128:]
    gv[:, 7] = np.tile(inp["l0_win_g_q"], 2)
    gv[:, 8] = np.tile(sw(inp["l0_win_g_q"]), 2)
    gv[:, 9] = np.tile(inp["l0_mla_g_kn"], 2)
    gv[:, 10] = np.tile(inp["l0_mla_g_qn"], 2)
    gv[:, 11] = np.tile(inp["l0_mla_g_qr"], 4)
    gv[:, 12] = np.tile(sw(inp["l0_mla_g_qr"]), 4)
    m["l0_gv"] = gv
    m["l0_wout"] = np.ascontiguousarray(inp["l0_w_out"], dtype=f)
    m["l0_sink"] = np.ascontiguousarray(np.repeat(inp["l0_win_sink"].astype(f).reshape(8), 128).reshape(1, 1024))
    return m


_INPUT_NAMES = (
    "x", "c", "ctx", "c_ctx",
    "l0_norm_g", "l0_w_mod", "l0_b_mod", "l0_ffn1_w_gu", "l0_ffn1_w_down", "l0_ffn2_w_gu", "l0_ffn2_w_down",
    "l0_w_in", "l0_mla_g_qa", "l0_mla_g_kva", "l0_mla_w_uq", "l0_mla_w_ukv", "l0_mla_g_qn", "l0_mla_g_qr",
    "l0_mla_g_kn", "l0_mla_g_kr", "l0_win_g_q", "l0_win_g_k", "l0_win_sink", "l0_w_out",
    "l1_norm_g", "l1_w_mod", "l1_b_mod", "l1_ffn1_w_gu", "l1_ffn1_w_down", "l1_ffn2_w_gu", "l1_ffn2_w_down",
    "l1_w_in", "l1_w_gk_f", "l1_b_gk_f", "l1_w_gk_b", "l1_b_gk_b", "l1_g_norm", "l1_w_out",
)


def kernel(**inp):
    inp = {k: np.asarray(inp[k]) for k in _INPUT_NAMES}
    nc = build_program()
    in_maps = [host_inputs(inp, b) for b in range(NCORES)]
    res = run_bass_kernel_spmd(nc, in_maps, core_ids=list(range(NCORES)))
    outs = [np.asarray(res.results[b]["outT"]).T for b in range(NCORES)]
    return np.ascontiguousarray(np.stack(outs, axis=0).astype(np.float32))
```

```python
import numpy as np
from contextlib import ExitStack
import ml_dtypes
import concourse.bass as bass
import concourse.mybir as mybir
from concourse.bass_utils import run_bass_kernel_spmd

F32 = mybir.dt.float32
BF16 = mybir.dt.bfloat16
AF = mybir.ActivationFunctionType
ALU = mybir.AluOpType

D = 1024
DFF = 2816
NF = DFF // 128
LC = 256
S = 8192
T = LC + S
EPS = 1e-6
NCORES = 8

ENGS = ("pe", "act", "dve", "pool", "sp")


class Tok:
    __slots__ = ("name", "lw", "rd", "dom")

    def __init__(self, name):
        self.name = name
        self.lw = None
        self.rd = {}
        self.dom = None


class Op:
    __slots__ = ("eng", "fn", "deps", "inc", "dom", "val", "sem", "pas", "isdma")


class SemPool:
    def __init__(self, nc, es, n):
        self.items = [[es.enter_context(nc.semaphore("sp%d" % i)), 0, False] for i in range(n)]

    def acquire(self):
        free = [it for it in self.items if not it[2]]
        it = min(free, key=lambda x: x[1])
        it[2] = True
        return it


SEMPOOL = [None]


class Pass:
    def __init__(self, nc, name):
        self.nc = nc
        self.sems = []
        self.name = name
        self.es = ExitStack()
        self.streams = {e: [] for e in ENGS}
        self.doms = {}
        self.last = {}
        self.nalloc = 0

    def sb(self, name, shape, dt):
        return self.es.enter_context(self.nc.sbuf_tensor(self.name + "_" + name, list(shape), dt))

    def ps(self, name, shape, dt=F32):
        return self.es.enter_context(self.nc.psum_tensor(self.name + "_" + name, list(shape), dt))

    def tok(self, name):
        return Tok(name)

    def _record(self, o, reads, writes):
        deps = set()
        for t in reads:
            if t.lw is not None:
                deps.add(t.lw)
        for t in writes:
            if t.lw is not None:
                deps.add(t.lw)
            for r in t.rd.values():
                deps.add(r)
        deps.discard(o)
        deps = [d for d in deps if d.pas is self]
        for d in deps:
            d.inc = True
        o.deps = deps
        for t in reads:
            if t not in writes:
                t.rd[o.dom] = o
        for t in writes:
            t.lw = o
            t.rd = {}
        self.streams[o.eng].append(o)
        self.last[o.dom] = o

    def op(self, eng, fn, reads=(), writes=()):
        o = Op()
        o.eng = eng
        o.fn = fn
        o.inc = False
        o.dom = eng
        o.pas = self
        o.isdma = False
        o.val = None
        o.sem = None
        self._record(o, list(reads), list(writes))
        return o

    def dma(self, out, in_, reads=(), writes=(), dom=None, q="sp"):
        assert dom is not None
        if dom.dom is None or dom.dom[0] is not self:
            it = SEMPOOL[0].acquire()
            self.sems.append(it)
            sem = it
            dom.dom = (self, sem, [0], "dma%d" % len(self.doms))
            self.doms[dom.dom[3]] = dom.dom
        o = Op()
        o.eng = q
        o.fn = lambda e, out=out, in_=in_: e.dma_start(out=out, in_=in_)
        o.inc = True
        o.dom = dom.dom[3]
        o.pas = self
        o.isdma = True
        dom.dom[2][0] += 16
        o.val = dom.dom[2][0]
        o.sem = dom.dom[1]
        self._record(o, list(reads), list(writes))
        return o

    def emit(self):
        nc = self.nc
        engsem = {}
        for e in ("pe", "act", "dve", "pool"):
            engsem[e] = SEMPOOL[0].acquire()
            self.sems.append(engsem[e])
        for e in ("pe", "act", "dve", "pool", "sp"):
            cnt = 0
            for o in self.streams[e]:
                if o.isdma:
                    continue
                if o.fn is None:
                    continue
                if o.inc:
                    cnt += 1
                    o.val = cnt
                    o.sem = engsem[e]
        finals = []
        for dom, o in self.last.items():
            if o.fn is None:
                continue
            if not o.isdma and not o.inc:
                o.inc = True
            finals.append(o)
        for e in ("pe", "act", "dve", "pool", "sp"):
            cnt = 0
            for o in self.streams[e]:
                if o.isdma or o.fn is None:
                    continue
                if o.inc:
                    cnt += 1
                    o.val = cnt
                    o.sem = engsem[e]
        handles = {"pe": nc.tensor, "act": nc.scalar, "dve": nc.vector, "pool": nc.gpsimd, "sp": nc.sync}
        streams = self.streams

        def run(ename, eh):
            seen = {}
            for o in streams[ename]:
                for d in o.deps:
                    if (not d.isdma) and d.eng == "pe" and ename == "pe" and not o.isdma:
                        continue
                    k = id(d.sem)
                    if seen.get(k, 0) < d.val:
                        eh.wait_ge(d.sem[0], d.sem[1] + d.val)
                        seen[k] = d.val
                if o.fn is None:
                    continue
                ins = o.fn(eh)
                if o.isdma:
                    ins.then_inc(o.sem[0], 16)
                elif o.inc:
                    ins.then_inc(o.sem[0], 1)
            for d in finals:
                k = id(d.sem)
                if seen.get(k, 0) < d.val:
                    eh.wait_ge(d.sem[0], d.sem[1] + d.val)
                    seen[k] = d.val

        with nc.named_scope(self.name), nc.Block() as blk:
            @blk.tensor
            def _(e):
                run("pe", e)

            @blk.scalar
            def _(e):
                run("act", e)

            @blk.vector
            def _(e):
                run("dve", e)

            @blk.gpsimd
            def _(e):
                run("pool", e)

            @blk.sync
            def _(e):
                run("sp", e)
        tot = {}
        for e in ENGS:
            for o in streams[e]:
                if o.fn is None:
                    continue
                if o.isdma:
                    tot[id(o.sem)] = tot.get(id(o.sem), 0) + 16
                elif o.inc:
                    tot[id(o.sem)] = tot.get(id(o.sem), 0) + 1
        for it in self.sems:
            it[1] += tot.get(id(it), 0)
            it[2] = False
            assert it[1] < 60000, it[1]

    def close(self):
        self.es.close()


def mm(P, out, lhsT, rhs, start, stop, reads, writes, sgc=False):
    if sgc:
        return P.op("pe", lambda e: e.matmul(out, lhsT, rhs, start=start, stop=stop, skip_group_check=True), reads, writes)
    return P.op("pe", lambda e: e.matmul(out, lhsT, rhs, start=start, stop=stop), reads, writes)


def token_tiles():
    tiles = [(0, LC, True)]
    for i in range(S // 512):
        tiles.append((LC + i * 512, 512, False))
    return tiles


def mod_pass(nc, G, dr):
    P = Pass(nc, "pm")
    c2 = P.sb("c2", [128, 2, 8], F32)
    r2 = P.sb("r2", [128, 2, 8], F32)
    tc2 = P.tok("c2")
    tr2 = P.tok("r2")
    P.dma(c2[:, 0, :], dr["c8"], writes=[tc2], dom=tc2)
    P.dma(c2[:, 1, :], dr["cc8"], writes=[tc2], dom=tc2)
    P.op("act", lambda e: e.activation(r2[:], c2[:], AF.Silu), [tc2], [tr2])
    GW = 1152
    NG = 9216 // GW
    wbuf = [P.sb("w%d" % i, [128, 8, GW], F32) for i in range(2)]
    twb = [P.tok("w%d" % i) for i in range(2)]
    bm = P.sb("bm", [128, 2, 72], F32)
    ng = P.sb("ng", [128, 2, 24], F32)
    tbm = P.tok("bm")
    for l in range(2):
        P.dma(bm[:, l, :], dr["l%d_bmod" % l], writes=[tbm], dom=tbm)
        P.dma(ng[:, l, :], dr["l%d_ng" % l], writes=[tbm], dom=tbm)
    pm = P.ps("pm", [128, 2, 72, 2], F32)
    tpm = P.tok("pm")
    it = 0
    for l in range(2):
        wm = dr["l%d_wmod" % l]
        for g in range(NG):
            b = it % 2
            it += 1
            for k in range(8):
                P.dma(wbuf[b][:, k, :], wm[k * 128:(k + 1) * 128, g * GW:(g + 1) * GW], writes=[twb[b]], dom=twb[b])
            for cb in range(GW // 128):
                col = g * (GW // 128) + cb
                for k in range(8):
                    mm(P, pm[:, l, col, :], wbuf[b][:, k, cb * 128:(cb + 1) * 128], r2[:, :, k],
                       k == 0, k == 7, [twb[b], tr2], [tpm])
    tG = G["tok"]
    for l in range(2):
        for w in range(2):
            P.op("dve", lambda e, l=l, w=w: e.tensor_tensor(G["MOD"][:, l, w, :], pm[:, l, :, w], bm[:, l, :], ALU.add),
                 [tpm, tbm], [tG])
    for l in range(2):
        for w in range(2):
            for i in range(3):
                sh = G["MOD"][:, l, w, (3 * i) * 8:(3 * i) * 8 + 8]
                sc = G["MOD"][:, l, w, (3 * i + 1) * 8:(3 * i + 1) * 8 + 8]
                gt = G["MOD"][:, l, w, (3 * i + 2) * 8:(3 * i + 2) * 8 + 8]
                P.op("dve", lambda e, l=l, w=w, i=i, sc=sc: e.scalar_tensor_tensor(
                    G["A"][:, l, w, i, :], sc, 1.0, ng[:, l, i * 8:(i + 1) * 8], ALU.add, ALU.mult), [tG, tbm], [tG])
                P.op("dve", lambda e, l=l, w=w, i=i, sh=sh: e.tensor_copy(G["B"][:, l, w, i, :], sh), [tG], [tG])
                fac = 1.0 if i == 1 else 0.5
                P.op("dve", lambda e, l=l, w=w, i=i, gt=gt, fac=fac: e.tensor_scalar(
                    G["GT"][:, l, w, i, :], gt, fac, None, ALU.mult), [tG], [tG])
    P.emit()
    P.close()


def emit_prenorm(P, R, G, l, w, i, x, tx, h, th, n, tag=""):
    sq, tsq = R["sq"], R["tsq"]
    for k in range(8):
        b = k % 2
        P.op("act", lambda e, k=k, b=b: e.activation(sq[b][:, :n], x[:, k, :n], AF.Square), [tx], [tsq[b]])
        mm(P, R["pss"][:, :n], G["ones1024"][:], sq[b][:, :n], k == 0, k == 7, [tsq[b], G["tokc"]], [R["tpss"]])
    emit_rsqrt(P, R["rstd"][:, :n], R["pss"][:, :n], R["rtmp"][:, :n], R["tpss"], R["trstd"], R["trtmp"])
    for k in range(8):
        b = k % 2
        tt, ttt = R["tt"], R["ttt"]
        P.op("dve", lambda e, k=k, b=b: e.scalar_tensor_tensor(
            tt[b][:, :n], x[:, k, :n], G["A"][:, l, w, i, k:k + 1], R["rstd"][:, :n], ALU.mult, ALU.mult),
            [tx, R["trstd"], G["tok"]], [ttt[b]])
        P.op("act", lambda e, k=k, b=b: e.activation(
            h[:, k, :n], tt[b][:, :n], AF.Identity, bias=G["B"][:, l, w, i, k:k + 1], scale=1.0),
            [ttt[b], G["tok"]], [th])


def emit_rsqrt(P, out, in_ps, tmp, tin, tout, ttmp):
    P.op("dve", lambda e: e.tensor_scalar(tmp, in_ps, EPS, None, ALU.add), [tin], [ttmp])
    P.op("act", lambda e: e.activation(tmp, tmp, AF.Sqrt), [ttmp], [ttmp])
    P.op("dve", lambda e: e.reciprocal(out, tmp), [ttmp], [tout])


def alloc_norm_scratch(P):
    R = {}
    R["rtmp"] = P.sb("rtmp", [128, 512], F32)
    R["trtmp"] = P.tok("rtmp")
    R["sq"] = [P.sb("sq%d" % i, [128, 512], BF16) for i in range(2)]
    R["tsq"] = [P.tok("sq%d" % i) for i in range(2)]
    R["tt"] = [P.sb("tt%d" % i, [128, 512], F32) for i in range(2)]
    R["ttt"] = [P.tok("tt%d" % i) for i in range(2)]
    R["rstd"] = P.sb("rstd", [128, 512], F32)
    R["trstd"] = P.tok("rstd")
    R["pss"] = P.ps("pss", [128, 512], F32)
    R["tpss"] = P.tok("pss")
    return R


def ffn_pass(nc, G, name, Xin, Xout, wgu_d, wd_d, l, i, tiles, xout_off=0):
    P = Pass(nc, name)
    Wgu = P.sb("wgu", [128, 8, 2 * DFF], BF16)
    Wd = P.sb("wd", [128, NF, D], BF16)
    tW = P.tok("W")
    act = P.sb("act", [128, NF, 512], BF16)
    tact = [P.tok("act%d" % j) for j in range(NF)]
    stage = act[:].rearrange("p a b -> p (a b)").bitcast(F32)
    SW = 1408
    tst = [P.tok("st%d" % j) for j in range(4)]
    jobs = []
    for k in range(8):
        for cb in range(2 * DFF // SW):
            jobs.append((wgu_d[k * 128:(k + 1) * 128, cb * SW:(cb + 1) * SW], Wgu[:, k, cb * SW:(cb + 1) * SW], SW))
    for f in range(NF):
        jobs.append((wd_d[f * 128:(f + 1) * 128, :], Wd[:, f, :], D))
    for j, (src, dst, wdt) in enumerate(jobs):
        s = j % 4
        sv = stage[:, s * SW:s * SW + wdt]
        P.dma(sv, src, writes=[tst[s]], dom=tst[s])
        eng = "dve" if j % 2 == 0 else "pool"
        P.op(eng, lambda e, dst=dst, sv=sv: e.tensor_copy(dst, sv), [tst[s]], [tW])
    for j in range(NF):
        for s in range(4):
            pass
    R = alloc_norm_scratch(P)
    xn = P.sb("xn", [128, 8, 512], F32)
    txn = P.tok("xn")
    h = P.sb("h", [128, 8, 512], BF16)
    th = P.tok("h")
    NXR = 4
    xr = [P.sb("xr%d" % j, [128, 512], F32) for j in range(NXR)]
    txr = [P.tok("xr%d" % j) for j in range(NXR)]
    sl = [P.sb("sl%d" % j, [128, 512], F32) for j in range(2)]
    tsl = [P.tok("sl%d" % j) for j in range(2)]
    pg = [P.ps("pg%d" % j, [128, 512], F32) for j in range(2)]
    pu = [P.ps("pu%d" % j, [128, 512], F32) for j in range(2)]
    tpg = [P.tok("pg%d" % j) for j in range(2)]
    tpu = [P.tok("pu%d" % j) for j in range(2)]
    py = [P.ps("py%d" % j, [128, 512], F32) for j in range(2)]
    tpy = [P.tok("py%d" % j) for j in range(2)]

    def load_x(t0, n):
        for k in range(8):
            P.dma(xn[:, k, :n], Xin[k * 128:(k + 1) * 128, t0:t0 + n], writes=[txn], dom=txn)

    alias_ops = []
    load_x(tiles[0][0], tiles[0][1])
    st = {"cnt": 0, "ycnt": 0}

    def prenorm_tile(tj):
        t0j, nj, cj = tiles[tj]
        emit_prenorm(P, R, G, l, 1 if cj else 0, i, xn, txn, h, th, nj)
        if tj + 1 < len(tiles):
            load_x(tiles[tj + 1][0], tiles[tj + 1][1])

    prenorm_tile(0)

    def tile_body(ti, t0, n, isctx):
        w = 1 if isctx else 0
        for f in range(NF):
            b = st["cnt"] % 2
            st["cnt"] += 1
            for k in range(8):
                mm(P, pg[b][:, :n], Wgu[:, k, f * 128:(f + 1) * 128], h[:, k, :n], k == 0, k == 7, [tW, th], [tpg[b]])
            for k in range(8):
                mm(P, pu[b][:, :n], Wgu[:, k, DFF + f * 128:DFF + (f + 1) * 128], h[:, k, :n], k == 0, k == 7,
                   [tW, th], [tpu[b]])
            P.op("act", lambda e, b=b: e.activation(sl[b][:, :n], pg[b][:, :n], AF.Silu), [tpg[b]], [tsl[b]])
            extra_w = list(tst) if ti == 0 else []
            P.op("dve", lambda e, b=b, f=f: e.tensor_tensor(act[:, f, :n], pu[b][:, :n], sl[b][:, :n], ALU.mult),
                 [tpu[b], tsl[b]], [tact[f]] + extra_w)

        def load_xr(dc, yc):
            r = yc % NXR
            P.dma(xr[r][:, :n], Xin[dc * 128:(dc + 1) * 128, t0:t0 + n], writes=[txr[r]], dom=txr[r])
        load_xr(0, st["ycnt"])
        load_xr(1, st["ycnt"] + 1)
        for dc in range(8):
            b = st["ycnt"] % 2
            r = st["ycnt"] % NXR
            if dc + 2 < 8:
                load_xr(dc + 2, st["ycnt"] + 2)
            st["ycnt"] += 1
            if dc == 4 and ti + 1 < len(tiles):
                prenorm_tile(ti + 1)
            for f in range(NF):
                mm(P, py[b][:, :n], Wd[:, f, dc * 128:(dc + 1) * 128], act[:, f, :n], f == 0, f == NF - 1,
                   [tW, tact[f]], [tpy[b]])
            P.op("dve", lambda e, b=b, r=r, dc=dc: e.scalar_tensor_tensor(
                xr[r][:, :n], py[b][:, :n], G["GT"][:, l, w, i, dc:dc + 1], xr[r][:, :n], ALU.mult, ALU.add),
                [tpy[b], txr[r], G["tok"]], [tpy[b], txr[r]])
            P.dma(Xout[dc * 128:(dc + 1) * 128, t0 - xout_off:t0 - xout_off + n], xr[r][:, :n], reads=[txr[r]],
                  dom=txr[r])

    for ti, (t0, n, isctx) in enumerate(tiles):
        tile_body(ti, t0, n, isctx)
    P.emit()
    P.close()


def load_cast_weights(P, jobs, stage_tiles, tst, tW):
    ns = len(stage_tiles)
    for j, (src, dst, wdt) in enumerate(jobs):
        s = j % ns
        sv = stage_tiles[s][:, :wdt]
        P.dma(sv, src, writes=[tst[s]], dom=tst[s])
        eng = "dve" if j % 2 == 0 else "pool"
        P.op(eng, lambda e, dst=dst, sv=sv: e.tensor_copy(dst, sv), [tst[s]], [tW])


class Rot:
    def __init__(self, P, name, shape, dt, nbuf, psum=False):
        self.t = [(P.ps if psum else P.sb)("%s%d" % (name, i), shape, dt) for i in range(nbuf)]
        self.k = [P.tok("%s%d" % (name, i)) for i in range(nbuf)]
        self.i = 0

    def next(self):
        j = self.i % len(self.t)
        self.i += 1
        return self.t[j], self.k[j]


def prep0_pass(nc, G, dr, SC, Xin, tiles):
    P = Pass(nc, "p0")
    NCH = 16
    W = P.sb("W", [128, 8, NCH * 128], BF16)
    Wuq = P.sb("Wuq", [128, 2, 1024], BF16)
    Wukv = P.sb("Wukv", [128, 1024], BF16)
    tW = P.tok("W")
    stg = [P.sb("stg%d" % j, [128, 2048], F32) for j in range(2)]
    tst = [P.tok("stg%d" % j) for j in range(2)]
    jobs = [(dr["l0_win"][k * 128:(k + 1) * 128, :], W[:, k, :], NCH * 128) for k in range(8)]
    jobs += [(dr["l0_wuq"][k * 128:(k + 1) * 128, :], Wuq[:, k, :], 1024) for k in range(2)]
    jobs += [(dr["l0_wukv"][:, :], Wukv[:, :], 1024)]
    load_cast_weights(P, jobs, stg, tst, tW)
    gv = P.sb("gv", [128, 16], F32)
    Ms = P.sb("Ms", [128, 4, 128], BF16)
    tcst = P.tok("cst")
    P.dma(gv[:], dr["l0_gv"], writes=[tcst], dom=tcst)
    P.dma(Ms[:], dr["Ms"].rearrange("m p c -> p m c"), writes=[tcst], dom=tcst)
    R = alloc_norm_scratch(P)
    xn = P.sb("xn", [128, 8, 512], F32)
    txn = P.tok("xn")
    h = P.sb("h", [128, 8, 512], BF16)
    th = P.tok("h")
    tabs = [[P.sb("tab%d_%d" % (b, j), [128, 512], F32) for j in range(4)] for b in range(2)]
    ttab = [P.tok("tab%d" % b) for b in range(2)]
    tabd = [dr["cosm"], dr["sinm"], dr["cosw"], dr["sinw"]]
    pz = Rot(P, "pz", [128, 512], F32, 6, psum=True)
    pss2 = Rot(P, "pss2", [128, 512], F32, 1, psum=True)
    sq2 = Rot(P, "sq2", [128, 512], BF16, 2)
    rst = Rot(P, "rst", [128, 512], F32, 2)
    rtm = Rot(P, "rtm", [128, 512], F32, 2)
    ta = Rot(P, "ta", [128, 512], F32, 2)
    tb = Rot(P, "tb", [128, 512], F32, 2)
    outb = Rot(P, "outb", [128, 512], BF16, 4)
    vout = Rot(P, "vout", [128, 512], BF16, 2)
    ckvn = P.sb("ckvn", [128, 512], BF16)
    tckvn = P.tok("ckvn")
    cqn = P.sb("cqn", [128, 2, 512], BF16)
    tcqn = P.tok("cqn")

    def load_x(t0, n):
        for k in range(8):
            P.dma(xn[:, k, :n], Xin[k * 128:(k + 1) * 128, t0:t0 + n], writes=[txn], dom=txn)

    def proj(chunk, n, rows=128):
        z, tz = pz.next()
        for k in range(8):
            mm(P, z[:rows, :n], W[:, k, chunk * 128:chunk * 128 + rows], h[:, k, :n], k == 0, k == 7, [tW, th], [tz])
        return z, tz

    def rstd_of(zs, M, rows, n):
        p2, tp2 = pss2.next()
        for j, (z, tz) in enumerate(zs):
            q, tq = sq2.next()
            P.op("act", lambda e, q=q, z=z: e.activation(q[:rows, :n], z[:rows, :n], AF.Square), [tz], [tq])
            mm(P, p2[:rows, :n], M[:rows, :rows], q[:rows, :n], j == 0, j == len(zs) - 1, [tq, tcst], [tp2])
        r, tr = rst.next()
        tm, ttm = rtm.next()
        emit_rsqrt(P, r[:rows, :n], p2[:rows, :n], tm[:rows, :n], tp2, tr, ttm)
        return r, tr

    def scale_out(z, tz, r, tr, gcol, dst, tdst, rows, n):
        P.op("dve", lambda e: e.scalar_tensor_tensor(dst, z[:rows, :n], gv[:rows, gcol:gcol + 1], r[:rows, :n],
                                                     ALU.mult, ALU.mult), [tz, tr, tcst], [tdst])

    def rope_out(z, tz, zs, tzs, r, tr, gcol, cos, sin, ttb, dst, tdst, rows, n):
        a, tka = ta.next()
        b, tkb = tb.next()
        scale_out(z, tz, r, tr, gcol, a[:rows, :n], tka, rows, n)
        scale_out(zs, tzs, r, tr, gcol + 1, b[:rows, :n], tkb, rows, n)
        P.op("pool", lambda e: e.tensor_tensor(a[:rows, :n], a[:rows, :n], cos[:rows, :n], ALU.mult), [tka, ttb], [tka])
        P.op("pool", lambda e: e.tensor_tensor(b[:rows, :n], b[:rows, :n], sin[:rows, :n], ALU.mult), [tkb, ttb], [tkb])
        P.op("pool", lambda e: e.tensor_tensor(dst, a[:rows, :n], b[:rows, :n], ALU.add), [tka, tkb], [tdst])

    def store(dst_dram, src, tsrc):
        P.dma(dst_dram, src, reads=[tsrc], dom=tsrc)

    def load_tabs(bi, t0, n):
        for j in range(4):
            P.dma(tabs[bi][j][:, :n], tabd[j][:, t0:t0 + n], writes=[ttab[bi]], dom=ttab[bi])

    M128, M64, M32, M256 = Ms[:, 0, :], Ms[:, 1, :], Ms[:, 2, :], Ms[:, 3, :]
    load_x(tiles[0][0], tiles[0][1])
    load_tabs(0, tiles[0][0], tiles[0][1])

    def tile_body(ti, t0, n, isctx):
        w = 1 if isctx else 0
        bi = ti % 2
        cm, sm, cw, sw = tabs[bi]
        ttb = ttab[bi]
        emit_prenorm(P, R, G, 0, w, 1, xn, txn, h, th, n)
        if ti + 1 < len(tiles):
            load_x(tiles[ti + 1][0], tiles[ti + 1][1])
            load_tabs(1 - bi, tiles[ti + 1][0], tiles[ti + 1][1])
        nsub = n // 128
        units = []

        def u_ckv():
            def A():
                return proj(0, n)

            def B(c):
                z, tz = c
                r, tr = rstd_of([(z, tz)], M128, 128, n)
                scale_out(z, tz, r, tr, 0, ckvn[:, :n], tckvn, 128, n)
            units.append((A, B))

        def u_rope(ch, chs, rows, M, gcol, cos, sin, dst_fn):
            def A():
                return proj(ch, n, rows), proj(chs, n, rows)

            def B(c):
                (z, tz), (zs, tzs) = c
                r, tr = rstd_of([(z, tz)], M, rows, n)
                o, to = outb.next()
                rope_out(z, tz, zs, tzs, r, tr, gcol, cos, sin, ttb, o[:rows, :n], to, rows, n)
                store(dst_fn(), o[:rows, :n], to)
            units.append((A, B))

        def u_cq():
            def A():
                return proj(5, n), proj(6, n)

            def B(c):
                (z0, tz0), (z1, tz1) = c
                r, tr = rstd_of([(z0, tz0), (z1, tz1)], M256, 128, n)
                scale_out(z0, tz0, r, tr, 5, cqn[:, 0, :n], tcqn, 128, n)
                scale_out(z1, tz1, r, tr, 6, cqn[:, 1, :n], tcqn, 128, n)
            units.append((A, B))

        def u_kn(j):
            def A():
                z, tz = pz.next()
                mm(P, z[:, :n], Wukv[:, j * 128:(j + 1) * 128], ckvn[:, :n], True, True, [tW, tckvn], [tz])
                return z, tz

            def B(c):
                z, tz = c
                r, tr = rstd_of([(z, tz)], M64, 128, n)
                o, to = outb.next()
                scale_out(z, tz, r, tr, 9, o[:, :n], to, 128, n)
                store(SC["KN"][j, :, t0:t0 + n], o[:, :n], to)
            units.append((A, B))

        def u_va(sub):
            def A():
                p, tp = pz.next()
                mm(P, p[:, :], ckvn[:, sub * 128:(sub + 1) * 128], Wukv[:, 512:1024], True, True, [tW, tckvn], [tp])
                return p, tp

            def B(c):
                p, tp = c
                v, tv = vout.next()
                P.op("act", lambda e: e.copy(v[:, :], p[:, :]), [tp], [tv])
                store(SC["VA"][t0 + sub * 128:t0 + (sub + 1) * 128, :], v[:, :], tv)
            units.append((A, B))

        def u_qn(j):
            def A():
                z, tz = pz.next()
                for k in range(2):
                    mm(P, z[:, :n], Wuq[:, k, j * 128:(j + 1) * 128], cqn[:, k, :n], k == 0, k == 1, [tW, tcqn], [tz])
                return z, tz

            def B(c):
                z, tz = c
                r, tr = rstd_of([(z, tz)], M64, 128, n)
                o, to = outb.next()
                scale_out(z, tz, r, tr, 10, o[:, :n], to, 128, n)
                store(SC["QN"][j, :, t0:t0 + n], o[:, :n], to)
            units.append((A, B))

        def u_qr(j):
            def A():
                z, tz = pz.next()
                for k in range(2):
                    mm(P, z[:, :n], Wuq[:, k, 512 + j * 128:512 + (j + 1) * 128], cqn[:, k, :n], k == 0, k == 1,
                       [tW, tcqn], [tz])
                zs, tzs = pz.next()
                for k in range(2):
                    mm(P, zs[:, :n], Wuq[:, k, 768 + j * 128:768 + (j + 1) * 128], cqn[:, k, :n], k == 0, k == 1,
                       [tW, tcqn], [tzs])
                return (z, tz), (zs, tzs)

            def B(c):
                (z, tz), (zs, tzs) = c
                r, tr = rstd_of([(z, tz)], M32, 128, n)
                o, to = outb.next()
                rope_out(z, tz, zs, tzs, r, tr, 11, cm, sm, ttb, o[:, :n], to, 128, n)
                store(SC["QR"][j, :, t0:t0 + n], o[:, :n], to)
            units.append((A, B))

        def u_vb(sub):
            def A():
                p, tp = pz.next()
                for k in range(8):
                    mm(P, p[:, :128], h[:, k, sub * 128:(sub + 1) * 128], W[:, k, 15 * 128:16 * 128], k == 0, k == 7,
                       [tW, th], [tp])
                return p, tp

            def B(c):
                p, tp = c
                v, tv = vout.next()
                P.op("act", lambda e: e.copy(v[:, :128], p[:, :128]), [tp], [tv])
                store(SC["VB"][t0 + sub * 128:t0 + (sub + 1) * 128, :], v[:, :128], tv)
            units.append((A, B))

        u_ckv()
        u_rope(1, 2, 32, M32, 1, cm, sm, lambda: SC["KR"][:, t0:t0 + n])
        u_rope(3, 4, 128, M64, 3, cw, sw, lambda: SC["KB"][:, t0:t0 + n])
        u_cq()
        for j in range(4):
            u_kn(j)
        for sub in range(nsub):
            u_va(sub)
        for j in range(4):
            u_qn(j)
        for j in range(2):
            u_qr(j)
        for j in range(4):
            u_rope(7 + j, 11 + j, 128, M64, 7, cw, sw, lambda j=j: SC["QB"][j, :, t0:t0 + n])
        for sub in range(nsub):
            u_vb(sub)
        pending = None
        for (A_, B_) in units:
            c = A_()
            if pending is not None:
                pending()
            pending = (lambda B_=B_, c=c: B_(c))
        pending()

    for ti, (t0, n, isctx) in enumerate(tiles):
        tile_body(ti, t0, n, isctx)
    P.emit()
    P.close()


def attn_finalize(P, A, po, tpo, n, dst, tdst, sink=None, pview=False):
    rsb, trsb = A["rsb"].next()
    if sink is not None:
        P.op("dve", lambda e: e.tensor_tensor(rsb[64:65, :n], po[64:65, :n], sink, ALU.add), [tpo, A["tcst"]], [trsb])
        P.op("dve", lambda e: e.reciprocal(rsb[64:65, :n], rsb[64:65, :n]), [trsb], [trsb])
    else:
        P.op("dve", lambda e: e.reciprocal(rsb[64:65, :n], po[64:65, :n]), [tpo], [trsb])
    pbc, tpbc = A["pbc"].next()
    mm(P, pbc[0:64, :n], A["onesf"][64:65, 0:64], rsb[64:65, :n], True, True, [trsb, A["tcst"]], [tpbc])
    bcs, tbcs = A["bcs"].next()
    P.op("act", lambda e: e.copy(bcs[0:64, :n], pbc[0:64, :n]), [tpbc], [tbcs])
    if pview:
        a0 = po[0:64, :n].rearrange("p (g q) -> p g q", g=4)
        a1 = bcs[0:64, :n].rearrange("p (g q) -> p g q", g=4)
    else:
        a0 = po[0:64, :n]
        a1 = bcs[0:64, :n]
    P.op("dve", lambda e: e.tensor_tensor(dst, a0, a1, ALU.mult), [tpo, tbcs], [tdst, tpo])


def attn_common(P, dr):
    A = {}
    A["rsb"] = Rot(P, "rsb", [128, 512], F32, 2)
    A["pbc"] = Rot(P, "pbc", [128, 512], F32, 1, psum=True)
    A["bcs"] = Rot(P, "bcs", [64, 512], F32, 2)
    A["onesf"] = P.sb("onesf", [128, 64], F32)
    A["tcst"] = P.tok("acst")
    P.dma(A["onesf"][:], dr["onesf"], writes=[A["tcst"]], dom=A["tcst"])
    return A


def mla_pass(nc, G, dr, SC, tiles, heads=range(8)):
    P = Pass(nc, "pa")
    A = attn_common(P, dr)
    NKT = T // 128
    Kt = [P.sb("Kt%d" % b, [96, T], BF16) for b in range(2)]
    Qt = [P.sb("Qt%d" % b, [96, T], BF16) for b in range(2)]
    Vh = [P.sb("Vh%d" % b, [128, NKT, 65], BF16) for b in range(2)]
    tKQV = [P.tok("kqv%d" % b) for b in range(2)]
    for b in range(2):
        P.op("pool", lambda e, b=b: e.memset(Vh[b][:, :, 64:65], 1.0), [], [tKQV[b]])
    pS = Rot(P, "pS", [128, 512], F32, 4, psum=True)
    pO = Rot(P, "pO", [128, 512], F32, 2, psum=True)
    Pm = Rot(P, "Pm", [128, 512], BF16, 4)
    ob = Rot(P, "ob", [64, 512], BF16, 2)
    scale = 96.0 ** -0.5
    VAv = SC["VA"].rearrange("(kt p) c -> p kt c", p=128)

    def load_head(hh, b):
        tk = tKQV[b]
        P.dma(Kt[b][0:64, :], SC["KN"][hh // 2, (hh % 2) * 64:(hh % 2) * 64 + 64, :], writes=[tk], dom=tk)
        P.dma(Kt[b][64:96, :], SC["KR"][:, :], writes=[tk], dom=tk)
        P.dma(Qt[b][0:64, :], SC["QN"][hh // 2, (hh % 2) * 64:(hh % 2) * 64 + 64, :], writes=[tk], dom=tk)
        P.dma(Qt[b][64:96, :], SC["QR"][hh // 4, (hh % 4) * 32:(hh % 4) * 32 + 32, :], writes=[tk], dom=tk)
        for c0 in range(0, NKT, 11):
            P.dma(Vh[b][:, c0:c0 + 11, 0:64], VAv[:, c0:c0 + 11, hh * 64:(hh + 1) * 64], writes=[tk], dom=tk)

    heads = list(heads)
    load_head(heads[0], 0)
    for hi, hh in enumerate(heads):
        b = hi % 2
        if hi + 1 < len(heads):
            load_head(heads[hi + 1], 1 - b)
        tk = tKQV[b]
        steps = []
        for (t0, n, isctx) in tiles:
            nkt = 2 if isctx else NKT
            for kt in range(nkt):
                steps.append((t0, n, kt, nkt))
        LAG = 2
        pend = []
        cur = {}

        def issue_S(t0, n, kt, nkt):
            ps, tps = pS.next()
            mm(P, ps[:, :n], Kt[b][:, kt * 128:(kt + 1) * 128], Qt[b][:, t0:t0 + n], True, True, [tk], [tps])
            pm, tpm = Pm.next()
            P.op("act", lambda e: e.activation(pm[:, :n], ps[:, :n], AF.Exp, scale=scale), [tps], [tpm, tps])
            return pm, tpm

        def issue_PV(t0, n, kt, nkt, pm, tpm):
            if kt == 0:
                cur["po"], cur["tpo"] = pO.next()
            po, tpo = cur["po"], cur["tpo"]
            mm(P, po[0:65, :n], Vh[b][:, kt, :], pm[:, :n], kt == 0, kt == nkt - 1, [tk, tpm], [tpo])
            if kt == nkt - 1:
                def fin(po=po, tpo=tpo, n=n, t0=t0):
                    o, to = ob.next()
                    attn_finalize(P, A, po, tpo, n, o[:, :n], to, None)
                    P.dma(SC["OT"][hh * 64:(hh + 1) * 64, t0:t0 + n], o[:, :n], reads=[to], dom=to)
                fin_q.append([6, fin])

        fin_q = []
        for si, st_ in enumerate(steps):
            pm, tpm = issue_S(*st_)
            pend.append((st_, pm, tpm))
            if len(pend) > LAG:
                s0, pm0, tpm0 = pend.pop(0)
                issue_PV(*s0, pm0, tpm0)
            for it in fin_q:
                it[0] -= 1
            while fin_q and fin_q[0][0] <= 0:
                fin_q.pop(0)[1]()
        while pend:
            s0, pm0, tpm0 = pend.pop(0)
            issue_PV(*s0, pm0, tpm0)
        while fin_q:
            fin_q.pop(0)[1]()
    P.emit()
    P.close()


def win_pass(nc, G, dr, SC):
    P = Pass(nc, "pw")
    A = attn_common(P, dr)
    tc = A["tcst"]
    masks = P.sb("masks", [128, 2, 512], BF16)
    P.dma(masks[:], dr["wmask"].rearrange("m p c -> p m c"), writes=[tc], dom=tc)
    identb = P.sb("identb", [128, 128], BF16)
    P.dma(identb[:], dr["ident"], writes=[tc], dom=tc)
    sink = P.sb("sink", [128, 1024], F32)
    P.dma(sink[64:65, :], dr["l0_sink"], writes=[tc], dom=tc)
    P.op("act", lambda e: e.activation(sink[64:65, :], sink[64:65, :], AF.Exp), [tc], [tc])
    Kc = P.sb("Kc", [64, 2, LC], BF16)
    Vc = P.sb("Vc", [128, 2, 2, 65], BF16)
    P.op("pool", lambda e: e.memset(Vc[:, :, :, 64:65], 1.0), [], [tc])
    VBv = SC["VB"].rearrange("(kt p) c -> p kt c", p=128)
    for nk in range(2):
        P.dma(Kc[:, nk, :], SC["KB"][nk * 64:(nk + 1) * 64, 0:LC], writes=[tc], dom=tc)
        P.dma(Vc[:, nk, :, 0:64], VBv[:, 0:2, nk * 64:(nk + 1) * 64], writes=[tc], dom=tc)
    Qg = Rot(P, "Qg", [64, 4, 512], BF16, 2)
    Kg = Rot(P, "Kg", [64, 6 * 128], BF16, 2)
    Vg = Rot(P, "Vg", [128, 6, 65], BF16, 2)
    for v, tv in zip(Vg.t, Vg.k):
        P.op("pool", lambda e, v=v: e.memset(v[:, :, 64:65], 1.0), [], [tv])
    pS = Rot(P, "pS", [128, 512], F32, 3, psum=True)
    pO = Rot(P, "pO", [128, 512], F32, 3, psum=True)
    Pm = Rot(P, "Pm", [128, 512], BF16, 3)
    obg = Rot(P, "obg", [64, 4, 512], BF16, 2)
    scale = 64.0 ** -0.5
    fin_q = []
    groups = [("ctx", 0)] + [("lat", g) for g in range(S // 512)]
    for nk in range(2):
        for kind, gi in groups:
            q, tq = Qg.next()
            if kind == "ctx":
                q0, nq = 0, LC
            else:
                q0, nq = LC + gi * 512, 512
            for g in range(4):
                hq = nk * 4 + g
                P.dma(q[:, g, :nq], SC["QB"][hq // 2, (hq % 2) * 64:(hq % 2) * 64 + 64, q0:q0 + nq], writes=[tq], dom=tq)
            if kind == "lat":
                blo = max(4 * gi - 1, 0)
                bhi = min(4 * gi + 4, S // 128 - 1)
                nb = bhi - blo + 1
                kg, tkg = Kg.next()
                vg, tvg = Vg.next()
                P.dma(kg[:, :nb * 128], SC["KB"][nk * 64:(nk + 1) * 64, LC + blo * 128:LC + (bhi + 1) * 128],
                      writes=[tkg], dom=tkg)
                P.dma(vg[:, :nb, 0:64], VBv[:, 2 + blo:2 + bhi + 1, nk * 64:(nk + 1) * 64], writes=[tvg], dom=tvg)
            og, tog = obg.next()
            for qb in range(nq // 128):
                keys = [("c", 0, None), ("c", 1, None)]
                if kind == "lat":
                    i = 4 * gi + qb
                    if i - 1 >= 0:
                        keys.append(("l", i - 1 - blo, 0))
                    keys.append(("l", i - blo, None))
                    if i + 1 <= S // 128 - 1:
                        keys.append(("l", i + 1 - blo, 1))
                po, tpo = pO.next()
                for ki, (kk, idx, mk) in enumerate(keys):
                    ps, tps = pS.next()
                    if kk == "c":
                        lhsT = Kc[:, nk, idx * 128:(idx + 1) * 128]
                        vv = Vc[:, nk, idx, :]
                        rd = [tc]
                    else:
                        lhsT = kg[:, idx * 128:(idx + 1) * 128]
                        vv = vg[:, idx, :]
                        rd = [tkg, tvg]
                    mm(P, ps[:, :].rearrange("p (g q) -> p g q", g=4), lhsT, q[:, :, qb * 128:(qb + 1) * 128],
                       True, mk is None, rd + [tq], [tps])
                    if mk is not None:
                        mm(P, ps[:, :], identb[:, :], masks[:, mk, :], False, True, [tc], [tps])
                    pm, tpm = Pm.next()
                    P.op("act", lambda e, pm=pm, ps=ps: e.activation(pm[:, :], ps[:, :], AF.Exp, scale=scale),
                         [tps], [tpm, tps])
                    mm(P, po[0:65, :], vv, pm[:, :], ki == 0, ki == len(keys) - 1, rd + [tpm], [tpo])
                while fin_q:
                    fin_q.pop(0)()

                def fin(po=po, tpo=tpo, og=og, tog=tog, qb=qb, nk=nk):
                    dst = og[:, :, qb * 128:(qb + 1) * 128]
                    attn_finalize(P, A, po, tpo, 512, dst, tog, sink=sink[64:65, nk * 512:(nk + 1) * 512], pview=True)
                fin_q.append(fin)
            while fin_q:
                fin_q.pop(0)()
            for g in range(4):
                hq = nk * 4 + g
                P.dma(SC["OT"][512 + hq * 64:512 + (hq + 1) * 64, q0:q0 + nq], og[:, g, :nq], reads=[tog], dom=tog)
    P.emit()
    P.close()


def out_pass(nc, G, name, Xin, Xout, OT, wout_d, l, tiles, ot_off=0):
    P = Pass(nc, name)
    Wo = P.sb("Wo", [128, 8, D], BF16)
    tW = P.tok("W")
    stg = [P.sb("stg%d" % j, [128, 1024], F32) for j in range(2)]
    tst = [P.tok("stg%d" % j) for j in range(2)]
    jobs = [(wout_d[k * 128:(k + 1) * 128, :], Wo[:, k, :], D) for k in range(8)]
    load_cast_weights(P, jobs, stg, tst, tW)
    ot = Rot(P, "ot", [128, 8, 512], BF16, 2)
    xr = Rot(P, "xr", [128, 512], F32, 4)
    py = Rot(P, "py", [128, 512], F32, 2, psum=True)

    def tile_body(t0, n, isctx):
        w = 1 if isctx else 0
        o, to = ot.next()
        for k in range(8):
            P.dma(o[:, k, :n], OT[k * 128:(k + 1) * 128, t0 - ot_off:t0 - ot_off + n], writes=[to], dom=to)
        for dc in range(8):
            x, tx = xr.next()
            P.dma(x[:, :n], Xin[dc * 128:(dc + 1) * 128, t0:t0 + n], writes=[tx], dom=tx)
            p, tp = py.next()
            for k in range(8):
                mm(P, p[:, :n], Wo[:, k, dc * 128:(dc + 1) * 128], o[:, k, :n], k == 0, k == 7, [tW, to], [tp])
            P.op("dve", lambda e, x=x, p=p, dc=dc: e.scalar_tensor_tensor(
                x[:, :n], p[:, :n], G["GT"][:, l, w, 1, dc:dc + 1], x[:, :n], ALU.mult, ALU.add),
                [tp, tx, G["tok"]], [tp, tx])
            P.dma(Xout[dc * 128:(dc + 1) * 128, t0:t0 + n], x[:, :n], reads=[tx], dom=tx)

    for (t0, n, isctx) in tiles:
        tile_body(t0, n, isctx)
    P.emit()
    P.close()


def prep1_pass(nc, G, dr, SC, Xin, tiles):
    P = Pass(nc, "p1")
    NCH = 25
    W = P.sb("W", [128, 8, NCH * 128], BF16)
    tW = P.tok("W")
    stg = [P.sb("stg%d" % j, [128, 1600], F32) for j in range(2)]
    tst = [P.tok("stg%d" % j) for j in range(2)]
    jobs = []
    for k in range(8):
        for hf in range(2):
            jobs.append((dr["l1_win"][k * 128:(k + 1) * 128, hf * 1600:(hf + 1) * 1600], W[:, k, hf * 1600:(hf + 1) * 1600], 1600))
    load_cast_weights(P, jobs, stg, tst, tW)
    tc = P.tok("cst")
    TRI = P.sb("TRI", [128, 2, 128], F32)
    ident = P.sb("ident", [128, 128], BF16)
    WG = P.sb("WG", [64, 512], F32)
    P.dma(TRI[:], dr["tri"].rearrange("m p c -> p m c"), writes=[tc], dom=tc)
    P.dma(ident[:], dr["ident"], writes=[tc], dom=tc)
    P.dma(WG[:], dr["l1_wg"], writes=[tc], dom=tc)
    LFB = P.sb("LFB", [64, 512], F32)
    tLFB = P.tok("LFB")
    P.op("dve", lambda e: e.memset(LFB[:], 1.0), [], [tLFB])
    R = alloc_norm_scratch(P)
    xn = P.sb("xn", [128, 8, 512], F32)
    txn = P.tok("xn")
    h = P.sb("h", [128, 8, 512], BF16)
    th = P.tok("h")
    qT = P.sb("qT", [128, 4, 512], F32)
    kT = P.sb("kT", [128, 4, 512], F32)
    tqT = P.tok("qT")
    tkT = P.tok("kT")
    og = Rot(P, "og", [128, 8, 512], BF16, 1)
    pz = Rot(P, "pz", [128, 512], F32, 3, psum=True)
    pv = pz
    pG = Rot(P, "pG", [128, 4, 128], F32, 2, psum=True)
    pT = Rot(P, "pT", [128, 512], BF16, 2, psum=True)
    ee = Rot(P, "ee", [128, 512], F32, 2)
    ll = Rot(P, "ll", [128, 512], F32, 3)
    E = Rot(P, "E", [128, 4, 128], F32, 2)
    Ei = Rot(P, "Ei", [128, 4, 128], F32, 2)
    QDs = [Rot(P, "QDs%d" % d, [128, 4, 512], BF16, 1) for d in range(2)]
    KIs = [Rot(P, "KIs%d" % d, [128, 4, 512], BF16, 1) for d in range(2)]
    KEs = Rot(P, "KEs", [128, 512], BF16, 2)
    keT = Rot(P, "keT", [128, 128], BF16, 12)
    Vs = Rot(P, "Vs", [128, 1024], BF16, 2)
    DECs = Rot(P, "DECs", [128, 2, 4, 8], F32, 2)
    qscale = 128.0 ** -0.5

    def load_x(t0, n):
        for k in range(8):
            P.dma(xn[:, k, :n], Xin[k * 128:(k + 1) * 128, t0:t0 + n], writes=[txn], dom=txn)

    def proj(chunk, n):
        z, tz = pz.next()
        for k in range(8):
            mm(P, z[:, :n], W[:, k, chunk * 128:(chunk + 1) * 128], h[:, k, :n], k == 0, k == 7, [tW, th], [tz])
        return z, tz

    load_x(tiles[0][0], tiles[0][1])

    def tile_body(ti, t0, n, isctx):
        w = 1 if isctx else 0
        emit_prenorm(P, R, G, 1, w, 1, xn, txn, h, th, n)
        if ti + 1 < len(tiles):
            load_x(tiles[ti + 1][0], tiles[ti + 1][1])
        nsub = n // 128
        c0 = t0 // 64
        for hd in range(4):
            z, tz = proj(hd, n)
            P.op("act", lambda e, z=z, hd=hd: e.copy(kT[:, hd, :n], z[:, :n]), [tz], [tkT, tz])
            z, tz = proj(4 + hd, n)
            P.op("act", lambda e, z=z, hd=hd: e.mul(qT[:, hd, :n], z[:, :n], qscale), [tz], [tqT, tz])
        z, tz = proj(16, n)
        P.op("act", lambda e, z=z: e.copy(LFB[0:16, :n], z[0:16, :n]), [tz], [tLFB])
        P.op("act", lambda e, z=z: e.copy(LFB[32:48, :n], z[32:48, :n]), [tz], [tLFB, tz])
        o_g, tog = og.next()
        for j in range(8):
            z, tz = proj(8 + j, n)
            P.op("act", lambda e, z=z, j=j: e.activation(o_g[:, j, :n], z[:, :n], AF.Silu), [tz], [tog, tz])
        for j in range(8):
            P.dma(SC["OG"][j * 128:(j + 1) * 128, t0:t0 + n], o_g[:, j, :n], reads=[tog], dom=tog)
        for sub in range(nsub):
            v, tv = Vs.next()
            for hf in range(2):
                p, tp = pv.next()
                for k in range(8):
                    mm(P, p[:, :], h[:, k, sub * 128:(sub + 1) * 128], W[:, k, (17 + 4 * hf) * 128:(21 + 4 * hf) * 128],
                       k == 0, k == 7, [tW, th], [tp])
                P.op("act", lambda e, v=v, p=p, hf=hf: e.copy(v[:, hf * 512:(hf + 1) * 512], p[:, :]), [tp], [tv, tp])
            P.dma(SC["V1"][t0 + sub * 128:t0 + (sub + 1) * 128, :], v[:, :], reads=[tv], dom=tv)
        dec, tdec = DECs.next()

        stg_bufs = {}
        for d in range(2):
            stg_bufs[d] = (QDs[d].next(), KIs[d].next())

        def st0(d, sub, c):
            r0 = 32 * d
            c["sl"] = slice(sub * 128, (sub + 1) * 128)
            c["p"], c["tp"] = pv.next()
            mm(P, c["p"][:, :], LFB[r0:r0 + 17, c["sl"]], WG[r0:r0 + 17, :], True, True, [tLFB, tc], [c["tp"]])

        def st1(d, sub, c):
            p, tp = c["p"], c["tp"]
            e1, te1 = ee.next()
            P.op("act", lambda e: e.activation(e1[:, :], p[:, :], AF.Exp, scale=-1.0), [tp], [te1, tp])
            c["l1"], c["tl1"] = ll.next()
            l1 = c["l1"]
            P.op("act", lambda e: e.activation(l1[:, :], e1[:, :], AF.Ln, bias=1.0), [te1], [c["tl1"]])

        def st2(d, sub, c):
            c["g"], c["tg"] = pG.next()
            for hd in range(4):
                mm(P, c["g"][:, hd, :], c["l1"][:, hd * 128:(hd + 1) * 128], TRI[:, d, :], True, True, [c["tl1"], tc], [c["tg"]])

        def st3(d, sub, c):
            (qd, tqd), (ki, tki) = stg_bufs[d]
            g, tg, sl_ = c["g"], c["tg"], c["sl"]
            Ex, tEx = E.next()
            Eix, tEix = Ei.next()
            P.op("act", lambda e: e.activation(Ex[:], g[:], AF.Exp), [tg], [tEx])
            P.op("act", lambda e: e.activation(Eix[:], g[:], AF.Exp, scale=-1.0), [tg], [tEix, tg])
            c["kts"] = []
            col0 = 63 if d == 0 else 0
            P.op("dve", lambda e: e.tensor_copy(dec[:, d, :, 2 * sub:2 * sub + 2], Ex[:, :, col0:col0 + 65:64]), [tEx], [tdec])
            for hd in range(4):
                P.op("dve", lambda e, hd=hd: e.tensor_tensor(qd[:, hd, sl_], qT[:, hd, sl_], Ex[:, hd, :], ALU.mult),
                     [tqT, tEx], [tqd])
                P.op("pool", lambda e, hd=hd: e.tensor_tensor(ki[:, hd, sl_], kT[:, hd, sl_], Eix[:, hd, :], ALU.mult),
                     [tkT, tEix], [tki])
                kt_, tkt_ = keT.next()
                for cc in range(2):
                    cs = slice(sub * 128 + 64 * cc, sub * 128 + 64 * (cc + 1))
                    P.op("dve", lambda e, hd=hd, cc=cc, cs=cs, kt_=kt_: e.scalar_tensor_tensor(
                        kt_[:, 64 * cc:64 * (cc + 1)], kT[:, hd, cs], dec[:, d, hd, 2 * sub + cc:2 * sub + cc + 1],
                        Eix[:, hd, 64 * cc:64 * (cc + 1)], ALU.mult, ALU.mult), [tkT, tdec, tEix], [tkt_])
                c["kts"].append((kt_, tkt_))

        def st4(d, sub, c):
            c["pt"], c["tpt"] = pT.next()
            pt = c["pt"]
            for hd in range(4):
                kt_, tkt_ = c["kts"][hd]
                P.op("pe", lambda e, hd=hd, kt_=kt_: e.transpose(pt[:, hd * 128:(hd + 1) * 128], kt_[:, :], ident[:]),
                     [tkt_, tc], [c["tpt"]])

        def st5(d, sub, c):
            ke_s, tke_s = KEs.next()
            pt, tpt = c["pt"], c["tpt"]
            P.op("act", lambda e: e.copy(ke_s[:, :], pt[:, :]), [tpt], [tke_s, tpt])
            P.dma(SC["KE"][d, t0 + sub * 128:t0 + (sub + 1) * 128, :], ke_s[:, :], reads=[tke_s], dom=tke_s)

        stages = [st0, st1, st2, st3, st4, st5]
        gunits = [(d, sub, {}) for d in range(2) for sub in range(nsub)]
        for t in range(len(gunits) + len(stages) - 1):
            for k in range(len(stages) - 1, -1, -1):
                u = t - k
                if 0 <= u < len(gunits):
                    stages[k](*gunits[u])
        for d in range(2):
            (qd, tqd), (ki, tki) = stg_bufs[d]
            for hd in range(4):
                P.dma(SC["QD"][d, hd, :, t0:t0 + n], qd[:, hd, :n], reads=[tqd], dom=tqd)
                P.dma(SC["KI"][d, hd, :, t0:t0 + n], ki[:, hd, :n], reads=[tki], dom=tki)
        nch = n // 64
        for d in range(2):
            P.dma(SC["DEC"][d, :, :, c0:c0 + nch], dec[:, d, :, :nch], reads=[tdec], dom=tdec)

    for ti, (t0, n, isctx) in enumerate(tiles):
        tile_body(ti, t0, n, isctx)
    P.emit()
    P.close()


def scan_pass(nc, G, dr, SC, d):
    P = Pass(nc, "s%d" % d)
    tc = P.tok("cst")
    MASK = P.sb("MASK", [128, 4, 128], F32)
    P.dma(MASK[:], dr["gmask"][d], writes=[tc], dom=tc)
    DEC = P.sb("DEC", [128, 4, T // 64], F32)
    P.dma(DEC[:], SC["DEC"][d], writes=[tc], dom=tc)
    M256 = P.sb("M256", [128, 128], BF16)
    P.dma(M256[:], dr["Ms"][3], writes=[tc], dom=tc)
    gn = P.sb("gn", [128, 2], F32)
    P.dma(gn[:], dr["l1_gn"], writes=[tc], dom=tc)
    S32 = P.sb("S32", [128, 4, 256], F32)
    S16 = P.sb("S16", [128, 4, 256], BF16)
    tS32h = [P.tok("S32_%d" % j) for j in range(4)]
    tS16 = P.tok("S16")
    P.op("dve", lambda e: e.memset(S32[:], 0.0), [], tS32h)
    P.op("pool", lambda e: e.memset(S16[:], 0.0), [], [tS16])
    QD = Rot(P, "QD", [128, 4, 512], BF16, 2)
    KI = Rot(P, "KI", [128, 4, 512], BF16, 2)
    KE = Rot(P, "KE", [128, 4, 512], BF16, 2)
    V = Rot(P, "V", [128, 4, 1024], BF16, 2)
    OB = Rot(P, "OB", [128, 8, 512], F32, 2)
    OGt = Rot(P, "OGt", [128, 8, 512], BF16, 2)
    O1 = Rot(P, "O1", [128, 8, 512], BF16, 2)
    Am = Rot(P, "Am", [128, 4, 128], BF16, 2)
    osum = Rot(P, "osum", [128, 8, 128], F32, 2)
    sqs = Rot(P, "sqs", [128, 8, 128], BF16, 2)
    rs = Rot(P, "rs", [128, 4, 128], F32, 2)
    rt = Rot(P, "rt", [128, 4, 128], F32, 2)
    o1f = Rot(P, "o1f", [128, 8, 128], F32, 2)
    pA = Rot(P, "pA", [128, 4, 128], F32, 2, psum=True)
    pO = Rot(P, "pO", [128, 8, 128], F32, 1, psum=True)
    pKV = Rot(P, "pKV", [128, 4, 256], F32, 1, psum=True)
    pss = Rot(P, "pss", [128, 4, 128], F32, 1, psum=True)
    KEv = SC["KE"][d].rearrange("(s p) c -> p s c", p=128)
    V1v = SC["V1"].rearrange("(s p) c -> p s c", p=128)
    blocks = [(0, LC, True)] + [(LC + i * 512, 512, False) for i in range(S // 512)]
    if d == 1:
        blocks = [blocks[0]] + blocks[1:][::-1]

    def state_update(ke, tke, v, tv, sub, c, chunk_id):
        p, tp = pKV.next()
        rows = slice(64 * c, 64 * (c + 1))
        for hd in range(4):
            mm(P, p[:, hd, :], ke[rows, sub, hd * 128:(hd + 1) * 128], v[rows, sub, hd * 256:(hd + 1) * 256],
               True, True, [tke, tv], [tp])
        for hd in range(4):
            P.op("dve", lambda e, hd=hd, p=p: e.scalar_tensor_tensor(
                S32[:, hd, :], S32[:, hd, :], DEC[:, hd, chunk_id:chunk_id + 1], p[:, hd, :], ALU.mult, ALU.add),
                [tp, tS32h[hd], tc], [tS32h[hd], tp])
        P.op("act", lambda e: e.copy(S16[:], S32[:]), tS32h, [tS16])

    comb_q = []

    def combine(os_, tos, sl_, o1, to1, ogt, togt):
        sq, tsq = sqs.next()
        P.op("act", lambda e: e.activation(sq[:], os_[:], AF.Square), [tos], [tsq])
        ps_, tps = pss.next()
        for hd in range(4):
            for dvc in range(2):
                mm(P, ps_[:, hd, :], M256[:], sq[:, hd * 2 + dvc, :], dvc == 0, dvc == 1, [tsq, tc], [tps])
        r, tr = rs.next()
        tm, ttm = rt.next()
        emit_rsqrt(P, r[:], ps_[:], tm[:], tps, tr, ttm)
        of, tof = o1f.next()
        for hd in range(4):
            for dvc in range(2):
                P.op("dve", lambda e, hd=hd, dvc=dvc: e.scalar_tensor_tensor(
                    of[:, hd * 2 + dvc, :], os_[:, hd * 2 + dvc, :], gn[:, dvc:dvc + 1], r[:, hd, :],
                    ALU.mult, ALU.mult), [tos, tr, tc], [tof])
        P.op("pool", lambda e: e.tensor_tensor(o1[:, :, sl_], of[:], ogt[:, :, sl_], ALU.mult), [tof, togt], [to1])

    for (t0, nb, isctx) in blocks:
        nsub = nb // 128
        ke, tke = KE.next()
        v, tv = V.next()
        s0 = t0 // 128
        P.dma(ke[:, :nsub, :], KEv[:, s0:s0 + nsub, :], writes=[tke], dom=tke)
        for sub in range(nsub):
            P.dma(v[:, sub, :], V1v[:, s0 + sub, :], writes=[tv], dom=tv)
        subs = list(range(nsub))
        corder = [0, 1]
        if d == 1:
            subs = subs[::-1]
            corder = [1, 0]
        if isctx:
            for sub in subs:
                for c in corder:
                    state_update(ke, tke, v, tv, sub, c, (t0 + sub * 128) // 64 + c)
            continue
        qd, tqd = QD.next()
        ki, tki = KI.next()
        for hd in range(4):
            P.dma(qd[:, hd, :], SC["QD"][d, hd, :, t0:t0 + nb], writes=[tqd], dom=tqd)
            P.dma(ki[:, hd, :], SC["KI"][d, hd, :, t0:t0 + nb], writes=[tki], dom=tki)
        if d == 1:
            ob, tob = OB.next()
        else:
            ob, tob = OB.next()
            ogt, togt = OGt.next()
            o1, to1 = O1.next()
            for j in range(8):
                P.dma(ob[:, j, :], SC["OBF"][j * 128:(j + 1) * 128, t0 - LC:t0 - LC + nb], writes=[tob], dom=tob)
                P.dma(ogt[:, j, :], SC["OG"][j * 128:(j + 1) * 128, t0:t0 + nb], writes=[togt], dom=togt)
        for sub in subs:
            sl_ = slice(sub * 128, (sub + 1) * 128)
            a, ta = pA.next()
            for hd in range(4):
                mm(P, a[:, hd, :], ki[:, hd, sl_], qd[:, hd, sl_], True, True, [tki, tqd], [ta])
            am, tam = Am.next()
            P.op("dve", lambda e, am=am, a=a: e.tensor_tensor(am[:], a[:], MASK[:], ALU.mult), [ta, tc], [tam, ta])
            po, tpo = pO.next()
            for hd in range(4):
                for dvc in range(2):
                    mm(P, po[:, hd * 2 + dvc, :], v[:, sub, hd * 256 + dvc * 128:hd * 256 + (dvc + 1) * 128], am[:, hd, :],
                       (hd * 2 + dvc) % 4 == 0, False, [tv, tam], [tpo], sgc=True)
            for ci, c in enumerate(corder):
                cs = slice(sub * 128 + 64 * c, sub * 128 + 64 * (c + 1))
                for hd in range(4):
                    for dvc in range(2):
                        mm(P, po[:, hd * 2 + dvc, 64 * c:64 * (c + 1)], S16[:, hd, dvc * 128:(dvc + 1) * 128], qd[:, hd, cs],
                           False, True, [tS16, tqd], [tpo], sgc=True)
                state_update(ke, tke, v, tv, sub, c, (t0 + sub * 128) // 64 + c)
            if d == 1:
                for hb in range(2):
                    P.op("act", lambda e, po=po, ob=ob, sl_=sl_, hb=hb: e.copy(ob[:, 4 * hb:4 * hb + 4, sl_], po[:, 4 * hb:4 * hb + 4, :]),
                         [tpo], [tob, tpo])
            else:
                os_, tos = osum.next()
                for hb in range(2):
                    P.op("dve", lambda e, os_=os_, po=po, ob=ob, sl_=sl_, hb=hb: e.tensor_tensor(
                        os_[:, 4 * hb:4 * hb + 4, :], po[:, 4 * hb:4 * hb + 4, :], ob[:, 4 * hb:4 * hb + 4, sl_], ALU.add),
                        [tpo, tob], [tos, tpo])
                while comb_q:
                    comb_q.pop(0)()

                def comb(os_=os_, tos=tos, sl_=sl_, o1=o1, to1=to1, ogt=ogt, togt=togt):
                    combine(os_, tos, sl_, o1, to1, ogt, togt)
                comb_q.append(comb)
        while comb_q:
            comb_q.pop(0)()
        if d == 1:
            for j in range(8):
                P.dma(SC["OBF"][j * 128:(j + 1) * 128, t0 - LC:t0 - LC + nb], ob[:, j, :], reads=[tob], dom=tob)
        else:
            for j in range(8):
                P.dma(SC["OT"][j * 128:(j + 1) * 128, t0:t0 + nb], o1[:, j, :], reads=[to1], dom=to1)
    P.emit()
    P.close()


def build_program(upto=99, dumps=()):
    nc = bass.Bass("TRN2", target_bir_lowering=False)
    dr = {}

    def din(name, shape, dt=F32):
        dr[name] = nc.dram_tensor(name, list(shape), dt, kind="ExternalInput").ap()

    din("xT", [D, T])
    din("c8", [128, 8])
    din("cc8", [128, 8])
    for l in range(2):
        din("l%d_wmod" % l, [D, 9 * D])
        din("l%d_bmod" % l, [128, 72])
        din("l%d_ng" % l, [128, 24])
        for f in (1, 2):
            din("l%d_ffn%d_wgu" % (l, f), [D, 2 * DFF])
            din("l%d_ffn%d_wd" % (l, f), [DFF, D])
    din("ones1024", [128, 128], BF16)
    din("l0_win", [D, 2048])
    din("l0_wuq", [256, 1024])
    din("l0_wukv", [128, 1024])
    din("l0_gv", [128, 16])
    din("l0_wout", [D, D])
    din("l0_sink", [1, 1024])
    din("Ms", [4, 128, 128], BF16)
    din("wmask", [2, 128, 512], BF16)
    din("onesf", [128, 64])
    for nm in ("cosm", "sinm", "cosw", "sinw"):
        din(nm, [128, T])
    din("l1_win", [D, 3200])
    din("l1_wg", [64, 512])
    din("l1_gn", [128, 2])
    din("l1_wout", [D, D])
    din("tri", [2, 128, 128])
    din("gmask", [2, 128, 4, 128])
    din("ident", [128, 128], BF16)
    SC = {}
    for nm, shp in (("QN", [4, 128, T]), ("QR", [2, 128, T]), ("KN", [4, 128, T]), ("KR", [32, T]), ("KB", [128, T]),
                    ("QB", [4, 128, T]), ("VA", [T, 512]), ("VB", [T, 128]), ("OT", [D, T])):
        SC[nm] = nc.dram_tensor("sc_" + nm, shp, BF16, kind="Internal").ap()
    for nm, shp, dt in (("QD", [2, 4, 128, T], BF16), ("KI", [2, 4, 128, T], BF16), ("KE", [2, T, 512], BF16),
                        ("V1", [T, 1024], BF16), ("OG", [D, T], BF16), ("DEC", [2, 128, 4, T // 64], F32),
                        ("OBF", [D, S], F32)):
        SC[nm] = nc.dram_tensor("sc_" + nm, shp, dt, kind="Internal").ap()
    out = nc.dram_tensor("outT", [D, S], F32, kind="ExternalOutput").ap()
    XA = nc.dram_tensor("XA", [D, T], F32, kind="Internal").ap()
    XB = nc.dram_tensor("XB", [D, T], F32, kind="Internal").ap()
    dump_aps = {}
    sc_dumps = {}
    for nm, shape in dumps:
        if nm in SC:
            sc_dumps[nm] = nc.dram_tensor("dump_" + nm, list(SC[nm].shape), SC[nm].dtype, kind="ExternalOutput").ap()
        else:
            dump_aps[nm] = nc.dram_tensor("dump_" + nm, list(shape), F32, kind="ExternalOutput").ap()

    with ExitStack() as ges:
        SEMPOOL[0] = SemPool(nc, ges, 96)
        G = {}
        G["MOD"] = ges.enter_context(nc.sbuf_tensor("gMOD", [128, 2, 2, 72], F32))
        G["A"] = ges.enter_context(nc.sbuf_tensor("gA", [128, 2, 2, 3, 8], F32))
        G["B"] = ges.enter_context(nc.sbuf_tensor("gB", [128, 2, 2, 3, 8], F32))
        G["GT"] = ges.enter_context(nc.sbuf_tensor("gGT", [128, 2, 2, 3, 8], F32))
        G["ones1024"] = ges.enter_context(nc.sbuf_tensor("gones", [128, 128], BF16))
        G["tok"] = Tok("G")
        G["tokc"] = Tok("Gc")
        P = Pass(nc, "pc")
        P.dma(G["ones1024"][:], dr["ones1024"], writes=[G["tokc"]], dom=G["tokc"])
        P.emit()
        P.close()
        mod_pass(nc, G, dr)
        tiles = token_tiles()
        final_src = XA
        if upto >= 1:
            ffn_pass(nc, G, "f1", dr["xT"], XA, dr["l0_ffn1_wgu"], dr["l0_ffn1_wd"], 0, 0, tiles)
        if upto >= 2:
            prep0_pass(nc, G, dr, SC, XA, tiles)
        if upto >= 3:
            mla_pass(nc, G, dr, SC, tiles)
            win_pass(nc, G, dr, SC)
        if upto >= 4:
            out_pass(nc, G, "o0", XA, XB, SC["OT"], dr["l0_wout"], 0, tiles)
            final_src = XB
        if upto >= 5:
            ffn_pass(nc, G, "f2", XB, XA, dr["l0_ffn2_wgu"], dr["l0_ffn2_wd"], 0, 2, tiles)
            final_src = XA
        lat_tiles = [t for t in tiles if not t[2]]
        if upto >= 6:
            ffn_pass(nc, G, "f3", XA, XB, dr["l1_ffn1_wgu"], dr["l1_ffn1_wd"], 1, 0, tiles)
            final_src = XB
        if upto >= 7:
            prep1_pass(nc, G, dr, SC, XB, tiles)
        if upto >= 8:
            scan_pass(nc, G, dr, SC, 1)
            scan_pass(nc, G, dr, SC, 0)
        if upto >= 9:
            out_pass(nc, G, "o1", XB, XA, SC["OT"], dr["l1_wout"], 1, lat_tiles)
            final_src = XA
        if upto >= 10:
            ffn_pass(nc, G, "f4", XA, out, dr["l1_ffn2_wgu"], dr["l1_ffn2_wd"], 1, 2, lat_tiles, xout_off=LC)
            final_src = None
        P = Pass(nc, "pd")
        tdd = P.tok("dd")
        for nm, ap in sc_dumps.items():
            src = SC[nm]
            if len(src.shape) == 2:
                for r0 in range(0, src.shape[0], 128):
                    P.dma(ap[r0:r0 + 128, :], src[r0:r0 + 128, :], dom=tdd)
            elif len(src.shape) == 3:
                for a in range(src.shape[0]):
                    for r0 in range(0, src.shape[1], 128):
                        P.dma(ap[a, r0:r0 + 128, :], src[a, r0:r0 + 128, :], dom=tdd)
            else:
                for a in range(src.shape[0]):
                    for b2 in range(src.shape[1]):
                        P.dma(ap[a, b2], src[a, b2], dom=tdd)
        cp = [P.sb("cp%d" % j, [128, 2048], F32) for j in range(2)]
        tcp = [P.tok("cp%d" % j) for j in range(2)]
        srcs = []
        if "MOD" in dump_aps:
            P.dma(dump_aps["MOD"], G["MOD"][:].rearrange("p a b c -> p (a b c)"),
                  reads=[G["tok"]], dom=tcp[0])
        jj = 0
        for nm, ap in list(dump_aps.items()) + [("__out", out)]:
            if nm == "MOD":
                continue
            src = {"XA": XA, "XB": XB}.get(nm, None)
            if src is None and nm != "__out":
                continue
            col0 = 0
            if nm == "__out":
                if final_src is None:
                    continue
                src = final_src
                col0 = LC
            ncol = ap.shape[1]
            for k in range(8):
                for c0 in range(0, ncol, 2048):
                    b = jj % 2
                    jj += 1
                    wd = min(2048, ncol - c0)
                    P.dma(cp[b][:, :wd], src[k * 128:(k + 1) * 128, col0 + c0:col0 + c0 + wd], writes=[tcp[b]], dom=tcp[b])
                    P.dma(ap[k * 128:(k + 1) * 128, c0:c0 + wd], cp[b][:, :wd], reads=[tcp[b]], dom=tcp[b])
        P.emit()
        P.close()
    return nc


def host_inputs(inp, b):
    f = np.float32
    m = {}
    xT = np.concatenate([inp["ctx"][b], inp["x"][b]], axis=0).T
    m["xT"] = np.ascontiguousarray(xT, dtype=f)
    m["c8"] = np.ascontiguousarray(inp["c"][b].reshape(8, 128).T, dtype=f)
    m["cc8"] = np.ascontiguousarray(inp["c_ctx"].reshape(8, 128).T, dtype=f)
    for l in range(2):
        p = "l%d_" % l
        m[p + "wmod"] = np.ascontiguousarray(inp[p + "w_mod"], dtype=f)
        m[p + "bmod"] = np.ascontiguousarray(inp[p + "b_mod"].reshape(72, 128).T, dtype=f)
        m[p + "ng"] = np.ascontiguousarray(inp[p + "norm_g"].reshape(24, 128).T, dtype=f)
        for k in (1, 2):
            m[p + "ffn%d_wgu" % k] = np.ascontiguousarray(inp[p + "ffn%d_w_gu" % k], dtype=f)
            m[p + "ffn%d_wd" % k] = np.ascontiguousarray(inp[p + "ffn%d_w_down" % k], dtype=f)
    m["ones1024"] = np.full((128, 128), 1.0 / 1024, dtype=ml_dtypes.bfloat16)
    m.update(host_consts())
    m.update(host_l0(inp))
    m.update(host_l1(inp))
    return m


def host_l1(inp):
    f = np.float32
    m = {}
    w = inp["l1_w_in"].astype(f)
    k, v, lf, lb, q, g = w[:, 0:512], w[:, 512:1536], w[:, 1536:1552], w[:, 1552:1568], w[:, 1568:2080], w[:, 2080:3104]
    z16 = np.zeros((D, 16), f)
    z80 = np.zeros((D, 80), f)
    m["l1_win"] = np.ascontiguousarray(np.concatenate([k, q, g, lf, z16, lb, z80, v], axis=1))
    assert m["l1_win"].shape == (D, 3200)
    wg = np.zeros((64, 512), f)
    wg[0:16] = inp["l1_w_gk_f"]
    wg[16] = inp["l1_b_gk_f"]
    wg[32:48] = inp["l1_w_gk_b"]
    wg[48] = inp["l1_b_gk_b"]
    m["l1_wg"] = wg
    m["l1_gn"] = np.ascontiguousarray(inp["l1_g_norm"].astype(f).reshape(2, 128).T)
    m["l1_wout"] = np.ascontiguousarray(inp["l1_w_out"], dtype=f)
    return m


def _swap(a):
    return a[:, np.arange(a.shape[1]) ^ 1]


def rope_tables(rot_dim, rep):
    n_freq = rot_dim // 4
    inv = (np.float32(10000.0) ** (-np.arange(n_freq, dtype=np.float32) / np.float32(n_freq))).astype(np.float32)
    t = np.arange(S)
    row = (t // 64).astype(np.float32)
    col = (t % 64).astype(np.float32)
    ang = np.concatenate([row[:, None] * inv, col[:, None] * inv], axis=-1).astype(np.float32)
    cos = np.cos(ang).astype(np.float32)
    sin = np.sin(ang).astype(np.float32)
    d = np.arange(rot_dim)
    C = np.ones((rot_dim, T), np.float32)
    Sg = np.zeros((rot_dim, T), np.float32)
    C[:, LC:] = cos[:, d // 2].T
    sign = np.where(d % 2 == 0, -1.0, 1.0).astype(np.float32)
    Sg[:, LC:] = sin[:, d // 2].T * sign[:, None]
    return np.ascontiguousarray(np.tile(C, (rep, 1))), np.ascontiguousarray(np.tile(Sg, (rep, 1)))


_CONSTS = {}


def host_consts():
    if _CONSTS:
        return _CONSTS
    bf = ml_dtypes.bfloat16
    m = {}
    Ms = np.zeros((4, 128, 128), np.float32)
    Ms[0] = 1.0 / 128
    for b in range(2):
        Ms[1, b * 64:(b + 1) * 64, b * 64:(b + 1) * 64] = 1.0 / 64
    for b in range(4):
        Ms[2, b * 32:(b + 1) * 32, b * 32:(b + 1) * 32] = 1.0 / 32
    Ms[3] = 1.0 / 256
    m["Ms"] = Ms.astype(bf)
    kk = np.arange(128)[:, None]
    qq = np.arange(128)[None, :]
    wm = np.stack([np.tile((kk >= qq).astype(np.float32), (1, 4)), np.tile((kk <= qq).astype(np.float32), (1, 4))])
    wm = (1.0 - wm) * np.float32(-30000.0)
    m["wmask"] = wm.astype(bf)
    m["onesf"] = np.ones((128, 64), np.float32)
    sidx = np.arange(128)[:, None]
    tidx = np.arange(128)[None, :]
    same = (sidx // 64) == (tidx // 64)
    lo = (same & (sidx <= tidx)).astype(np.float32)
    hi = (same & (sidx >= tidx)).astype(np.float32)
    m["tri"] = np.stack([lo, hi]) * np.float32(-1.0 / 16.0)
    m["gmask"] = np.ascontiguousarray(np.stack([np.repeat(lo[:, None, :], 4, axis=1), np.repeat(hi[:, None, :], 4, axis=1)]))
    m["ident"] = np.eye(128, dtype=np.float32).astype(bf)
    m["cosm"], m["sinm"] = rope_tables(32, 4)
    m["cosw"], m["sinw"] = rope_tables(64, 2)
    _CONSTS.update(m)
    return _CONSTS


def host_l0(inp):
    f = np.float32
    m = {}
    w = inp["l0_w_in"].astype(f)
    ckv, kr, wk, wv, cq, wq = w[:, 0:128], w[:, 128:160], w[:, 160:288], w[:, 288:416], w[:, 416:672], w[:, 672:1184]
    pad = np.zeros((D, 96), f)
    m["l0_win"] = np.ascontiguousarray(np.concatenate(
        [ckv, kr, pad, _swap(kr), pad, wk, _swap(wk), cq, wq, _swap(wq), wv], axis=1))
    assert m["l0_win"].shape == (D, 2048)
    uq = inp["l0_mla_w_uq"].astype(f).reshape(256, 8, 96)
    nope = uq[:, :, :64].reshape(256, 512)
    rope = uq[:, :, 64:].reshape(256, 256)
    m["l0_wuq"] = np.ascontiguousarray(np.concatenate([nope, rope, _swap(rope)], axis=1))
    ukv = inp["l0_mla_w_ukv"].astype(f).reshape(128, 8, 128)
    m["l0_wukv"] = np.ascontiguousarray(np.concatenate([ukv[:, :, :64].reshape(128, 512), ukv[:, :, 64:].reshape(128, 512)], axis=1))
    gv = np.ones((128, 16), f)
    sw = lambda g: g[np.arange(g.shape[0]) ^ 1]
    gv[:, 0] = inp["l0_mla_g_kva"]
    gv[:32, 1] = inp["l0_mla_g_kr"]
    gv[:32, 2] = sw(inp["l0_mla_g_kr"])
    gv[:, 3] = np.tile(inp["l0_win_g_k"], 2)
    gv[:, 4] = np.tile(sw(inp["l0_win_g_k"]), 2)
    gv[:, 5] = inp["l0_mla_g_qa"][:128]
    gv[:, 6] = inp["l0_mla_g_qa"][128:]
    gv[:, 7] = np.tile(inp["l0_win_g_q"], 2)
    gv[:, 8] = np.tile(sw(inp["l0_win_g_q"]), 2)
    gv[:, 9] = np.tile(inp["l0_mla_g_kn"], 2)
    gv[:, 10] = np.tile(inp["l0_mla_g_qn"], 2)
    gv[:, 11] = np.tile(inp["l0_mla_g_qr"], 4)
    gv[:, 12] = np.tile(sw(inp["l0_mla_g_qr"]), 4)
    m["l0_gv"] = gv
    m["l0_wout"] = np.ascontiguousarray(inp["l0_w_out"], dtype=f)
    m["l0_sink"] = np.ascontiguousarray(np.repeat(inp["l0_win_sink"].astype(f).reshape(8), 128).reshape(1, 1024))
    return m


_INPUT_NAMES = (
    "x", "c", "ctx", "c_ctx",
    "l0_norm_g", "l0_w_mod", "l0_b_mod", "l0_ffn1_w_gu", "l0_ffn1_w_down", "l0_ffn2_w_gu", "l0_ffn2_w_down",
    "l0_w_in", "l0_mla_g_qa", "l0_mla_g_kva", "l0_mla_w_uq", "l0_mla_w_ukv", "l0_mla_g_qn", "l0_mla_g_qr",
    "l0_mla_g_kn", "l0_mla_g_kr", "l0_win_g_q", "l0_win_g_k", "l0_win_sink", "l0_w_out",
    "l1_norm_g", "l1_w_mod", "l1_b_mod", "l1_ffn1_w_gu", "l1_ffn1_w_down", "l1_ffn2_w_gu", "l1_ffn2_w_down",
    "l1_w_in", "l1_w_gk_f", "l1_b_gk_f", "l1_w_gk_b", "l1_b_gk_b", "l1_g_norm", "l1_w_out",
)


def kernel(**inp):
    inp = {k: np.asarray(inp[k]) for k in _INPUT_NAMES}
    nc = build_program()
    in_maps = [host_inputs(inp, b) for b in range(NCORES)]
    res = run_bass_kernel_spmd(nc, in_maps, core_ids=list(range(NCORES)))
    outs = [np.asarray(res.results[b]["outT"]).T for b in range(NCORES)]
    return np.ascontiguousarray(np.stack(outs, axis=0).astype(np.float32))
```

```python
import numpy as np
from contextlib import ExitStack
import ml_dtypes
import concourse.bass as bass
import concourse.mybir as mybir
from concourse.bass_utils import run_bass_kernel_spmd

F32 = mybir.dt.float32
BF16 = mybir.dt.bfloat16
AF = mybir.ActivationFunctionType
ALU = mybir.AluOpType

D = 1024
DFF = 2816
NF = DFF // 128
LC = 256
S = 8192
T = LC + S
EPS = 1e-6
NCORES = 8

ENGS = ("pe", "act", "dve", "pool", "sp")


class Tok:
    __slots__ = ("name", "lw", "rd", "dom")

    def __init__(self, name):
        self.name = name
        self.lw = None
        self.rd = {}
        self.dom = None


class Op:
    __slots__ = ("eng", "fn", "deps", "inc", "dom", "val", "sem", "pas", "isdma")


class SemPool:
    def __init__(self, nc, es, n):
        self.items = [[es.enter_context(nc.semaphore("sp%d" % i)), 0, False] for i in range(n)]

    def acquire(self):
        free = [it for it in self.items if not it[2]]
        it = min(free, key=lambda x: x[1])
        it[2] = True
        return it


SEMPOOL = [None]


class Pass:
    def __init__(self, nc, name):
        self.nc = nc
        self.sems = []
        self.name = name
        self.es = ExitStack()
        self.streams = {e: [] for e in ENGS}
        self.doms = {}
        self.last = {}
        self.nalloc = 0

    def sb(self, name, shape, dt):
        return self.es.enter_context(self.nc.sbuf_tensor(self.name + "_" + name, list(shape), dt))

    def ps(self, name, shape, dt=F32):
        return self.es.enter_context(self.nc.psum_tensor(self.name + "_" + name, list(shape), dt))

    def tok(self, name):
        return Tok(name)

    def _record(self, o, reads, writes):
        deps = set()
        for t in reads:
            if t.lw is not None:
                deps.add(t.lw)
        for t in writes:
            if t.lw is not None:
                deps.add(t.lw)
            for r in t.rd.values():
                deps.add(r)
        deps.discard(o)
        deps = [d for d in deps if d.pas is self]
        for d in deps:
            d.inc = True
        o.deps = deps
        for t in reads:
            if t not in writes:
                t.rd[o.dom] = o
        for t in writes:
            t.lw = o
            t.rd = {}
        self.streams[o.eng].append(o)
        self.last[o.dom] = o

    def op(self, eng, fn, reads=(), writes=()):
        o = Op()
        o.eng = eng
        o.fn = fn
        o.inc = False
        o.dom = eng
        o.pas = self
        o.isdma = False
        o.val = None
        o.sem = None
        self._record(o, list(reads), list(writes))
        return o

    def dma(self, out, in_, reads=(), writes=(), dom=None, q="sp"):
        assert dom is not None
        if dom.dom is None or dom.dom[0] is not self:
            it = SEMPOOL[0].acquire()
            self.sems.append(it)
            sem = it
            dom.dom = (self, sem, [0], "dma%d" % len(self.doms))
            self.doms[dom.dom[3]] = dom.dom
        o = Op()
        o.eng = q
        o.fn = lambda e, out=out, in_=in_: e.dma_start(out=out, in_=in_)
        o.inc = True
        o.dom = dom.dom[3]
        o.pas = self
        o.isdma = True
        dom.dom[2][0] += 16
        o.val = dom.dom[2][0]
        o.sem = dom.dom[1]
        self._record(o, list(reads), list(writes))
        return o

    def emit(self):
        nc = self.nc
        engsem = {}
        for e in ("pe", "act", "dve", "pool"):
            engsem[e] = SEMPOOL[0].acquire()
            self.sems.append(engsem[e])
        for e in ("pe", "act", "dve", "pool", "sp"):
            cnt = 0
            for o in self.streams[e]:
                if o.isdma:
                    continue
                if o.fn is None:
                    continue
                if o.inc:
                    cnt += 1
                    o.val = cnt
                    o.sem = engsem[e]
        finals = []
        for dom, o in self.last.items():
            if o.fn is None:
                continue
            if not o.isdma and not o.inc:
                o.inc = True
            finals.append(o)
        for e in ("pe", "act", "dve", "pool", "sp"):
            cnt = 0
            for o in self.streams[e]:
                if o.isdma or o.fn is None:
                    continue
                if o.inc:
                    cnt += 1
                    o.val = cnt
                    o.sem = engsem[e]
        handles = {"pe": nc.tensor, "act": nc.scalar, "dve": nc.vector, "pool": nc.gpsimd, "sp": nc.sync}
        streams = self.streams

        def run(ename, eh):
            seen = {}
            for o in streams[ename]:
                for d in o.deps:
                    if (not d.isdma) and d.eng == "pe" and ename == "pe" and not o.isdma:
                        continue
                    k = id(d.sem)
                    if seen.get(k, 0) < d.val:
                        eh.wait_ge(d.sem[0], d.sem[1] + d.val)
                        seen[k] = d.val
                if o.fn is None:
                    continue
                ins = o.fn(eh)
                if o.isdma:
                    ins.then_inc(o.sem[0], 16)
                elif o.inc:
                    ins.then_inc(o.sem[0], 1)
            for d in finals:
                k = id(d.sem)
                if seen.get(k, 0) < d.val:
                    eh.wait_ge(d.sem[0], d.sem[1] + d.val)
                    seen[k] = d.val

        with nc.named_scope(self.name), nc.Block() as blk:
            @blk.tensor
            def _(e):
                run("pe", e)

            @blk.scalar
            def _(e):
                run("act", e)

            @blk.vector
            def _(e):
                run("dve", e)

            @blk.gpsimd
            def _(e):
                run("pool", e)

            @blk.sync
            def _(e):
                run("sp", e)
        tot = {}
        for e in ENGS:
            for o in streams[e]:
                if o.fn is None:
                    continue
                if o.isdma:
                    tot[id(o.sem)] = tot.get(id(o.sem), 0) + 16
                elif o.inc:
                    tot[id(o.sem)] = tot.get(id(o.sem), 0) + 1
        for it in self.sems:
            it[1] += tot.get(id(it), 0)
            it[2] = False
            assert it[1] < 60000, it[1]

    def close(self):
        self.es.close()


def mm(P, out, lhsT, rhs, start, stop, reads, writes, sgc=False):
    if sgc:
        return P.op("pe", lambda e: e.matmul(out, lhsT, rhs, start=start, stop=stop, skip_group_check=True), reads, writes)
    return P.op("pe", lambda e: e.matmul(out, lhsT, rhs, start=start, stop=stop), reads, writes)


def token_tiles():
    tiles = [(0, LC, True)]
    for i in range(S // 512):
        tiles.append((LC + i * 512, 512, False))
    return tiles


def mod_pass(nc, G, dr, preload=None):
    P = Pass(nc, "pm")
    pre = FfnPreload(P, *preload) if preload is not None else None
    c2 = P.sb("c2", [128, 2, 8], F32)
    r2 = P.sb("r2", [128, 2, 8], F32)
    tc2 = P.tok("c2")
    tr2 = P.tok("r2")
    P.dma(c2[:, 0, :], dr["c8"], writes=[tc2], dom=tc2)
    P.dma(c2[:, 1, :], dr["cc8"], writes=[tc2], dom=tc2)
    P.op("act", lambda e: e.activation(r2[:], c2[:], AF.Silu), [tc2], [tr2])
    GW = 384
    NG = 9216 // GW
    wbuf = [P.sb("w%d" % i, [128, 8, GW], F32) for i in range(2)]
    twb = [P.tok("w%d" % i) for i in range(2)]
    bm = P.sb("bm", [128, 2, 72], F32)
    ng = P.sb("ng", [128, 2, 24], F32)
    tbm = P.tok("bm")
    for l in range(2):
        P.dma(bm[:, l, :], dr["l%d_bmod" % l], writes=[tbm], dom=tbm)
        P.dma(ng[:, l, :], dr["l%d_ng" % l], writes=[tbm], dom=tbm)
    pm = P.ps("pm", [128, 2, 72, 2], F32)
    tpm = P.tok("pm")
    it = 0
    for l in range(2):
        wm = dr["l%d_wmod" % l]
        for g in range(NG):
            b = it % 2
            it += 1
            for k in range(8):
                P.dma(wbuf[b][:, k, :], wm[k * 128:(k + 1) * 128, g * GW:(g + 1) * GW], writes=[twb[b]], dom=twb[b])
            if pre is not None:
                pre.pump(2)
            for cb in range(GW // 128):
                col = g * (GW // 128) + cb
                for k in range(8):
                    mm(P, pm[:, l, col, :], wbuf[b][:, k, cb * 128:(cb + 1) * 128], r2[:, :, k],
                       k == 0, k == 7, [twb[b], tr2], [tpm])
    tG = G["tok"]
    for l in range(2):
        for w in range(2):
            P.op("dve", lambda e, l=l, w=w: e.tensor_tensor(G["MOD"][:, l, w, :], pm[:, l, :, w], bm[:, l, :], ALU.add),
                 [tpm, tbm], [tG])
    for l in range(2):
        for w in range(2):
            for i in range(3):
                sh = G["MOD"][:, l, w, (3 * i) * 8:(3 * i) * 8 + 8]
                sc = G["MOD"][:, l, w, (3 * i + 1) * 8:(3 * i + 1) * 8 + 8]
                gt = G["MOD"][:, l, w, (3 * i + 2) * 8:(3 * i + 2) * 8 + 8]
                P.op("dve", lambda e, l=l, w=w, i=i, sc=sc: e.scalar_tensor_tensor(
                    G["A"][:, l, w, i, :], sc, 1.0, ng[:, l, i * 8:(i + 1) * 8], ALU.add, ALU.mult), [tG, tbm], [tG])
                P.op("dve", lambda e, l=l, w=w, i=i, sh=sh: e.tensor_copy(G["B"][:, l, w, i, :], sh), [tG], [tG])
                fac = 1.0 if i == 1 else 0.5
                P.op("dve", lambda e, l=l, w=w, i=i, gt=gt, fac=fac: e.tensor_scalar(
                    G["GT"][:, l, w, i, :], gt, fac, None, ALU.mult), [tG], [tG])
    if pre is not None:
        pre.flush()
    P.emit()
    P.close()


def emit_prenorm(P, R, G, l, w, i, x, tx, h, th, n, tag=""):
    sq, tsq = R["sq"], R["tsq"]
    for k in range(8):
        b = k % 2
        P.op("act", lambda e, k=k, b=b: e.activation(sq[b][:, :n], x[:, k, :n], AF.Square), [tx], [tsq[b]])
        mm(P, R["pss"][:, :n], G["ones1024"][:], sq[b][:, :n], k == 0, k == 7, [tsq[b], G["tokc"]], [R["tpss"]])
    emit_rsqrt(P, R["rstd"][:, :n], R["pss"][:, :n], R["rtmp"][:, :n], R["tpss"], R["trstd"], R["trtmp"])
    for k in range(8):
        b = k % 2
        tt, ttt = R["tt"], R["ttt"]
        P.op("dve", lambda e, k=k, b=b: e.scalar_tensor_tensor(
            tt[b][:, :n], x[:, k, :n], G["A"][:, l, w, i, k:k + 1], R["rstd"][:, :n], ALU.mult, ALU.mult),
            [tx, R["trstd"], G["tok"]], [ttt[b]])
        P.op("act", lambda e, k=k, b=b: e.activation(
            h[:, k, :n], tt[b][:, :n], AF.Identity, bias=G["B"][:, l, w, i, k:k + 1], scale=1.0),
            [ttt[b], G["tok"]], [th])


def emit_rsqrt(P, out, in_ps, tmp, tin, tout, ttmp):
    P.op("dve", lambda e: e.tensor_scalar(tmp, in_ps, EPS, None, ALU.add), [tin], [ttmp])
    P.op("act", lambda e: e.activation(tmp, tmp, AF.Sqrt), [ttmp], [ttmp])
    P.op("dve", lambda e: e.reciprocal(out, tmp), [ttmp], [tout])


def alloc_norm_scratch(P):
    R = {}
    R["rtmp"] = P.sb("rtmp", [128, 512], F32)
    R["trtmp"] = P.tok("rtmp")
    R["sq"] = [P.sb("sq%d" % i, [128, 512], BF16) for i in range(2)]
    R["tsq"] = [P.tok("sq%d" % i) for i in range(2)]
    R["tt"] = [P.sb("tt%d" % i, [128, 512], F32) for i in range(2)]
    R["ttt"] = [P.tok("tt%d" % i) for i in range(2)]
    R["rstd"] = P.sb("rstd", [128, 512], F32)
    R["trstd"] = P.tok("rstd")
    R["pss"] = P.ps("pss", [128, 512], F32)
    R["tpss"] = P.tok("pss")
    return R


def ffn_pass(nc, G, name, Xin, Xout, wgu_d, wd_d, l, i, tiles, xout_off=0, pre=None):
    P = Pass(nc, name)
    if pre is None:
        Wgu = P.sb("wgu", [128, 8, 2 * DFF], BF16)
        Wd = P.sb("wd", [128, NF, D], BF16)
    else:
        Wgu, Wd = pre
    tW = P.tok("W")
    act = P.sb("act", [128, NF, 512], BF16)
    tact = [P.tok("act%d" % j) for j in range(NF)]
    stage = act[:].rearrange("p a b -> p (a b)").bitcast(F32)
    SW = 1408
    tst = [P.tok("st%d" % j) for j in range(4)]
    jobs = []
    for k in range(8):
        for cb in range(2 * DFF // SW):
            jobs.append((wgu_d[k * 128:(k + 1) * 128, cb * SW:(cb + 1) * SW], Wgu[:, k, cb * SW:(cb + 1) * SW], SW))
    for f in range(NF):
        jobs.append((wd_d[f * 128:(f + 1) * 128, :], Wd[:, f, :], D))
    if pre is not None:
        jobs = []
    for j, (src, dst, wdt) in enumerate(jobs):
        s = j % 4
        sv = stage[:, s * SW:s * SW + wdt]
        P.dma(sv, src, writes=[tst[s]], dom=tst[s])
        eng = "dve" if j % 2 == 0 else "pool"
        P.op(eng, lambda e, dst=dst, sv=sv: e.tensor_copy(dst, sv), [tst[s]], [tW])
    for j in range(NF):
        for s in range(4):
            pass
    R = alloc_norm_scratch(P)
    xn = P.sb("xn", [128, 8, 512], F32)
    txn = P.tok("xn")
    h = P.sb("h", [128, 8, 512], BF16)
    th = P.tok("h")
    NXR = 4
    xr = [P.sb("xr%d" % j, [128, 512], F32) for j in range(NXR)]
    txr = [P.tok("xr%d" % j) for j in range(NXR)]
    sl = [P.sb("sl%d" % j, [128, 512], F32) for j in range(2)]
    tsl = [P.tok("sl%d" % j) for j in range(2)]
    pg = [P.ps("pg%d" % j, [128, 512], F32) for j in range(2)]
    pu = [P.ps("pu%d" % j, [128, 512], F32) for j in range(2)]
    tpg = [P.tok("pg%d" % j) for j in range(2)]
    tpu = [P.tok("pu%d" % j) for j in range(2)]
    py = [P.ps("py%d" % j, [128, 512], F32) for j in range(2)]
    tpy = [P.tok("py%d" % j) for j in range(2)]

    def load_x(t0, n):
        for k in range(8):
            P.dma(xn[:, k, :n], Xin[k * 128:(k + 1) * 128, t0:t0 + n], writes=[txn], dom=txn)

    alias_ops = []
    load_x(tiles[0][0], tiles[0][1])
    st = {"cnt": 0, "ycnt": 0}

    def prenorm_tile(tj):
        t0j, nj, cj = tiles[tj]
        emit_prenorm(P, R, G, l, 1 if cj else 0, i, xn, txn, h, th, nj)
        if tj + 1 < len(tiles):
            load_x(tiles[tj + 1][0], tiles[tj + 1][1])

    prenorm_tile(0)

    def tile_body(ti, t0, n, isctx):
        w = 1 if isctx else 0
        for f in range(NF):
            b = st["cnt"] % 2
            st["cnt"] += 1
            for k in range(8):
                mm(P, pg[b][:, :n], Wgu[:, k, f * 128:(f + 1) * 128], h[:, k, :n], k == 0, k == 7, [tW, th], [tpg[b]])
            for k in range(8):
                mm(P, pu[b][:, :n], Wgu[:, k, DFF + f * 128:DFF + (f + 1) * 128], h[:, k, :n], k == 0, k == 7,
                   [tW, th], [tpu[b]])
            P.op("act", lambda e, b=b: e.activation(sl[b][:, :n], pg[b][:, :n], AF.Silu), [tpg[b]], [tsl[b]])
            extra_w = list(tst) if ti == 0 else []
            P.op("dve", lambda e, b=b, f=f: e.tensor_tensor(act[:, f, :n], pu[b][:, :n], sl[b][:, :n], ALU.mult),
                 [tpu[b], tsl[b]], [tact[f]] + extra_w)

        def load_xr(dc, yc):
            r = yc % NXR
            P.dma(xr[r][:, :n], Xin[dc * 128:(dc + 1) * 128, t0:t0 + n], writes=[txr[r]], dom=txr[r])
        load_xr(0, st["ycnt"])
        load_xr(1, st["ycnt"] + 1)
        for dc in range(8):
            b = st["ycnt"] % 2
            r = st["ycnt"] % NXR
            if dc + 2 < 8:
                load_xr(dc + 2, st["ycnt"] + 2)
            st["ycnt"] += 1
            if dc == 4 and ti + 1 < len(tiles):
                prenorm_tile(ti + 1)
            for f in range(NF):
                mm(P, py[b][:, :n], Wd[:, f, dc * 128:(dc + 1) * 128], act[:, f, :n], f == 0, f == NF - 1,
                   [tW, tact[f]], [tpy[b]])
            P.op("dve", lambda e, b=b, r=r, dc=dc: e.scalar_tensor_tensor(
                xr[r][:, :n], py[b][:, :n], G["GT"][:, l, w, i, dc:dc + 1], xr[r][:, :n], ALU.mult, ALU.add),
                [tpy[b], txr[r], G["tok"]], [tpy[b], txr[r]])
            P.dma(Xout[dc * 128:(dc + 1) * 128, t0 - xout_off:t0 - xout_off + n], xr[r][:, :n], reads=[txr[r]],
                  dom=txr[r])

    for ti, (t0, n, isctx) in enumerate(tiles):
        tile_body(ti, t0, n, isctx)
    P.emit()
    P.close()


def load_cast_weights(P, jobs, stage_tiles, tst, tW):
    ns = len(stage_tiles)
    for j, (src, dst, wdt) in enumerate(jobs):
        s = j % ns
        sv = stage_tiles[s][:, :wdt]
        P.dma(sv, src, writes=[tst[s]], dom=tst[s])
        eng = "dve" if j % 2 == 0 else "pool"
        P.op(eng, lambda e, dst=dst, sv=sv: e.tensor_copy(dst, sv), [tst[s]], [tW])


class FfnPreload:
    def __init__(self, P, wgu_d, wd_d, Wgu, Wd):
        self.P = P
        SW = 1408
        self.stg = [P.sb("wstg%d" % j, [128, SW], F32) for j in range(4)]
        self.tst = [P.tok("wstg%d" % j) for j in range(4)]
        self.tW = P.tok("Wpre")
        self.jobs = []
        for k in range(8):
            for cb in range(2 * DFF // SW):
                self.jobs.append((wgu_d[k * 128:(k + 1) * 128, cb * SW:(cb + 1) * SW], Wgu[:, k, cb * SW:(cb + 1) * SW], SW))
        for f in range(NF):
            self.jobs.append((wd_d[f * 128:(f + 1) * 128, :], Wd[:, f, :], D))
        self.j = 0

    def pump(self, k):
        P = self.P
        for _ in range(k):
            if self.j >= len(self.jobs):
                return
            src, dst, wdt = self.jobs[self.j]
            s_ = self.j % 4
            sv = self.stg[s_][:, :wdt]
            P.dma(sv, src, writes=[self.tst[s_]], dom=self.tst[s_])
            eng = "dve" if self.j % 2 == 0 else "pool"
            P.op(eng, lambda e, dst=dst, sv=sv: e.tensor_copy(dst, sv), [self.tst[s_]], [self.tW])
            self.j += 1

    def flush(self):
        self.pump(len(self.jobs))


class Rot:
    def __init__(self, P, name, shape, dt, nbuf, psum=False):
        self.t = [(P.ps if psum else P.sb)("%s%d" % (name, i), shape, dt) for i in range(nbuf)]
        self.k = [P.tok("%s%d" % (name, i)) for i in range(nbuf)]
        self.i = 0

    def next(self):
        j = self.i % len(self.t)
        self.i += 1
        return self.t[j], self.k[j]


def prep0_pass(nc, G, dr, SC, Xin, tiles):
    P = Pass(nc, "p0")
    NCH = 16
    W = P.sb("W", [128, 8, NCH * 128], BF16)
    Wuq = P.sb("Wuq", [128, 2, 1024], BF16)
    Wukv = P.sb("Wukv", [128, 1024], BF16)
    tW = P.tok("W")
    stg = [P.sb("stg%d" % j, [128, 2048], F32) for j in range(2)]
    tst = [P.tok("stg%d" % j) for j in range(2)]
    jobs = [(dr["l0_win"][k * 128:(k + 1) * 128, :], W[:, k, :], NCH * 128) for k in range(8)]
    jobs += [(dr["l0_wuq"][k * 128:(k + 1) * 128, :], Wuq[:, k, :], 1024) for k in range(2)]
    jobs += [(dr["l0_wukv"][:, :], Wukv[:, :], 1024)]
    load_cast_weights(P, jobs, stg, tst, tW)
    gv = P.sb("gv", [128, 16], F32)
    Ms = P.sb("Ms", [128, 4, 128], BF16)
    tcst = P.tok("cst")
    P.dma(gv[:], dr["l0_gv"], writes=[tcst], dom=tcst)
    P.dma(Ms[:], dr["Ms"].rearrange("m p c -> p m c"), writes=[tcst], dom=tcst)
    R = alloc_norm_scratch(P)
    xn = P.sb("xn", [128, 8, 512], F32)
    txn = P.tok("xn")
    h = P.sb("h", [128, 8, 512], BF16)
    th = P.tok("h")
    tabs = [[P.sb("tab%d_%d" % (b, j), [128, 512], F32) for j in range(4)] for b in range(2)]
    ttab = [P.tok("tab%d" % b) for b in range(2)]
    tabd = [dr["cosm"], dr["sinm"], dr["cosw"], dr["sinw"]]
    pz = Rot(P, "pz", [128, 512], F32, 6, psum=True)
    pss2 = Rot(P, "pss2", [128, 512], F32, 1, psum=True)
    sq2 = Rot(P, "sq2", [128, 512], BF16, 2)
    rst = Rot(P, "rst", [128, 512], F32, 2)
    rtm = Rot(P, "rtm", [128, 512], F32, 2)
    ta = Rot(P, "ta", [128, 512], F32, 2)
    tb = Rot(P, "tb", [128, 512], F32, 2)
    outb = Rot(P, "outb", [128, 512], BF16, 4)
    vout = Rot(P, "vout", [128, 512], BF16, 2)
    ckvn = P.sb("ckvn", [128, 512], BF16)
    tckvn = P.tok("ckvn")
    cqn = P.sb("cqn", [128, 2, 512], BF16)
    tcqn = P.tok("cqn")

    def load_x(t0, n):
        for k in range(8):
            P.dma(xn[:, k, :n], Xin[k * 128:(k + 1) * 128, t0:t0 + n], writes=[txn], dom=txn)

    def proj(chunk, n, rows=128):
        z, tz = pz.next()
        for k in range(8):
            mm(P, z[:rows, :n], W[:, k, chunk * 128:chunk * 128 + rows], h[:, k, :n], k == 0, k == 7, [tW, th], [tz])
        return z, tz

    def rstd_of(zs, M, rows, n):
        p2, tp2 = pss2.next()
        for j, (z, tz) in enumerate(zs):
            q, tq = sq2.next()
            P.op("act", lambda e, q=q, z=z: e.activation(q[:rows, :n], z[:rows, :n], AF.Square), [tz], [tq])
            mm(P, p2[:rows, :n], M[:rows, :rows], q[:rows, :n], j == 0, j == len(zs) - 1, [tq, tcst], [tp2])
        r, tr = rst.next()
        tm, ttm = rtm.next()
        emit_rsqrt(P, r[:rows, :n], p2[:rows, :n], tm[:rows, :n], tp2, tr, ttm)
        return r, tr

    def scale_out(z, tz, r, tr, gcol, dst, tdst, rows, n):
        P.op("dve", lambda e: e.scalar_tensor_tensor(dst, z[:rows, :n], gv[:rows, gcol:gcol + 1], r[:rows, :n],
                                                     ALU.mult, ALU.mult), [tz, tr, tcst], [tdst])

    def rope_out(z, tz, zs, tzs, r, tr, gcol, cos, sin, ttb, dst, tdst, rows, n):
        a, tka = ta.next()
        b, tkb = tb.next()
        scale_out(z, tz, r, tr, gcol, a[:rows, :n], tka, rows, n)
        scale_out(zs, tzs, r, tr, gcol + 1, b[:rows, :n], tkb, rows, n)
        P.op("pool", lambda e: e.tensor_tensor(a[:rows, :n], a[:rows, :n], cos[:rows, :n], ALU.mult), [tka, ttb], [tka])
        P.op("pool", lambda e: e.tensor_tensor(b[:rows, :n], b[:rows, :n], sin[:rows, :n], ALU.mult), [tkb, ttb], [tkb])
        P.op("pool", lambda e: e.tensor_tensor(dst, a[:rows, :n], b[:rows, :n], ALU.add), [tka, tkb], [tdst])

    def store(dst_dram, src, tsrc):
        P.dma(dst_dram, src, reads=[tsrc], dom=tsrc)

    def load_tabs(bi, t0, n):
        for j in range(4):
            P.dma(tabs[bi][j][:, :n], tabd[j][:, t0:t0 + n], writes=[ttab[bi]], dom=ttab[bi])

    M128, M64, M32, M256 = Ms[:, 0, :], Ms[:, 1, :], Ms[:, 2, :], Ms[:, 3, :]
    load_x(tiles[0][0], tiles[0][1])
    load_tabs(0, tiles[0][0], tiles[0][1])

    def tile_body(ti, t0, n, isctx):
        w = 1 if isctx else 0
        bi = ti % 2
        cm, sm, cw, sw = tabs[bi]
        ttb = ttab[bi]
        emit_prenorm(P, R, G, 0, w, 1, xn, txn, h, th, n)
        if ti + 1 < len(tiles):
            load_x(tiles[ti + 1][0], tiles[ti + 1][1])
            load_tabs(1 - bi, tiles[ti + 1][0], tiles[ti + 1][1])
        nsub = n // 128
        units = []

        def u_ckv():
            def A():
                return proj(0, n)

            def B(c):
                z, tz = c
                r, tr = rstd_of([(z, tz)], M128, 128, n)
                scale_out(z, tz, r, tr, 0, ckvn[:, :n], tckvn, 128, n)
            units.append((A, B))

        def u_rope(ch, chs, rows, M, gcol, cos, sin, dst_fn):
            def A():
                return proj(ch, n, rows), proj(chs, n, rows)

            def B(c):
                (z, tz), (zs, tzs) = c
                r, tr = rstd_of([(z, tz)], M, rows, n)
                o, to = outb.next()
                rope_out(z, tz, zs, tzs, r, tr, gcol, cos, sin, ttb, o[:rows, :n], to, rows, n)
                store(dst_fn(), o[:rows, :n], to)
            units.append((A, B))

        def u_cq():
            def A():
                return proj(5, n), proj(6, n)

            def B(c):
                (z0, tz0), (z1, tz1) = c
                r, tr = rstd_of([(z0, tz0), (z1, tz1)], M256, 128, n)
                scale_out(z0, tz0, r, tr, 5, cqn[:, 0, :n], tcqn, 128, n)
                scale_out(z1, tz1, r, tr, 6, cqn[:, 1, :n], tcqn, 128, n)
            units.append((A, B))

        def u_kn(j):
            def A():
                z, tz = pz.next()
                mm(P, z[:, :n], Wukv[:, j * 128:(j + 1) * 128], ckvn[:, :n], True, True, [tW, tckvn], [tz])
                return z, tz

            def B(c):
                z, tz = c
                r, tr = rstd_of([(z, tz)], M64, 128, n)
                o, to = outb.next()
                scale_out(z, tz, r, tr, 9, o[:, :n], to, 128, n)
                store(SC["KN"][j, :, t0:t0 + n], o[:, :n], to)
            units.append((A, B))

        def u_va(sub):
            def A():
                p, tp = pz.next()
                mm(P, p[:, :], ckvn[:, sub * 128:(sub + 1) * 128], Wukv[:, 512:1024], True, True, [tW, tckvn], [tp])
                return p, tp

            def B(c):
                p, tp = c
                v, tv = vout.next()
                P.op("act", lambda e: e.copy(v[:, :], p[:, :]), [tp], [tv])
                store(SC["VA"][t0 + sub * 128:t0 + (sub + 1) * 128, :], v[:, :], tv)
            units.append((A, B))

        def u_qn(j):
            def A():
                z, tz = pz.next()
                for k in range(2):
                    mm(P, z[:, :n], Wuq[:, k, j * 128:(j + 1) * 128], cqn[:, k, :n], k == 0, k == 1, [tW, tcqn], [tz])
                return z, tz

            def B(c):
                z, tz = c
                r, tr = rstd_of([(z, tz)], M64, 128, n)
                o, to = outb.next()
                scale_out(z, tz, r, tr, 10, o[:, :n], to, 128, n)
                store(SC["QN"][j, :, t0:t0 + n], o[:, :n], to)
            units.append((A, B))

        def u_qr(j):
            def A():
                z, tz = pz.next()
                for k in range(2):
                    mm(P, z[:, :n], Wuq[:, k, 512 + j * 128:512 + (j + 1) * 128], cqn[:, k, :n], k == 0, k == 1,
                       [tW, tcqn], [tz])
                zs, tzs = pz.next()
                for k in range(2):
                    mm(P, zs[:, :n], Wuq[:, k, 768 + j * 128:768 + (j + 1) * 128], cqn[:, k, :n], k == 0, k == 1,
                       [tW, tcqn], [tzs])
                return (z, tz), (zs, tzs)

            def B(c):
                (z, tz), (zs, tzs) = c
                r, tr = rstd_of([(z, tz)], M32, 128, n)
                o, to = outb.next()
                rope_out(z, tz, zs, tzs, r, tr, 11, cm, sm, ttb, o[:, :n], to, 128, n)
                store(SC["QR"][j, :, t0:t0 + n], o[:, :n], to)
            units.append((A, B))

        def u_vb(sub):
            def A():
                p, tp = pz.next()
                for k in range(8):
                    mm(P, p[:, :128], h[:, k, sub * 128:(sub + 1) * 128], W[:, k, 15 * 128:16 * 128], k == 0, k == 7,
                       [tW, th], [tp])
                return p, tp

            def B(c):
                p, tp = c
                v, tv = vout.next()
                P.op("act", lambda e: e.copy(v[:, :128], p[:, :128]), [tp], [tv])
                store(SC["VB"][t0 + sub * 128:t0 + (sub + 1) * 128, :], v[:, :128], tv)
            units.append((A, B))

        u_ckv()
        u_rope(1, 2, 32, M32, 1, cm, sm, lambda: SC["KR"][:, t0:t0 + n])
        u_rope(3, 4, 128, M64, 3, cw, sw, lambda: SC["KB"][:, t0:t0 + n])
        u_cq()
        for j in range(4):
            u_kn(j)
        for sub in range(nsub):
            u_va(sub)
        for j in range(4):
            u_qn(j)
        for j in range(2):
            u_qr(j)
        for j in range(4):
            u_rope(7 + j, 11 + j, 128, M64, 7, cw, sw, lambda j=j: SC["QB"][j, :, t0:t0 + n])
        for sub in range(nsub):
            u_vb(sub)
        pending = None
        for (A_, B_) in units:
            c = A_()
            if pending is not None:
                pending()
            pending = (lambda B_=B_, c=c: B_(c))
        pending()

    for ti, (t0, n, isctx) in enumerate(tiles):
        tile_body(ti, t0, n, isctx)
    P.emit()
    P.close()


def attn_finalize(P, A, po, tpo, n, dst, tdst, sink=None, pview=False):
    rsb, trsb = A["rsb"].next()
    if sink is not None:
        P.op("dve", lambda e: e.tensor_tensor(rsb[64:65, :n], po[64:65, :n], sink, ALU.add), [tpo, A["tcst"]], [trsb])
        P.op("dve", lambda e: e.reciprocal(rsb[64:65, :n], rsb[64:65, :n]), [trsb], [trsb])
    else:
        P.op("dve", lambda e: e.reciprocal(rsb[64:65, :n], po[64:65, :n]), [tpo], [trsb])
    pbc, tpbc = A["pbc"].next()
    mm(P, pbc[0:64, :n], A["onesf"][64:65, 0:64], rsb[64:65, :n], True, True, [trsb, A["tcst"]], [tpbc])
    bcs, tbcs = A["bcs"].next()
    P.op("act", lambda e: e.copy(bcs[0:64, :n], pbc[0:64, :n]), [tpbc], [tbcs])
    if pview:
        a0 = po[0:64, :n].rearrange("p (g q) -> p g q", g=4)
        a1 = bcs[0:64, :n].rearrange("p (g q) -> p g q", g=4)
    else:
        a0 = po[0:64, :n]
        a1 = bcs[0:64, :n]
    P.op("dve", lambda e: e.tensor_tensor(dst, a0, a1, ALU.mult), [tpo, tbcs], [tdst, tpo])


def attn_common(P, dr):
    A = {}
    A["rsb"] = Rot(P, "rsb", [128, 512], F32, 2)
    A["pbc"] = Rot(P, "pbc", [128, 512], F32, 1, psum=True)
    A["bcs"] = Rot(P, "bcs", [64, 512], F32, 2)
    A["onesf"] = P.sb("onesf", [128, 64], F32)
    A["tcst"] = P.tok("acst")
    P.dma(A["onesf"][:], dr["onesf"], writes=[A["tcst"]], dom=A["tcst"])
    return A


def mla_pass(nc, G, dr, SC, tiles, heads=range(8)):
    P = Pass(nc, "pa")
    A = attn_common(P, dr)
    NKT = T // 128
    Kt = [P.sb("Kt%d" % b, [96, T], BF16) for b in range(2)]
    Qt = [P.sb("Qt%d" % b, [96, T], BF16) for b in range(2)]
    Vh = [P.sb("Vh%d" % b, [128, NKT, 65], BF16) for b in range(2)]
    tKQV = [P.tok("kqv%d" % b) for b in range(2)]
    for b in range(2):
        P.op("pool", lambda e, b=b: e.memset(Vh[b][:, :, 64:65], 1.0), [], [tKQV[b]])
    pS = Rot(P, "pS", [128, 512], F32, 4, psum=True)
    pO = Rot(P, "pO", [128, 512], F32, 2, psum=True)
    Pm = Rot(P, "Pm", [128, 512], BF16, 4)
    ob = Rot(P, "ob", [64, 512], BF16, 2)
    scale = 96.0 ** -0.5
    VAv = SC["VA"].rearrange("(kt p) c -> p kt c", p=128)

    def load_head(hh, b):
        tk = tKQV[b]
        P.dma(Kt[b][0:64, :], SC["KN"][hh // 2, (hh % 2) * 64:(hh % 2) * 64 + 64, :], writes=[tk], dom=tk)
        P.dma(Kt[b][64:96, :], SC["KR"][:, :], writes=[tk], dom=tk)
        P.dma(Qt[b][0:64, :], SC["QN"][hh // 2, (hh % 2) * 64:(hh % 2) * 64 + 64, :], writes=[tk], dom=tk)
        P.dma(Qt[b][64:96, :], SC["QR"][hh // 4, (hh % 4) * 32:(hh % 4) * 32 + 32, :], writes=[tk], dom=tk)
        for c0 in range(0, NKT, 11):
            P.dma(Vh[b][:, c0:c0 + 11, 0:64], VAv[:, c0:c0 + 11, hh * 64:(hh + 1) * 64], writes=[tk], dom=tk)

    heads = list(heads)
    load_head(heads[0], 0)
    for hi, hh in enumerate(heads):
        b = hi % 2
        if hi + 1 < len(heads):
            load_head(heads[hi + 1], 1 - b)
        tk = tKQV[b]
        steps = []
        for (t0, n, isctx) in tiles:
            nkt = 2 if isctx else NKT
            for kt in range(nkt):
                steps.append((t0, n, kt, nkt))
        LAG = 2
        pend = []
        cur = {}

        def issue_S(t0, n, kt, nkt):
            ps, tps = pS.next()
            mm(P, ps[:, :n], Kt[b][:, kt * 128:(kt + 1) * 128], Qt[b][:, t0:t0 + n], True, True, [tk], [tps])
            pm, tpm = Pm.next()
            P.op("act", lambda e: e.activation(pm[:, :n], ps[:, :n], AF.Exp, scale=scale), [tps], [tpm, tps])
            return pm, tpm

        def issue_PV(t0, n, kt, nkt, pm, tpm):
            if kt == 0:
                cur["po"], cur["tpo"] = pO.next()
            po, tpo = cur["po"], cur["tpo"]
            mm(P, po[0:65, :n], Vh[b][:, kt, :], pm[:, :n], kt == 0, kt == nkt - 1, [tk, tpm], [tpo])
            if kt == nkt - 1:
                def fin(po=po, tpo=tpo, n=n, t0=t0):
                    o, to = ob.next()
                    attn_finalize(P, A, po, tpo, n, o[:, :n], to, None)
                    P.dma(SC["OT"][hh * 64:(hh + 1) * 64, t0:t0 + n], o[:, :n], reads=[to], dom=to)
                fin_q.append([6, fin])

        fin_q = []
        for si, st_ in enumerate(steps):
            pm, tpm = issue_S(*st_)
            pend.append((st_, pm, tpm))
            if len(pend) > LAG:
                s0, pm0, tpm0 = pend.pop(0)
                issue_PV(*s0, pm0, tpm0)
            for it in fin_q:
                it[0] -= 1
            while fin_q and fin_q[0][0] <= 0:
                fin_q.pop(0)[1]()
        while pend:
            s0, pm0, tpm0 = pend.pop(0)
            issue_PV(*s0, pm0, tpm0)
        while fin_q:
            fin_q.pop(0)[1]()
    P.emit()
    P.close()


def win_pass(nc, G, dr, SC):
    P = Pass(nc, "pw")
    A = attn_common(P, dr)
    tc = A["tcst"]
    masks = P.sb("masks", [128, 2, 512], BF16)
    P.dma(masks[:], dr["wmask"].rearrange("m p c -> p m c"), writes=[tc], dom=tc)
    identb = P.sb("identb", [128, 128], BF16)
    P.dma(identb[:], dr["ident"], writes=[tc], dom=tc)
    sink = P.sb("sink", [128, 1024], F32)
    P.dma(sink[64:65, :], dr["l0_sink"], writes=[tc], dom=tc)
    P.op("act", lambda e: e.activation(sink[64:65, :], sink[64:65, :], AF.Exp), [tc], [tc])
    Kc = P.sb("Kc", [64, 2, LC], BF16)
    Vc = P.sb("Vc", [128, 2, 2, 65], BF16)
    P.op("pool", lambda e: e.memset(Vc[:, :, :, 64:65], 1.0), [], [tc])
    VBv = SC["VB"].rearrange("(kt p) c -> p kt c", p=128)
    for nk in range(2):
        P.dma(Kc[:, nk, :], SC["KB"][nk * 64:(nk + 1) * 64, 0:LC], writes=[tc], dom=tc)
        P.dma(Vc[:, nk, :, 0:64], VBv[:, 0:2, nk * 64:(nk + 1) * 64], writes=[tc], dom=tc)
    Qg = Rot(P, "Qg", [64, 4, 512], BF16, 2)
    Kg = Rot(P, "Kg", [64, 6 * 128], BF16, 2)
    Vg = Rot(P, "Vg", [128, 6, 65], BF16, 2)
    for v, tv in zip(Vg.t, Vg.k):
        P.op("pool", lambda e, v=v: e.memset(v[:, :, 64:65], 1.0), [], [tv])
    pS = Rot(P, "pS", [128, 512], F32, 4, psum=True)
    pO = Rot(P, "pO", [128, 512], F32, 3, psum=True)
    Pm = Rot(P, "Pm", [128, 512], BF16, 4)
    obg = Rot(P, "obg", [64, 4, 512], BF16, 2)
    scale = 64.0 ** -0.5
    fin_q = []
    groups = [("ctx", 0)] + [("lat", g) for g in range(S // 512)]
    for nk in range(2):
        for kind, gi in groups:
            q, tq = Qg.next()
            if kind == "ctx":
                q0, nq = 0, LC
            else:
                q0, nq = LC + gi * 512, 512
            for g in range(4):
                hq = nk * 4 + g
                P.dma(q[:, g, :nq], SC["QB"][hq // 2, (hq % 2) * 64:(hq % 2) * 64 + 64, q0:q0 + nq], writes=[tq], dom=tq)
            if kind == "lat":
                blo = max(4 * gi - 1, 0)
                bhi = min(4 * gi + 4, S // 128 - 1)
                nb = bhi - blo + 1
                kg, tkg = Kg.next()
                vg, tvg = Vg.next()
                P.dma(kg[:, :nb * 128], SC["KB"][nk * 64:(nk + 1) * 64, LC + blo * 128:LC + (bhi + 1) * 128],
                      writes=[tkg], dom=tkg)
                P.dma(vg[:, :nb, 0:64], VBv[:, 2 + blo:2 + bhi + 1, nk * 64:(nk + 1) * 64], writes=[tvg], dom=tvg)
            og, tog = obg.next()
            steps = []
            for qb in range(nq // 128):
                keys = [("c", 0, None), ("c", 1, None)]
                if kind == "lat":
                    i = 4 * gi + qb
                    if i - 1 >= 0:
                        keys.append(("l", i - 1 - blo, 0))
                    keys.append(("l", i - blo, None))
                    if i + 1 <= S // 128 - 1:
                        keys.append(("l", i + 1 - blo, 1))
                for ki, (kk, idx, mk) in enumerate(keys):
                    if kk == "c":
                        lhsT = Kc[:, nk, idx * 128:(idx + 1) * 128]
                        vv = Vc[:, nk, idx, :]
                        rd = [tc]
                    else:
                        lhsT = kg[:, idx * 128:(idx + 1) * 128]
                        vv = vg[:, idx, :]
                        rd = [tkg, tvg]
                    steps.append(dict(qb=qb, first=(ki == 0), last=(ki == len(keys) - 1), lhsT=lhsT, vv=vv, rd=rd, mk=mk))
            cur = {}

            def issue_S(stp):
                ps, tps = pS.next()
                qb = stp["qb"]
                mm(P, ps[:, :].rearrange("p (g q) -> p g q", g=4), stp["lhsT"], q[:, :, qb * 128:(qb + 1) * 128],
                   True, stp["mk"] is None, stp["rd"] + [tq], [tps])
                if stp["mk"] is not None:
                    mm(P, ps[:, :], identb[:, :], masks[:, stp["mk"], :], False, True, [tc], [tps])
                pm, tpm = Pm.next()
                P.op("act", lambda e: e.activation(pm[:, :], ps[:, :], AF.Exp, scale=scale), [tps], [tpm, tps])
                stp["pm"], stp["tpm"] = pm, tpm

            def issue_PV(stp):
                if stp["first"]:
                    cur["po"], cur["tpo"] = pO.next()
                po, tpo = cur["po"], cur["tpo"]
                mm(P, po[0:65, :], stp["vv"], stp["pm"][:, :], stp["first"], stp["last"], stp["rd"] + [stp["tpm"]], [tpo])
                if stp["last"]:
                    while fin_q:
                        fin_q.pop(0)()

                    def fin(po=po, tpo=tpo, qb=stp["qb"]):
                        dst = og[:, :, qb * 128:(qb + 1) * 128]
                        attn_finalize(P, A, po, tpo, 512, dst, tog, sink=sink[64:65, nk * 512:(nk + 1) * 512], pview=True)
                    fin_q.append(fin)

            pend = []
            for stp in steps:
                issue_S(stp)
                pend.append(stp)
                if len(pend) > 2:
                    issue_PV(pend.pop(0))
            while pend:
                issue_PV(pend.pop(0))
            while fin_q:
                fin_q.pop(0)()
            for g in range(4):
                hq = nk * 4 + g
                P.dma(SC["OT"][512 + hq * 64:512 + (hq + 1) * 64, q0:q0 + nq], og[:, g, :nq], reads=[tog], dom=tog)
    P.emit()
    P.close()


def out_pass(nc, G, name, Xin, Xout, OT, wout_d, l, tiles, ot_off=0, preload=None):
    P = Pass(nc, name)
    pre = FfnPreload(P, *preload) if preload is not None else None
    Wo = P.sb("Wo", [128, 8, D], BF16)
    tW = P.tok("W")
    stg = [P.sb("stg%d" % j, [128, 1024], F32) for j in range(2)]
    tst = [P.tok("stg%d" % j) for j in range(2)]
    jobs = [(wout_d[k * 128:(k + 1) * 128, :], Wo[:, k, :], D) for k in range(8)]
    load_cast_weights(P, jobs, stg, tst, tW)
    ot = Rot(P, "ot", [128, 8, 512], BF16, 2)
    xr = Rot(P, "xr", [128, 512], F32, 4)
    py = Rot(P, "py", [128, 512], F32, 2, psum=True)

    def tile_body(t0, n, isctx):
        w = 1 if isctx else 0
        o, to = ot.next()
        for k in range(8):
            P.dma(o[:, k, :n], OT[k * 128:(k + 1) * 128, t0 - ot_off:t0 - ot_off + n], writes=[to], dom=to)
        for dc in range(8):
            x, tx = xr.next()
            P.dma(x[:, :n], Xin[dc * 128:(dc + 1) * 128, t0:t0 + n], writes=[tx], dom=tx)
            p, tp = py.next()
            for k in range(8):
                mm(P, p[:, :n], Wo[:, k, dc * 128:(dc + 1) * 128], o[:, k, :n], k == 0, k == 7, [tW, to], [tp])
            P.op("dve", lambda e, x=x, p=p, dc=dc: e.scalar_tensor_tensor(
                x[:, :n], p[:, :n], G["GT"][:, l, w, 1, dc:dc + 1], x[:, :n], ALU.mult, ALU.add),
                [tp, tx, G["tok"]], [tp, tx])
            P.dma(Xout[dc * 128:(dc + 1) * 128, t0:t0 + n], x[:, :n], reads=[tx], dom=tx, q="act")

    for (t0, n, isctx) in tiles:
        tile_body(t0, n, isctx)
        if pre is not None:
            pre.pump(6)
    if pre is not None:
        pre.flush()
    P.emit()
    P.close()


def prep1_pass(nc, G, dr, SC, Xin, tiles):
    P = Pass(nc, "p1")
    NCH = 25
    W = P.sb("W", [128, 8, NCH * 128], BF16)
    tW = P.tok("W")
    stg = [P.sb("stg%d" % j, [128, 1600], F32) for j in range(2)]
    tst = [P.tok("stg%d" % j) for j in range(2)]
    jobs = []
    for k in range(8):
        for hf in range(2):
            jobs.append((dr["l1_win"][k * 128:(k + 1) * 128, hf * 1600:(hf + 1) * 1600], W[:, k, hf * 1600:(hf + 1) * 1600], 1600))
    load_cast_weights(P, jobs, stg, tst, tW)
    tc = P.tok("cst")
    TRI = P.sb("TRI", [128, 2, 128], F32)
    ident = P.sb("ident", [128, 128], BF16)
    WG = P.sb("WG", [64, 512], F32)
    P.dma(TRI[:], dr["tri"].rearrange("m p c -> p m c"), writes=[tc], dom=tc)
    P.dma(ident[:], dr["ident"], writes=[tc], dom=tc)
    P.dma(WG[:], dr["l1_wg"], writes=[tc], dom=tc)
    LFB = P.sb("LFB", [64, 512], F32)
    tLFB = P.tok("LFB")
    P.op("dve", lambda e: e.memset(LFB[:], 1.0), [], [tLFB])
    R = alloc_norm_scratch(P)
    xn = P.sb("xn", [128, 8, 512], F32)
    txn = P.tok("xn")
    h = P.sb("h", [128, 8, 512], BF16)
    th = P.tok("h")
    qT = P.sb("qT", [128, 4, 512], F32)
    kT = P.sb("kT", [128, 4, 512], F32)
    tqT = P.tok("qT")
    tkT = P.tok("kT")
    og = Rot(P, "og", [128, 8, 512], BF16, 1)
    pz = Rot(P, "pz", [128, 512], F32, 3, psum=True)
    pv = pz
    pG = Rot(P, "pG", [128, 4, 128], F32, 2, psum=True)
    pT = Rot(P, "pT", [128, 512], BF16, 2, psum=True)
    ee = Rot(P, "ee", [128, 512], F32, 2)
    ll = Rot(P, "ll", [128, 512], F32, 3)
    E = Rot(P, "E", [128, 4, 128], F32, 2)
    Ei = Rot(P, "Ei", [128, 4, 128], F32, 2)
    QDs = [Rot(P, "QDs%d" % d, [128, 4, 512], BF16, 1) for d in range(2)]
    KIs = [Rot(P, "KIs%d" % d, [128, 4, 512], BF16, 1) for d in range(2)]
    KEs = Rot(P, "KEs", [128, 512], BF16, 2)
    keT = Rot(P, "keT", [128, 128], BF16, 12)
    Vs = Rot(P, "Vs", [128, 1024], BF16, 2)
    DECs = Rot(P, "DECs", [128, 2, 4, 8], F32, 2)
    qscale = 128.0 ** -0.5

    def load_x(t0, n):
        for k in range(8):
            P.dma(xn[:, k, :n], Xin[k * 128:(k + 1) * 128, t0:t0 + n], writes=[txn], dom=txn)

    def proj(chunk, n):
        z, tz = pz.next()
        for k in range(8):
            mm(P, z[:, :n], W[:, k, chunk * 128:(chunk + 1) * 128], h[:, k, :n], k == 0, k == 7, [tW, th], [tz])
        return z, tz

    load_x(tiles[0][0], tiles[0][1])

    def tile_body(ti, t0, n, isctx):
        w = 1 if isctx else 0
        emit_prenorm(P, R, G, 1, w, 1, xn, txn, h, th, n)
        if ti + 1 < len(tiles):
            load_x(tiles[ti + 1][0], tiles[ti + 1][1])
        nsub = n // 128
        c0 = t0 // 64
        for hd in range(4):
            z, tz = proj(hd, n)
            P.op("act", lambda e, z=z, hd=hd: e.copy(kT[:, hd, :n], z[:, :n]), [tz], [tkT, tz])
            z, tz = proj(4 + hd, n)
            P.op("act", lambda e, z=z, hd=hd: e.mul(qT[:, hd, :n], z[:, :n], qscale), [tz], [tqT, tz])
        z, tz = proj(16, n)
        P.op("act", lambda e, z=z: e.copy(LFB[0:16, :n], z[0:16, :n]), [tz], [tLFB])
        P.op("act", lambda e, z=z: e.copy(LFB[32:48, :n], z[32:48, :n]), [tz], [tLFB, tz])
        o_g, tog = og.next()
        for j in range(8):
            z, tz = proj(8 + j, n)
            P.op("act", lambda e, z=z, j=j: e.activation(o_g[:, j, :n], z[:, :n], AF.Silu), [tz], [tog, tz])
        for j in range(8):
            P.dma(SC["OG"][j * 128:(j + 1) * 128, t0:t0 + n], o_g[:, j, :n], reads=[tog], dom=tog)
        for sub in range(nsub):
            v, tv = Vs.next()
            for hf in range(2):
                p, tp = pv.next()
                for k in range(8):
                    mm(P, p[:, :], h[:, k, sub * 128:(sub + 1) * 128], W[:, k, (17 + 4 * hf) * 128:(21 + 4 * hf) * 128],
                       k == 0, k == 7, [tW, th], [tp])
                P.op("act", lambda e, v=v, p=p, hf=hf: e.copy(v[:, hf * 512:(hf + 1) * 512], p[:, :]), [tp], [tv, tp])
            P.dma(SC["V1"][t0 + sub * 128:t0 + (sub + 1) * 128, :], v[:, :], reads=[tv], dom=tv)
        dec, tdec = DECs.next()

        stg_bufs = {}
        for d in range(2):
            stg_bufs[d] = (QDs[d].next(), KIs[d].next())

        def st0(d, sub, c):
            r0 = 32 * d
            c["sl"] = slice(sub * 128, (sub + 1) * 128)
            c["p"], c["tp"] = pv.next()
            mm(P, c["p"][:, :], LFB[r0:r0 + 17, c["sl"]], WG[r0:r0 + 17, :], True, True, [tLFB, tc], [c["tp"]])

        def st1(d, sub, c):
            p, tp = c["p"], c["tp"]
            e1, te1 = ee.next()
            P.op("act", lambda e: e.activation(e1[:, :], p[:, :], AF.Exp, scale=-1.0), [tp], [te1, tp])
            c["l1"], c["tl1"] = ll.next()
            l1 = c["l1"]
            P.op("act", lambda e: e.activation(l1[:, :], e1[:, :], AF.Ln, bias=1.0), [te1], [c["tl1"]])

        def st2(d, sub, c):
            c["g"], c["tg"] = pG.next()
            for hd in range(4):
                mm(P, c["g"][:, hd, :], c["l1"][:, hd * 128:(hd + 1) * 128], TRI[:, d, :], True, True, [c["tl1"], tc], [c["tg"]])

        def st3(d, sub, c):
            (qd, tqd), (ki, tki) = stg_bufs[d]
            g, tg, sl_ = c["g"], c["tg"], c["sl"]
            Ex, tEx = E.next()
            Eix, tEix = Ei.next()
            P.op("act", lambda e: e.activation(Ex[:], g[:], AF.Exp), [tg], [tEx])
            P.op("act", lambda e: e.activation(Eix[:], g[:], AF.Exp, scale=-1.0), [tg], [tEix, tg])
            c["kts"] = []
            col0 = 63 if d == 0 else 0
            P.op("dve", lambda e: e.tensor_copy(dec[:, d, :, 2 * sub:2 * sub + 2], Ex[:, :, col0:col0 + 65:64]), [tEx], [tdec])
            for hd in range(4):
                P.op("dve", lambda e, hd=hd: e.tensor_tensor(qd[:, hd, sl_], qT[:, hd, sl_], Ex[:, hd, :], ALU.mult),
                     [tqT, tEx], [tqd])
                P.op("pool", lambda e, hd=hd: e.tensor_tensor(ki[:, hd, sl_], kT[:, hd, sl_], Eix[:, hd, :], ALU.mult),
                     [tkT, tEix], [tki])
                kt_, tkt_ = keT.next()
                for cc in range(2):
                    cs = slice(sub * 128 + 64 * cc, sub * 128 + 64 * (cc + 1))
                    P.op("dve", lambda e, hd=hd, cc=cc, cs=cs, kt_=kt_: e.scalar_tensor_tensor(
                        kt_[:, 64 * cc:64 * (cc + 1)], kT[:, hd, cs], dec[:, d, hd, 2 * sub + cc:2 * sub + cc + 1],
                        Eix[:, hd, 64 * cc:64 * (cc + 1)], ALU.mult, ALU.mult), [tkT, tdec, tEix], [tkt_])
                c["kts"].append((kt_, tkt_))

        def st4(d, sub, c):
            c["pt"], c["tpt"] = pT.next()
            pt = c["pt"]
            for hd in range(4):
                kt_, tkt_ = c["kts"][hd]
                P.op("pe", lambda e, hd=hd, kt_=kt_: e.transpose(pt[:, hd * 128:(hd + 1) * 128], kt_[:, :], ident[:]),
                     [tkt_, tc], [c["tpt"]])

        def st5(d, sub, c):
            ke_s, tke_s = KEs.next()
            pt, tpt = c["pt"], c["tpt"]
            P.op("act", lambda e: e.copy(ke_s[:, :], pt[:, :]), [tpt], [tke_s, tpt])
            P.dma(SC["KE"][d, t0 + sub * 128:t0 + (sub + 1) * 128, :], ke_s[:, :], reads=[tke_s], dom=tke_s)

        stages = [st0, st1, st2, st3, st4, st5]
        gunits = [(d, sub, {}) for d in range(2) for sub in range(nsub)]
        for t in range(len(gunits) + len(stages) - 1):
            for k in range(len(stages) - 1, -1, -1):
                u = t - k
                if 0 <= u < len(gunits):
                    stages[k](*gunits[u])
        for d in range(2):
            (qd, tqd), (ki, tki) = stg_bufs[d]
            for hd in range(4):
                P.dma(SC["QD"][d, hd, :, t0:t0 + n], qd[:, hd, :n], reads=[tqd], dom=tqd)
                P.dma(SC["KI"][d, hd, :, t0:t0 + n], ki[:, hd, :n], reads=[tki], dom=tki)
        nch = n // 64
        for d in range(2):
            P.dma(SC["DEC"][d, :, :, c0:c0 + nch], dec[:, d, :, :nch], reads=[tdec], dom=tdec)

    for ti, (t0, n, isctx) in enumerate(tiles):
        tile_body(ti, t0, n, isctx)
    P.emit()
    P.close()


def scan_pass(nc, G, dr, SC, d):
    P = Pass(nc, "s%d" % d)
    tc = P.tok("cst")
    MASK = P.sb("MASK", [128, 4, 128], F32)
    P.dma(MASK[:], dr["gmask"][d], writes=[tc], dom=tc)
    DEC = P.sb("DEC", [128, 4, T // 64], F32)
    P.dma(DEC[:], SC["DEC"][d], writes=[tc], dom=tc)
    M256 = P.sb("M256", [128, 128], BF16)
    P.dma(M256[:], dr["Ms"][3], writes=[tc], dom=tc)
    gn = P.sb("gn", [128, 2], F32)
    P.dma(gn[:], dr["l1_gn"], writes=[tc], dom=tc)
    S32 = P.sb("S32", [128, 4, 256], F32)
    S16r = Rot(P, "S16", [128, 4, 256], BF16, 2)
    tS32h = [P.tok("S32_%d" % j) for j in range(4)]
    P.op("dve", lambda e: e.memset(S32[:], 0.0), [], tS32h)
    scur = {}
    scur["S16"], scur["tS16"] = S16r.next()
    P.op("pool", lambda e: e.memset(scur["S16"][:], 0.0), [], [scur["tS16"]])
    QD = Rot(P, "QD", [128, 4, 512], BF16, 2)
    KI = Rot(P, "KI", [128, 4, 512], BF16, 2)
    KE = Rot(P, "KE", [128, 4, 512], BF16, 2)
    V = Rot(P, "V", [128, 4, 1024], BF16, 2)
    OB = Rot(P, "OB", [128, 8, 512], F32, 2)
    OGt = Rot(P, "OGt", [128, 8, 512], BF16, 2)
    O1 = Rot(P, "O1", [128, 8, 512], BF16, 2)
    Am = Rot(P, "Am", [128, 4, 128], BF16, 2)
    osum = Rot(P, "osum", [128, 8, 128], F32, 2)
    sqs = Rot(P, "sqs", [128, 8, 128], BF16, 2)
    rs = Rot(P, "rs", [128, 4, 128], F32, 2)
    rt = Rot(P, "rt", [128, 4, 128], F32, 2)
    o1f = Rot(P, "o1f", [128, 8, 128], F32, 2)
    pA = Rot(P, "pA", [128, 4, 128], F32, 1, psum=True)
    pO = Rot(P, "pO", [128, 8, 128], F32, 1, psum=True)
    pKV = Rot(P, "pKV", [128, 4, 256], F32, 2, psum=True)
    pss = Rot(P, "pss", [128, 4, 128], F32, 1, psum=True)
    KEv = SC["KE"][d].rearrange("(s p) c -> p s c", p=128)
    V1v = SC["V1"].rearrange("(s p) c -> p s c", p=128)
    blocks = [(0, LC, True)] + [(LC + i * 512, 512, False) for i in range(S // 512)]
    if d == 1:
        blocks = [blocks[0]] + blocks[1:][::-1]

    def kv_mm(ke, tke, v, tv, sub, c):
        p, tp = pKV.next()
        rows = slice(64 * c, 64 * (c + 1))
        for hd in range(4):
            mm(P, p[:, hd, :], ke[rows, sub, hd * 128:(hd + 1) * 128], v[rows, sub, hd * 256:(hd + 1) * 256],
               True, True, [tke, tv], [tp])
        return p, tp

    def state_update(p, tp, chunk_id):
        for hd in range(4):
            P.op("dve", lambda e, hd=hd: e.scalar_tensor_tensor(
                S32[:, hd, :], S32[:, hd, :], DEC[:, hd, chunk_id:chunk_id + 1], p[:, hd, :], ALU.mult, ALU.add),
                [tp, tS32h[hd], tc], [tS32h[hd], tp])
        nS, tnS = S16r.next()
        P.op("act", lambda e: e.copy(nS[:], S32[:]), tS32h, [tnS])
        scur["S16"], scur["tS16"] = nS, tnS

    comb_q = []

    def combine(os_, tos, sl_, o1, to1, ogt, togt):
        sq, tsq = sqs.next()
        P.op("act", lambda e: e.activation(sq[:], os_[:], AF.Square), [tos], [tsq])
        ps_, tps = pss.next()
        for hd in range(4):
            for dvc in range(2):
                mm(P, ps_[:, hd, :], M256[:], sq[:, hd * 2 + dvc, :], dvc == 0, dvc == 1, [tsq, tc], [tps])
        r, tr = rs.next()
        tm, ttm = rt.next()
        emit_rsqrt(P, r[:], ps_[:], tm[:], tps, tr, ttm)
        of, tof = o1f.next()
        for hd in range(4):
            for dvc in range(2):
                P.op("dve", lambda e, hd=hd, dvc=dvc: e.scalar_tensor_tensor(
                    of[:, hd * 2 + dvc, :], os_[:, hd * 2 + dvc, :], gn[:, dvc:dvc + 1], r[:, hd, :],
                    ALU.mult, ALU.mult), [tos, tr, tc], [tof])
        P.op("pool", lambda e: e.tensor_tensor(o1[:, :, sl_], of[:], ogt[:, :, sl_], ALU.mult), [tof, togt], [to1])

    for (t0, nb, isctx) in blocks:
        nsub = nb // 128
        ke, tke = KE.next()
        v, tv = V.next()
        s0 = t0 // 128
        P.dma(ke[:, :nsub, :], KEv[:, s0:s0 + nsub, :], writes=[tke], dom=tke)
        for sub in range(nsub):
            P.dma(v[:, sub, :], V1v[:, s0 + sub, :], writes=[tv], dom=tv)
        subs = list(range(nsub))
        corder = [0, 1]
        if d == 1:
            subs = subs[::-1]
            corder = [1, 0]
        if isctx:
            for sub in subs:
                kvs = [kv_mm(ke, tke, v, tv, sub, c) for c in corder]
                for (p_, tp_), c in zip(kvs, corder):
                    state_update(p_, tp_, (t0 + sub * 128) // 64 + c)
            continue
        qd, tqd = QD.next()
        ki, tki = KI.next()
        for hd in range(4):
            P.dma(qd[:, hd, :], SC["QD"][d, hd, :, t0:t0 + nb], writes=[tqd], dom=tqd)
            P.dma(ki[:, hd, :], SC["KI"][d, hd, :, t0:t0 + nb], writes=[tki], dom=tki)
        if d == 1:
            ob, tob = OB.next()
        else:
            ob, tob = OB.next()
            ogt, togt = OGt.next()
            o1, to1 = O1.next()
            for j in range(8):
                P.dma(ob[:, j, :], SC["OBF"][j * 128:(j + 1) * 128, t0 - LC:t0 - LC + nb], writes=[tob], dom=tob)
                P.dma(ogt[:, j, :], SC["OG"][j * 128:(j + 1) * 128, t0:t0 + nb], writes=[togt], dom=togt)
        for sub in subs:
            sl_ = slice(sub * 128, (sub + 1) * 128)
            a, ta = pA.next()
            for hd in range(4):
                mm(P, a[:, hd, :], ki[:, hd, sl_], qd[:, hd, sl_], True, True, [tki, tqd], [ta])
            am, tam = Am.next()
            P.op("dve", lambda e, am=am, a=a: e.tensor_tensor(am[:], a[:], MASK[:], ALU.mult), [ta, tc], [tam, ta])
            kvs = [kv_mm(ke, tke, v, tv, sub, c) for c in corder]
            po, tpo = pO.next()
            for hd in range(4):
                for dvc in range(2):
                    mm(P, po[:, hd * 2 + dvc, :], v[:, sub, hd * 256 + dvc * 128:hd * 256 + (dvc + 1) * 128], am[:, hd, :],
                       (hd * 2 + dvc) % 4 == 0, False, [tv, tam], [tpo], sgc=True)
            for ci, c in enumerate(corder):
                cs = slice(sub * 128 + 64 * c, sub * 128 + 64 * (c + 1))
                for hd in range(4):
                    for dvc in range(2):
                        mm(P, po[:, hd * 2 + dvc, 64 * c:64 * (c + 1)], scur["S16"][:, hd, dvc * 128:(dvc + 1) * 128], qd[:, hd, cs],
                           False, True, [scur["tS16"], tqd], [tpo], sgc=True)
                state_update(kvs[ci][0], kvs[ci][1], (t0 + sub * 128) // 64 + c)
            if d == 1:
                for hb in range(2):
                    P.op("act", lambda e, po=po, ob=ob, sl_=sl_, hb=hb: e.copy(ob[:, 4 * hb:4 * hb + 4, sl_], po[:, 4 * hb:4 * hb + 4, :]),
                         [tpo], [tob, tpo])
            else:
                os_, tos = osum.next()
                for hb in range(2):
                    P.op("dve", lambda e, os_=os_, po=po, ob=ob, sl_=sl_, hb=hb: e.tensor_tensor(
                        os_[:, 4 * hb:4 * hb + 4, :], po[:, 4 * hb:4 * hb + 4, :], ob[:, 4 * hb:4 * hb + 4, sl_], ALU.add),
                        [tpo, tob], [tos, tpo])
                while comb_q:
                    comb_q.pop(0)()

                def comb(os_=os_, tos=tos, sl_=sl_, o1=o1, to1=to1, ogt=ogt, togt=togt):
                    combine(os_, tos, sl_, o1, to1, ogt, togt)
                comb_q.append(comb)
        while comb_q:
            comb_q.pop(0)()
        if d == 1:
            for j in range(8):
                P.dma(SC["OBF"][j * 128:(j + 1) * 128, t0 - LC:t0 - LC + nb], ob[:, j, :], reads=[tob], dom=tob)
        else:
            for j in range(8):
                P.dma(SC["OT"][j * 128:(j + 1) * 128, t0:t0 + nb], o1[:, j, :], reads=[to1], dom=to1)
    P.emit()
    P.close()


def build_program(upto=99, dumps=()):
    nc = bass.Bass("TRN2", target_bir_lowering=False)
    dr = {}

    def din(name, shape, dt=F32):
        dr[name] = nc.dram_tensor(name, list(shape), dt, kind="ExternalInput").ap()

    din("xT", [D, T])
    din("c8", [128, 8])
    din("cc8", [128, 8])
    for l in range(2):
        din("l%d_wmod" % l, [D, 9 * D])
        din("l%d_bmod" % l, [128, 72])
        din("l%d_ng" % l, [128, 24])
        for f in (1, 2):
            din("l%d_ffn%d_wgu" % (l, f), [D, 2 * DFF])
            din("l%d_ffn%d_wd" % (l, f), [DFF, D])
    din("ones1024", [128, 128], BF16)
    din("l0_win", [D, 2048])
    din("l0_wuq", [256, 1024])
    din("l0_wukv", [128, 1024])
    din("l0_gv", [128, 16])
    din("l0_wout", [D, D])
    din("l0_sink", [1, 1024])
    din("Ms", [4, 128, 128], BF16)
    din("wmask", [2, 128, 512], BF16)
    din("onesf", [128, 64])
    for nm in ("cosm", "sinm", "cosw", "sinw"):
        din(nm, [128, T])
    din("l1_win", [D, 3200])
    din("l1_wg", [64, 512])
    din("l1_gn", [128, 2])
    din("l1_wout", [D, D])
    din("tri", [2, 128, 128])
    din("gmask", [2, 128, 4, 128])
    din("ident", [128, 128], BF16)
    SC = {}
    for nm, shp in (("QN", [4, 128, T]), ("QR", [2, 128, T]), ("KN", [4, 128, T]), ("KR", [32, T]), ("KB", [128, T]),
                    ("QB", [4, 128, T]), ("VA", [T, 512]), ("VB", [T, 128]), ("OT", [D, T])):
        SC[nm] = nc.dram_tensor("sc_" + nm, shp, BF16, kind="Internal").ap()
    for nm, shp, dt in (("QD", [2, 4, 128, T], BF16), ("KI", [2, 4, 128, T], BF16), ("KE", [2, T, 512], BF16),
                        ("V1", [T, 1024], BF16), ("OG", [D, T], BF16), ("DEC", [2, 128, 4, T // 64], F32),
                        ("OBF", [D, S], F32)):
        SC[nm] = nc.dram_tensor("sc_" + nm, shp, dt, kind="Internal").ap()
    out = nc.dram_tensor("outT", [D, S], F32, kind="ExternalOutput").ap()
    XA = nc.dram_tensor("XA", [D, T], F32, kind="Internal").ap()
    XB = nc.dram_tensor("XB", [D, T], F32, kind="Internal").ap()
    dump_aps = {}
    sc_dumps = {}
    for nm, shape in dumps:
        if nm in SC:
            sc_dumps[nm] = nc.dram_tensor("dump_" + nm, list(SC[nm].shape), SC[nm].dtype, kind="ExternalOutput").ap()
        else:
            dump_aps[nm] = nc.dram_tensor("dump_" + nm, list(shape), F32, kind="ExternalOutput").ap()

    with ExitStack() as ges:
        SEMPOOL[0] = SemPool(nc, ges, 96)
        G = {}
        G["MOD"] = ges.enter_context(nc.sbuf_tensor("gMOD", [128, 2, 2, 72], F32))
        G["A"] = ges.enter_context(nc.sbuf_tensor("gA", [128, 2, 2, 3, 8], F32))
        G["B"] = ges.enter_context(nc.sbuf_tensor("gB", [128, 2, 2, 3, 8], F32))
        G["GT"] = ges.enter_context(nc.sbuf_tensor("gGT", [128, 2, 2, 3, 8], F32))
        G["ones1024"] = ges.enter_context(nc.sbuf_tensor("gones", [128, 128], BF16))
        G["tok"] = Tok("G")
        G["tokc"] = Tok("Gc")
        P = Pass(nc, "pc")
        P.dma(G["ones1024"][:], dr["ones1024"], writes=[G["tokc"]], dom=G["tokc"])
        P.emit()
        P.close()
        def alloc_w(tag):
            wes = ExitStack()
            Wgu_ = wes.enter_context(nc.sbuf_tensor("pwgu" + tag, [128, 8, 2 * DFF], BF16))
            Wd_ = wes.enter_context(nc.sbuf_tensor("pwd" + tag, [128, NF, D], BF16))
            return wes, (Wgu_, Wd_)

        wes, W1 = alloc_w("1")
        mod_pass(nc, G, dr, preload=(dr["l0_ffn1_wgu"], dr["l0_ffn1_wd"]) + W1)
        tiles = token_tiles()
        final_src = XA
        if upto >= 1:
            ffn_pass(nc, G, "f1", dr["xT"], XA, dr["l0_ffn1_wgu"], dr["l0_ffn1_wd"], 0, 0, tiles, pre=W1)
        wes.close()
        if upto >= 2:
            prep0_pass(nc, G, dr, SC, XA, tiles)
        if upto >= 3:
            mla_pass(nc, G, dr, SC, tiles)
            win_pass(nc, G, dr, SC)
        if upto >= 4:
            wes, W2 = alloc_w("2")
            out_pass(nc, G, "o0", XA, XB, SC["OT"], dr["l0_wout"], 0, tiles,
                     preload=(dr["l0_ffn2_wgu"], dr["l0_ffn2_wd"]) + W2)
            final_src = XB
            if upto >= 5:
                ffn_pass(nc, G, "f2", XB, XA, dr["l0_ffn2_wgu"], dr["l0_ffn2_wd"], 0, 2, tiles, pre=W2)
                final_src = XA
            wes.close()
        lat_tiles = [t for t in tiles if not t[2]]
        if upto >= 6:
            ffn_pass(nc, G, "f3", XA, XB, dr["l1_ffn1_wgu"], dr["l1_ffn1_wd"], 1, 0, tiles)
            final_src = XB
        if upto >= 7:
            prep1_pass(nc, G, dr, SC, XB, tiles)
        if upto >= 8:
            scan_pass(nc, G, dr, SC, 1)
            scan_pass(nc, G, dr, SC, 0)
        if upto >= 9:
            wes, W4 = alloc_w("4")
            out_pass(nc, G, "o1", XB, XA, SC["OT"], dr["l1_wout"], 1, lat_tiles,
                     preload=(dr["l1_ffn2_wgu"], dr["l1_ffn2_wd"]) + W4)
            final_src = XA
            if upto >= 10:
                ffn_pass(nc, G, "f4", XA, out, dr["l1_ffn2_wgu"], dr["l1_ffn2_wd"], 1, 2, lat_tiles, xout_off=LC, pre=W4)
                final_src = None
            wes.close()
        P = Pass(nc, "pd")
        tdd = P.tok("dd")
        for nm, ap in sc_dumps.items():
            src = SC[nm]
            if len(src.shape) == 2:
                for r0 in range(0, src.shape[0], 128):
                    P.dma(ap[r0:r0 + 128, :], src[r0:r0 + 128, :], dom=tdd)
            elif len(src.shape) == 3:
                for a in range(src.shape[0]):
                    for r0 in range(0, src.shape[1], 128):
                        P.dma(ap[a, r0:r0 + 128, :], src[a, r0:r0 + 128, :], dom=tdd)
            else:
                for a in range(src.shape[0]):
                    for b2 in range(src.shape[1]):
                        P.dma(ap[a, b2], src[a, b2], dom=tdd)
        cp = [P.sb("cp%d" % j, [128, 2048], F32) for j in range(2)]
        tcp = [P.tok("cp%d" % j) for j in range(2)]
        srcs = []
        if "MOD" in dump_aps:
            P.dma(dump_aps["MOD"], G["MOD"][:].rearrange("p a b c -> p (a b c)"),
                  reads=[G["tok"]], dom=tcp[0])
        jj = 0
        for nm, ap in list(dump_aps.items()) + [("__out", out)]:
            if nm == "MOD":
                continue
            src = {"XA": XA, "XB": XB}.get(nm, None)
            if src is None and nm != "__out":
                continue
            col0 = 0
            if nm == "__out":
                if final_src is None:
                    continue
                src = final_src
                col0 = LC
            ncol = ap.shape[1]
            for k in range(8):
                for c0 in range(0, ncol, 2048):
                    b = jj % 2
                    jj += 1
                    wd = min(2048, ncol - c0)
                    P.dma(cp[b][:, :wd], src[k * 128:(k + 1) * 128, col0 + c0:col0 + c0 + wd], writes=[tcp[b]], dom=tcp[b])
                    P.dma(ap[k * 128:(k + 1) * 128, c0:c0 + wd], cp[b][:, :wd], reads=[tcp[b]], dom=tcp[b])
        P.emit()
        P.close()
    return nc


def host_inputs(inp, b):
    f = np.float32
    m = {}
    xT = np.concatenate([inp["ctx"][b], inp["x"][b]], axis=0).T
    m["xT"] = np.ascontiguousarray(xT, dtype=f)
    m["c8"] = np.ascontiguousarray(inp["c"][b].reshape(8, 128).T, dtype=f)
    m["cc8"] = np.ascontiguousarray(inp["c_ctx"].reshape(8, 128).T, dtype=f)
    for l in range(2):
        p = "l%d_" % l
        m[p + "wmod"] = np.ascontiguousarray(inp[p + "w_mod"], dtype=f)
        m[p + "bmod"] = np.ascontiguousarray(inp[p + "b_mod"].reshape(72, 128).T, dtype=f)
        m[p + "ng"] = np.ascontiguousarray(inp[p + "norm_g"].reshape(24, 128).T, dtype=f)
        for k in (1, 2):
            m[p + "ffn%d_wgu" % k] = np.ascontiguousarray(inp[p + "ffn%d_w_gu" % k], dtype=f)
            m[p + "ffn%d_wd" % k] = np.ascontiguousarray(inp[p + "ffn%d_w_down" % k], dtype=f)
    m["ones1024"] = np.full((128, 128), 1.0 / 1024, dtype=ml_dtypes.bfloat16)
    m.update(host_consts())
    m.update(host_l0(inp))
    m.update(host_l1(inp))
    return m


def host_l1(inp):
    f = np.float32
    m = {}
    w = inp["l1_w_in"].astype(f)
    k, v, lf, lb, q, g = w[:, 0:512], w[:, 512:1536], w[:, 1536:1552], w[:, 1552:1568], w[:, 1568:2080], w[:, 2080:3104]
    z16 = np.zeros((D, 16), f)
    z80 = np.zeros((D, 80), f)
    m["l1_win"] = np.ascontiguousarray(np.concatenate([k, q, g, lf, z16, lb, z80, v], axis=1))
    assert m["l1_win"].shape == (D, 3200)
    wg = np.zeros((64, 512), f)
    wg[0:16] = inp["l1_w_gk_f"]
    wg[16] = inp["l1_b_gk_f"]
    wg[32:48] = inp["l1_w_gk_b"]
    wg[48] = inp["l1_b_gk_b"]
    m["l1_wg"] = wg
    m["l1_gn"] = np.ascontiguousarray(inp["l1_g_norm"].astype(f).reshape(2, 128).T)
    m["l1_wout"] = np.ascontiguousarray(inp["l1_w_out"], dtype=f)
    return m


def _swap(a):
    return a[:, np.arange(a.shape[1]) ^ 1]


def rope_tables(rot_dim, rep):
    n_freq = rot_dim // 4
    inv = (np.float32(10000.0) ** (-np.arange(n_freq, dtype=np.float32) / np.float32(n_freq))).astype(np.float32)
    t = np.arange(S)
    row = (t // 64).astype(np.float32)
    col = (t % 64).astype(np.float32)
    ang = np.concatenate([row[:, None] * inv, col[:, None] * inv], axis=-1).astype(np.float32)
    cos = np.cos(ang).astype(np.float32)
    sin = np.sin(ang).astype(np.float32)
    d = np.arange(rot_dim)
    C = np.ones((rot_dim, T), np.float32)
    Sg = np.zeros((rot_dim, T), np.float32)
    C[:, LC:] = cos[:, d // 2].T
    sign = np.where(d % 2 == 0, -1.0, 1.0).astype(np.float32)
    Sg[:, LC:] = sin[:, d // 2].T * sign[:, None]
    return np.ascontiguousarray(np.tile(C, (rep, 1))), np.ascontiguousarray(np.tile(Sg, (rep, 1)))


_CONSTS = {}


def host_consts():
    if _CONSTS:
        return _CONSTS
    bf = ml_dtypes.bfloat16
    m = {}
    Ms = np.zeros((4, 128, 128), np.float32)
    Ms[0] = 1.0 / 128
    for b in range(2):
        Ms[1, b * 64:(b + 1) * 64, b * 64:(b + 1) * 64] = 1.0 / 64
    for b in range(4):
        Ms[2, b * 32:(b + 1) * 32, b * 32:(b + 1) * 32] = 1.0 / 32
    Ms[3] = 1.0 / 256
    m["Ms"] = Ms.astype(bf)
    kk = np.arange(128)[:, None]
    qq = np.arange(128)[None, :]
    wm = np.stack([np.tile((kk >= qq).astype(np.float32), (1, 4)), np.tile((kk <= qq).astype(np.float32), (1, 4))])
    wm = (1.0 - wm) * np.float32(-30000.0)
    m["wmask"] = wm.astype(bf)
    m["onesf"] = np.ones((128, 64), np.float32)
    sidx = np.arange(128)[:, None]
    tidx = np.arange(128)[None, :]
    same = (sidx // 64) == (tidx // 64)
    lo = (same & (sidx <= tidx)).astype(np.float32)
    hi = (same & (sidx >= tidx)).astype(np.float32)
    m["tri"] = np.stack([lo, hi]) * np.float32(-1.0 / 16.0)
    m["gmask"] = np.ascontiguousarray(np.stack([np.repeat(lo[:, None, :], 4, axis=1), np.repeat(hi[:, None, :], 4, axis=1)]))
    m["ident"] = np.eye(128, dtype=np.float32).astype(bf)
    m["cosm"], m["sinm"] = rope_tables(32, 4)
    m["cosw"], m["sinw"] = rope_tables(64, 2)
    _CONSTS.update(m)
    return _CONSTS


def host_l0(inp):
    f = np.float32
    m = {}
    w = inp["l0_w_in"].astype(f)
    ckv, kr, wk, wv, cq, wq = w[:, 0:128], w[:, 128:160], w[:, 160:288], w[:, 288:416], w[:, 416:672], w[:, 672:1184]
    pad = np.zeros((D, 96), f)
    m["l0_win"] = np.ascontiguousarray(np.concatenate(
        [ckv, kr, pad, _swap(kr), pad, wk, _swap(wk), cq, wq, _swap(wq), wv], axis=1))
    assert m["l0_win"].shape == (D, 2048)
    uq = inp["l0_mla_w_uq"].astype(f).reshape(256, 8, 96)
    nope = uq[:, :, :64].reshape(256, 512)
    rope = uq[:, :, 64:].reshape(256, 256)
    m["l0_wuq"] = np.ascontiguousarray(np.concatenate([nope, rope, _swap(rope)], axis=1))
    ukv = inp["l0_mla_w_ukv"].astype(f).reshape(128, 8, 128)
    m["l0_wukv"] = np.ascontiguousarray(np.concatenate([ukv[:, :, :64].reshape(128, 512), ukv[:, :, 64:].reshape(128, 512)], axis=1))
    gv = np.ones((128, 16), f)
    sw = lambda g: g[np.arange(g.shape[0]) ^ 1]
    gv[:, 0] = inp["l0_mla_g_kva"]
    gv[:32, 1] = inp["l0_mla_g_kr"]
    gv[:32, 2] = sw(inp["l0_mla_g_kr"])
    gv[:, 3] = np.tile(inp["l0_win_g_k"], 2)
    gv[:, 4] = np.tile(sw(inp["l0_win_g_k"]), 2)
    gv[:, 5] = inp["l0_mla_g_qa"][:128]
    gv[:, 6] = inp["l0_mla_g_qa"][128:]
    gv[:, 7] = np.tile(inp["l0_win_g_q"], 2)
    gv[:, 8] = np.tile(sw(inp["l0_win_g_q"]), 2)
    gv[:, 9] = np.tile(inp["l0_mla_g_kn"], 2)
    gv[:, 10] = np.tile(inp["l0_mla_g_qn"], 2)
    gv[:, 11] = np.tile(inp["l0_mla_g_qr"], 4)
    gv[:, 12] = np.tile(sw(inp["l0_mla_g_qr"]), 4)
    m["l0_gv"] = gv
    m["l0_wout"] = np.ascontiguousarray(inp["l0_w_out"], dtype=f)
    m["l0_sink"] = np.ascontiguousarray(np.repeat(inp["l0_win_sink"].astype(f).reshape(8), 128).reshape(1, 1024))
    return m


_INPUT_NAMES = (
    "x", "c", "ctx", "c_ctx",
    "l0_norm_g", "l0_w_mod", "l0_b_mod", "l0_ffn1_w_gu", "l0_ffn1_w_down", "l0_ffn2_w_gu", "l0_ffn2_w_down",
    "l0_w_in", "l0_mla_g_qa", "l0_mla_g_kva", "l0_mla_w_uq", "l0_mla_w_ukv", "l0_mla_g_qn", "l0_mla_g_qr",
    "l0_mla_g_kn", "l0_mla_g_kr", "l0_win_g_q", "l0_win_g_k", "l0_win_sink", "l0_w_out",
    "l1_norm_g", "l1_w_mod", "l1_b_mod", "l1_ffn1_w_gu", "l1_ffn1_w_down", "l1_ffn2_w_gu", "l1_ffn2_w_down",
    "l1_w_in", "l1_w_gk_f", "l1_b_gk_f", "l1_w_gk_b", "l1_b_gk_b", "l1_g_norm", "l1_w_out",
)


def kernel(**inp):
    inp = {k: np.asarray(inp[k]) for k in _INPUT_NAMES}
    nc = build_program()
    in_maps = [host_inputs(inp, b) for b in range(NCORES)]
    res = run_bass_kernel_spmd(nc, in_maps, core_ids=list(range(NCORES)))
    outs = [np.asarray(res.results[b]["outT"]).T for b in range(NCORES)]
    return np.ascontiguousarray(np.stack(outs, axis=0).astype(np.float32))
```
